# Optimizing a Trainium2 kernel written in Bass

```python
import jax, jax.numpy as jnp
from jax import lax
import numpy as np

D_MODEL = 4096
BATCH = 2
SEQ = 4096
DEPTH = 2

NSA_HEADS = 16
NSA_KV_GROUPS = 4
NSA_HPG = NSA_HEADS // NSA_KV_GROUPS
HEAD_DIM = 128
CMP_LEN = 32
CMP_STRIDE = 16
SLC_BLOCK = 64
SLC_TOPK = 16
SLC_Q_BLOCK = 64
WINDOW = 512
WIN_Q_BLOCK = 128
ATTN_SCALE = HEAD_DIM ** -0.5
FORCED_BONUS = 1e6
NEG_INF = -1e30
HGRN_HEADS = 16
HGRN_DK = 128
HGRN_DV = 128
HGRN_CHUNK = 64
ROPE_THETA = 500000.0
ROT_DIM = HEAD_DIM // 4
D_FF = 4 * D_MODEL
NORM_EPS = 1e-6

NSA_WIDTH = NSA_HEADS * HEAD_DIM
KV_WIDTH = NSA_KV_GROUPS * HEAD_DIM
HGRN_KW = HGRN_HEADS * HGRN_DK
HGRN_VW = HGRN_HEADS * HGRN_DV
IN_SIZES = (NSA_WIDTH, KV_WIDTH, KV_WIDTH, KV_WIDTH, KV_WIDTH, KV_WIDTH, KV_WIDTH, 3 * NSA_HEADS,
            HGRN_KW, HGRN_KW, HGRN_VW, HGRN_VW, D_MODEL, D_MODEL)
IN_WIDTH = sum(IN_SIZES)
IN_SPLITS = tuple(int(v) for v in np.cumsum(IN_SIZES)[:-1])

kernel_name = 'nsa_hgrn2_hybrid_block'


def rms_norm(x, w):
    xf = x.astype(jnp.float32)
    y = xf * lax.rsqrt(jnp.mean(xf * xf, axis=-1, keepdims=True) + NORM_EPS)
    return (y * w.astype(jnp.float32)).astype(x.dtype)


def partial_rope(x, positions):
    inv = ROPE_THETA ** (-jnp.arange(0, ROT_DIM, 2, dtype=jnp.float32) / ROT_DIM)
    ang = positions.astype(jnp.float32)[..., None] * inv
    cos = jnp.cos(ang)[:, :, None, :].astype(x.dtype)
    sin = jnp.sin(ang)[:, :, None, :].astype(x.dtype)
    half = ROT_DIM // 2
    x1, x2, xp = x[..., :half], x[..., half:ROT_DIM], x[..., ROT_DIM:]
    return jnp.concatenate([x1 * cos - x2 * sin, x2 * cos + x1 * sin, xp], axis=-1)


def compress_blocks(k_raw, pos, w1, w2):
    S = k_raw.shape[1]
    nc = (S - CMP_LEN) // CMP_STRIDE + 1
    idx = np.arange(nc)[:, None] * CMP_STRIDE + np.arange(CMP_LEN)[None, :]
    kb = k_raw[:, idx] + pos[None, None, :, None, :]
    hid = jax.nn.gelu(jnp.einsum('bnlgd,lde->bnge', kb, w1))
    return jnp.einsum('bnge,ef->bgnf', hid, w2)


def compressed_attention(q, k_cmp, v_cmp):
    S = q.shape[3]
    nc = k_cmp.shape[2]
    s = jnp.einsum('bgjtd,bgnd->bgjtn', q, k_cmp).astype(jnp.float32) * ATTN_SCALE
    t = np.arange(S)
    blk_end = np.arange(nc) * CMP_STRIDE + CMP_LEN - 1
    mask = blk_end[None, :] <= t[:, None]
    p = jnp.where(mask, jax.nn.softmax(jnp.where(mask, s, NEG_INF), axis=-1), 0.0)
    o = jnp.einsum('bgjtn,bgnd->bgjtd', p.astype(v_cmp.dtype), v_cmp)
    return o, p.sum(axis=2)


def cmp_slc_overlap(S):
    nc = (S - CMP_LEN) // CMP_STRIDE + 1
    nsel = S // SLC_BLOCK
    cs = np.arange(nc) * CMP_STRIDE
    ce = cs + CMP_LEN - 1
    ss = np.arange(nsel) * SLC_BLOCK
    se = ss + SLC_BLOCK - 1
    ov = np.minimum(ce[:, None], se[None, :]) - np.maximum(cs[:, None], ss[None, :]) + 1
    return np.maximum(ov, 0).astype(np.float32)


def select_blocks(imp):
    S = imp.shape[2]
    nsel = S // SLC_BLOCK
    p_slc = jnp.einsum('bgtn,nm->bgtm', imp, jnp.asarray(cmp_slc_overlap(S)))
    t = np.arange(S)
    blk = np.arange(nsel)
    cur = t // SLC_BLOCK
    valid = blk[None, :] * SLC_BLOCK <= t[:, None]
    forced = (blk[None, :] == 0) | (blk[None, :] == cur[:, None]) | (blk[None, :] == cur[:, None] - 1)
    bonus = np.where(forced, FORCED_BONUS, 0.0).astype(np.float32)
    score = jnp.where(valid, p_slc + bonus, -jnp.inf)
    vals, idx = lax.top_k(score, min(SLC_TOPK, nsel))
    return idx, jnp.isfinite(vals)


def selected_attention(q, k, v, idx, ok):
    B, G, J, S, HD = q.shape
    nsel = S // SLC_BLOCK
    kb = k.reshape(B, G, nsel, SLC_BLOCK, HD)
    vb = v.reshape(B, G, nsel, SLC_BLOCK, HD)
    bi = jnp.arange(B)[:, None, None, None]
    gi = jnp.arange(G)[None, :, None, None]
    n_topk = idx.shape[-1]

    def one_block(n):
        start = n * SLC_Q_BLOCK
        qc = lax.dynamic_slice_in_dim(q, start, SLC_Q_BLOCK, axis=3)
        ic = lax.dynamic_slice_in_dim(idx, start, SLC_Q_BLOCK, axis=2)
        okc = lax.dynamic_slice_in_dim(ok, start, SLC_Q_BLOCK, axis=2)
        k_sel = kb[bi, gi, ic]
        v_sel = vb[bi, gi, ic]
        s = jnp.einsum('bgjqd,bgqkld->bgjqkl', qc, k_sel).astype(jnp.float32) * ATTN_SCALE
        kpos = ic[..., None] * SLC_BLOCK + jnp.arange(SLC_BLOCK)
        tq = start + jnp.arange(SLC_Q_BLOCK)
        mask = (okc[..., None] & (kpos <= tq[None, None, :, None, None]))[:, :, None]
        s = jnp.where(mask, s, NEG_INF).reshape(B, G, J, SLC_Q_BLOCK, n_topk * SLC_BLOCK)
        p = jax.nn.softmax(s, axis=-1).reshape(B, G, J, SLC_Q_BLOCK, n_topk, SLC_BLOCK)
        return jnp.einsum('bgjqkl,bgqkld->bgjqd', p.astype(v_sel.dtype), v_sel)

    out = lax.map(one_block, jnp.arange(S // SLC_Q_BLOCK))
    return jnp.moveaxis(out, 0, 3).reshape(B, G, J, S, HD)


def window_attention(q, k, v):
    B, G, J, S, HD = q.shape
    nqb = S // WIN_Q_BLOCK
    nband = WINDOW // WIN_Q_BLOCK + 1
    pad = ((0, 0), (0, 0), (WINDOW, 0), (0, 0))
    kc = jnp.pad(k, pad).reshape(B, G, nqb + nband - 1, WIN_Q_BLOCK, HD)
    vc = jnp.pad(v, pad).reshape(B, G, nqb + nband - 1, WIN_Q_BLOCK, HD)
    kband = jnp.stack([kc[:, :, j:j + nqb] for j in range(nband)], axis=3).reshape(B, G, nqb, nband * WIN_Q_BLOCK, HD)
    vband = jnp.stack([vc[:, :, j:j + nqb] for j in range(nband)], axis=3).reshape(B, G, nqb, nband * WIN_Q_BLOCK, HD)
    qb = q.reshape(B, G, J, nqb, WIN_Q_BLOCK, HD)
    s = jnp.einsum('bgjnqd,bgnkd->bgjnqk', qb, kband).astype(jnp.float32) * ATTN_SCALE
    tpos = np.arange(nqb)[:, None] * WIN_Q_BLOCK + np.arange(WIN_Q_BLOCK)[None, :]
    kpos = np.arange(nqb)[:, None] * WIN_Q_BLOCK - WINDOW + np.arange(nband * WIN_Q_BLOCK)[None, :]
    mask = ((kpos[:, None, :] <= tpos[:, :, None]) & (kpos[:, None, :] > tpos[:, :, None] - WINDOW)
            & (kpos[:, None, :] >= 0))
    p = jax.nn.softmax(jnp.where(mask, s, NEG_INF), axis=-1)
    o = jnp.einsum('bgjnqk,bgnkd->bgjnqd', p.astype(vband.dtype), vband)
    return o.reshape(B, G, J, S, HD)


def hgrn2(q, z, i, g, lb, norm_w):
    B, S, H, _ = q.shape
    zf = z.astype(jnp.float32)
    log_f = jnp.logaddexp(jnp.log(lb), jnp.log1p(-lb) + jax.nn.log_sigmoid(zf))
    k = (1.0 - lb) * jax.nn.sigmoid(-zf)
    C = HGRN_CHUNK
    nch = S // C

    def chunks(t):
        return t.astype(jnp.float32).reshape(B, nch, C, H, t.shape[-1]).transpose(1, 0, 3, 2, 4)

    causal = np.tril(np.ones((C, C), dtype=bool))[:, :, None]

    def step(state, inp):
        qc, kc, vc, lfc = inp
        b = jnp.cumsum(lfc, axis=2)
        diff = b[:, :, :, None, :] - b[:, :, None, :, :]
        decay = jnp.where(causal, jnp.exp(jnp.where(causal, diff, 0.0)), 0.0)
        att = jnp.einsum('bhtd,bhsd,bhtsd->bhts', qc, kc, decay)
        o = jnp.einsum('bhts,bhse->bhte', att, vc) + jnp.einsum('bhtd,bhde->bhte', qc * jnp.exp(b), state)
        b_last = b[:, :, -1:, :]
        state = (jnp.exp(b_last[:, :, 0, :])[..., None] * state
                 + jnp.einsum('bhsd,bhse->bhde', kc * jnp.exp(b_last - b), vc))
        return state, o

    state0 = jnp.zeros((B, H, HGRN_DK, HGRN_DV), jnp.float32)
    _, o = lax.scan(step, state0, (chunks(q), chunks(k), chunks(i), chunks(log_f)))
    o = o.transpose(1, 0, 3, 2, 4).reshape(B, S, H, HGRN_DV)
    o = rms_norm(o, norm_w) * jax.nn.silu(g.astype(jnp.float32))
    return o.astype(g.dtype).reshape(B, S, H * HGRN_DV)


def hybrid_mixer(h, positions, w_in, cmp_pos, cmp_w1, cmp_w2, lb, g_norm_w, w_up_a, w_up_b, w_out):
    B, S, _ = h.shape
    G, J, HD = NSA_KV_GROUPS, NSA_HPG, HEAD_DIM
    (q, kc, vc, ks, vs, kw, vw, ng, hq, hf, hi, hg, ga, gb) = jnp.split(h @ w_in, IN_SPLITS, axis=-1)
    q = q.reshape(B, S, NSA_HEADS, HD)

    def grp_q(t):
        return t.reshape(B, S, G, J, HD).transpose(0, 2, 3, 1, 4)

    def kv(t):
        return t.reshape(B, S, G, HD)

    def grp_k(t):
        return t.transpose(0, 2, 1, 3)

    q_plain = grp_q(q)
    q_rot = grp_q(partial_rope(q, positions))
    k_cmp = compress_blocks(kv(kc), cmp_pos[0], cmp_w1[0], cmp_w2[0])
    v_cmp = compress_blocks(kv(vc), cmp_pos[1], cmp_w1[1], cmp_w2[1])
    o_cmp, imp = compressed_attention(q_plain, k_cmp, v_cmp)
    blk_idx, blk_ok = select_blocks(imp)
    o_slc = selected_attention(q_rot, grp_k(partial_rope(kv(ks), positions)), grp_k(kv(vs)), blk_idx, blk_ok)
    o_win = window_attention(q_rot, grp_k(partial_rope(kv(kw), positions)), grp_k(kv(vw)))
    gates = jax.nn.sigmoid(ng).reshape(B, S, 3, G, J).transpose(2, 0, 3, 4, 1)[..., None]
    o_nsa = gates[0] * o_cmp + gates[1] * o_slc + gates[2] * o_win
    o_nsa = o_nsa.transpose(0, 3, 1, 2, 4).reshape(B, S, NSA_WIDTH)
    o_hgrn = hgrn2(hq.reshape(B, S, HGRN_HEADS, HGRN_DK), hf.reshape(B, S, HGRN_HEADS, HGRN_DK),
                   hi.reshape(B, S, HGRN_HEADS, HGRN_DV), hg.reshape(B, S, HGRN_HEADS, HGRN_DV),
                   lb.reshape(HGRN_HEADS, HGRN_DK), g_norm_w)
    y = jax.nn.sigmoid(ga) * (o_nsa @ w_up_a) + jax.nn.sigmoid(gb) * (o_hgrn @ w_up_b)
    return y @ w_out


def setup_inputs(seed: int = 0) -> dict:
    key = jax.random.key(seed)
    ks = jax.random.split(key, 20)
    f32 = jnp.float32
    nrm = lambda k, shape, scale: jax.random.normal(k, shape, f32) * scale
    offset = jax.random.randint(ks[2], (BATCH, 1), 0, 1024)
    positions = (jnp.arange(SEQ, dtype=jnp.int32)[None, :] + offset).astype(jnp.int32)
    return {
        'x': nrm(ks[0], (BATCH, SEQ, D_MODEL), 1.0),
        'c': nrm(ks[1], (BATCH, D_MODEL), 1.0),
        'positions': positions,
        'ada_w': nrm(ks[3], (DEPTH, D_MODEL, 6 * D_MODEL), 0.5 * D_MODEL ** -0.5),
        'ada_b': nrm(ks[4], (DEPTH, 6 * D_MODEL), 0.02),
        'norm_mix_w': 1.0 + nrm(ks[5], (DEPTH, D_MODEL), 0.02),
        'w_in': nrm(ks[6], (DEPTH, D_MODEL, IN_WIDTH), D_MODEL ** -0.5),
        'nsa_cmp_pos': nrm(ks[7], (DEPTH, 2, CMP_LEN, HEAD_DIM), 0.02),
        'nsa_cmp_w1': nrm(ks[8], (DEPTH, 2, CMP_LEN, HEAD_DIM, HEAD_DIM), (CMP_LEN * HEAD_DIM) ** -0.5),
        'nsa_cmp_w2': nrm(ks[9], (DEPTH, 2, HEAD_DIM, HEAD_DIM), HEAD_DIM ** -0.5),
        'hgrn_lb_logits': nrm(ks[10], (DEPTH, HGRN_KW), 0.5),
        'hgrn_norm_w': 1.0 + nrm(ks[11], (DEPTH, HGRN_DV), 0.02),
        'w_up_a': nrm(ks[12], (DEPTH, NSA_WIDTH, D_MODEL), NSA_WIDTH ** -0.5),
        'w_up_b': nrm(ks[13], (DEPTH, HGRN_VW, D_MODEL), HGRN_VW ** -0.5),
        'w_out': nrm(ks[14], (DEPTH, D_MODEL, D_MODEL), D_MODEL ** -0.5),
        'norm_mlp_w': 1.0 + nrm(ks[15], (DEPTH, D_MODEL), 0.02),
        'w_mlp1': nrm(ks[16], (DEPTH, D_MODEL, D_FF), D_MODEL ** -0.5),
        'w_mlp2': nrm(ks[17], (DEPTH, D_FF, D_MODEL), D_FF ** -0.5),
        'final_norm_w': 1.0 + nrm(ks[18], (D_MODEL,), 0.02),
    }


def reference(x, c, positions, ada_w, ada_b, norm_mix_w, w_in, nsa_cmp_pos, nsa_cmp_w1, nsa_cmp_w2,
              hgrn_lb_logits, hgrn_norm_w, w_up_a, w_up_b, w_out, norm_mlp_w, w_mlp1, w_mlp2, final_norm_w):
    lb_all = jnp.cumsum(jax.nn.softmax(hgrn_lb_logits.astype(jnp.float32), axis=0), axis=0)
    lb_all = lb_all - lb_all[0:1]
    c_act = jax.nn.silu(c)
    for l in range(DEPTH):
        mod = c_act @ ada_w[l] + ada_b[l]
        sh1, sc1, g1, sh2, sc2, g2 = [m[:, None, :] for m in jnp.split(mod, 6, axis=-1)]
        h = rms_norm(x, norm_mix_w[l]) * (1.0 + sc1) + sh1
        x = x + g1 * hybrid_mixer(h, positions, w_in[l], nsa_cmp_pos[l], nsa_cmp_w1[l], nsa_cmp_w2[l],
                                  lb_all[l], hgrn_norm_w[l], w_up_a[l], w_up_b[l], w_out[l])
        h = rms_norm(x, norm_mlp_w[l]) * (1.0 + sc2) + sh2
        x = x + g2 * (jnp.square(jax.nn.relu(h @ w_mlp1[l])) @ w_mlp2[l])
    return rms_norm(x, final_norm_w)
```

```python
import contextlib
import numpy as np
import concourse.bass as bass
import concourse.mybir as mybir

F32 = mybir.dt.float32
BF16 = mybir.dt.bfloat16
I32 = mybir.dt.int32
U8 = mybir.dt.uint8
ALU = mybir.AluOpType
AF = mybir.ActivationFunctionType
AX = mybir.AxisListType


class Buf:
    __slots__ = ("name", "w", "r", "dsem")

    def __init__(self, name):
        self.name = name
        self.w = None
        self.r = []
        self.dsem = None


class Prog:
    ENG = ("sp", "act", "dve", "pool", "pe")

    def __init__(self, nc):
        self.nc = nc
        self.stack = contextlib.ExitStack()
        self.streams = {k: [] for k in self.ENG}
        self.sems = {}
        self.cnt = {}
        self.known = {k: {} for k in self.ENG}
        self.nbuf = 0
        for k in ("act", "dve", "pool", "pe"):
            self._newsem(k)

    def _newsem(self, key):
        s = self.stack.enter_context(self.nc.semaphore("s_" + key))
        self.sems[key] = s
        self.cnt[key] = 0
        return s

    def sbuf(self, name, shape, dtype):
        t = self.stack.enter_context(self.nc.sbuf_tensor("sb_" + name, list(shape), dtype))
        return t, Buf(name)

    def psum(self, name, shape, dtype=F32):
        t = self.stack.enter_context(self.nc.psum_tensor("ps_" + name, list(shape), dtype))
        return t, Buf(name)

    def buf(self, name):
        return Buf(name)

    def _need(self, eng, tok, deps):
        if tok is None:
            return
        k, v = tok
        if deps.get(k, 0) < v:
            deps[k] = v

    def _emit_waits(self, eng, deps):
        kn = self.known[eng]
        for k, v in deps.items():
            if k.startswith("d_"):
                v = max(v, self.cnt[k])
            if kn.get(k, 0) < v:
                kn[k] = v
                sem = self.sems[k]
                self.streams[eng].append(("wait", sem, v))

    def _deps(self, eng, reads, writes):
        deps = {}
        for b in reads:
            self._need(eng, b.w, deps)
        for b in writes:
            self._need(eng, b.w, deps)
            for t in b.r:
                self._need(eng, t, deps)
        self._emit_waits(eng, deps)

    def _commit(self, tok, reads, writes):
        for b in reads:
            b.r.append(tok)
            if len(b.r) > 64:
                m = {}
                for k, v in b.r:
                    if m.get(k, 0) < v:
                        m[k] = v
                b.r = list(m.items())
        for b in writes:
            b.w = tok
            b.r = []

    def op(self, eng, fn, reads=(), writes=()):
        self._deps(eng, reads, writes)
        self.cnt[eng] += 1
        tok = (eng, self.cnt[eng])
        self.streams[eng].append(("op", fn, self.sems[eng], 1))
        self._commit(tok, reads, writes)
        return tok

    def dma(self, eng, out, in_, sb, reads=(), writes=(), **kw):
        if sb.dsem is None:
            self.nbuf += 1
            sb.dsem = "d_%d_%s" % (self.nbuf, sb.name)
            self._newsem(sb.dsem)
        self._deps(eng, reads, writes)
        k = sb.dsem
        self.cnt[k] += 16
        tok = (k, self.cnt[k])
        self.streams[eng].append(("op", (lambda e: e.dma_start(out=out, in_=in_, **kw)), self.sems[k], 16))
        self._commit(tok, reads, writes)
        return tok

    def finish(self, eng="sp"):
        deps = {k: v for k, v in self.cnt.items() if v > 0}
        self._emit_waits(eng, deps)

    def emit(self):
        nc = self.nc
        with nc.Block() as block:
            decos = {"sp": block.sync, "act": block.scalar, "dve": block.vector,
                     "pool": block.gpsimd, "pe": block.tensor}
            for key in self.ENG:
                stream = self.streams[key]
                if not stream:
                    continue

                def body(e, stream=stream):
                    for it in stream:
                        if it[0] == "wait":
                            e.wait_ge(it[1], it[2])
                        else:
                            ins = it[1](e)
                            ins.then_inc(it[2], it[3])

                decos[key](body)

    def close(self):
        self.stack.close()


def simulate(prog):
    val = {id(s): 0 for s in prog.sems.values()}
    pos = {k: 0 for k in prog.ENG}
    progress = True
    while progress:
        progress = False
        for k in prog.ENG:
            st = prog.streams[k]
            while pos[k] < len(st):
                it = st[pos[k]]
                if it[0] == "wait":
                    if val[id(it[1])] >= it[2]:
                        pos[k] += 1
                        progress = True
                    else:
                        break
                else:
                    val[id(it[2])] += it[3]
                    pos[k] += 1
                    progress = True
    stuck = {k: (pos[k], len(prog.streams[k])) for k in prog.ENG if pos[k] < len(prog.streams[k])}
    return stuck


from concourse.bass_utils import run_bass_kernel_spmd
import math


f32 = np.float32
NEGV = -30000.0
def nsa_consts():
    S = 4096
    t = np.arange(S)
    c = {}
    c["ident"] = np.eye(128, dtype=f32)
    inv = (500000.0 ** (-np.arange(0, 32, 2, dtype=np.float32) / 32)).astype(f32)
    c["inv2"] = np.concatenate([inv, inv]).reshape(32, 1).astype(f32)
    pm = np.zeros((32, 32), f32)
    for d in range(16):
        pm[d + 16, d] = -1.0
        pm[d, d + 16] = 1.0
    c["pm"] = pm
    n = np.arange(256)
    end = n * 16 + 31
    vis = (end[:, None] <= t[None, :]) & (n[:, None] < 255)
    cm = np.where(vis, 0.0, NEGV).astype(f32).reshape(2, 128, S).transpose(1, 0, 2).reshape(128, 2 * S)
    c["cmpmask"] = np.ascontiguousarray(cm)
    cs = np.arange(255) * 16; ce = cs + 31
    ss = np.arange(64) * 64; se = ss + 63
    ov = np.maximum(np.minimum(ce[:, None], se[None, :]) - np.maximum(cs[:, None], ss[None, :]) + 1, 0).astype(f32)
    ov = np.concatenate([ov, np.zeros((1, 64), f32)], 0)
    c["ovt"] = np.ascontiguousarray(ov.reshape(2, 128, 64).transpose(1, 0, 2))
    blk = np.arange(64)
    cur = t // 64
    forced = (blk[None, :] == 0) | (blk[None, :] == cur[:, None]) | (blk[None, :] == cur[:, None] - 1)
    valid = blk[None, :] * 64 <= t[:, None]
    st = np.where(valid, np.where(forced, 1e6, 0.0), -1e9).astype(f32)
    c["seltab"] = np.ascontiguousarray(st.reshape(32, 128, 64).transpose(1, 0, 2)).reshape(128, 32 * 64)
    c["etab"] = (np.arange(S)[None, :] // 64 == blk[:, None]).astype(f32)
    p = np.arange(128); u = np.arange(512)
    dg = np.stack([np.where(128 * a + p[:, None] <= u[None, :], 0.0, NEGV) for a in range(4)], 1).astype(f32)
    c["diag"] = np.ascontiguousarray(dg).reshape(128, 4 * 512)
    bd = []
    for cc in range(8):
        key = 128 * (cc - 4) + p[:, None]
        ok = (key <= u[None, :]) & (key > u[None, :] - 512)
        bd.append(np.where(ok, 0.0, NEGV))
    c["band"] = np.ascontiguousarray(np.stack(bd, 1).astype(f32)).reshape(128, 8 * 512)
    return c


NCOL = 6144


def build_M():
    nc = bass.Bass("TRN2", target_bir_lowering=False)
    dt = nc.dram_tensor
    c = dt("c", [2, 4096], F32, kind="ExternalInput").ap()
    adaw = dt("adaw", [128, 32, NCOL], F32, kind="ExternalInput").ap()
    adab = dt("adab", [2, NCOL], F32, kind="ExternalInput").ap()
    mod = dt("mod", [2, NCOL], F32, kind="ExternalOutput").ap()
    p = Prog(nc)
    cT, cT_b = p.sbuf("cT", [128, 2, 32], F32)
    bias, bias_b = p.sbuf("bias", [2, NCOL], F32)
    res, res_b = p.sbuf("res", [2, NCOL], F32)
    wt = [p.sbuf("wt%d" % i, [128, 32, 512], F32) for i in range(2)]
    ps = [p.psum("ps%d" % i, [2, 512], F32) for i in range(2)]
    p.dma("sp", cT[:], c.rearrange("b (p kt) -> p b kt", kt=32), cT_b, writes=[cT_b])
    p.dma("sp", bias[:], adab, bias_b, writes=[bias_b])
    p.op("act", lambda e: e.activation(out=cT[:], in_=cT[:], func=AF.Silu), reads=[cT_b], writes=[cT_b])
    for ct in range(NCOL // 512):
        w, wb = wt[ct % 2]
        sl = slice(ct * 512, (ct + 1) * 512)
        p.dma("sp" if ct % 2 == 0 else "act", w[:], adaw[:, :, sl], wb, writes=[wb])
        pt, ptb = ps[ct % 2]

        def mm(e, w=w, pt=pt):
            r = None
            for kt in range(32):
                r = e.matmul(pt[:], lhsT=cT[:, :, kt], rhs=w[:, kt, :], start=(kt == 0), stop=(kt == 31))
            return r
        p.op("pe", mm, reads=[wb, cT_b], writes=[ptb])
        p.op("dve", lambda e, pt=pt, sl=sl: e.tensor_tensor(out=res[:, sl], in0=pt[:], in1=bias[:, sl], op=ALU.add),
             reads=[ptb, bias_b], writes=[res_b])
    p.dma("sp", mod, res[:], res_b, reads=[res_b])
    p.finish("sp")
    p.emit()
    p.close()
    return nc


EPS = 1e-6
NT = 1292
NTS = [(0, 512), (512, 512), (1024, 268)]


def build_A1():
    nc = bass.Bass("TRN2", target_bir_lowering=False)
    dt = nc.dram_tensor
    xT = dt("xT", [32, 128, 4096], F32, kind="ExternalInput").ap()
    vecs = dt("vecs", [128, 3, 32], F32, kind="ExternalInput").ap()
    wF = dt("wF", [16, 128, 4096], F32, kind="ExternalInput").ap()
    wT = dt("wT", [128, 32, NT], F32, kind="ExternalInput").ap()
    ones_d = dt("ones", [128, 128], F32, kind="ExternalInput").ap()
    outF = dt("outF", [16, 128, 4096], F32, kind="ExternalOutput").ap()
    outT = dt("outT", [4096, NT], F32, kind="ExternalOutput").ap()
    p = Prog(nc)
    ones, ones_b = p.sbuf("ones", [128, 128], BF16)
    vc, vc_b = p.sbuf("vc", [128, 3, 32], F32)
    A1, A1_b = p.sbuf("A1", [128, 32], F32)
    EPSB, epsb_b = p.sbuf("epsb", [128, 1], F32)
    rstd, rstd_b = p.sbuf("rstd", [128, 1024], F32)
    big0, big0_b = p.sbuf("big0", [128, 32, 1024], BF16)
    wTs, wTs_b = p.sbuf("wTs", [128, 32, NT], BF16)
    wf = [p.sbuf("wf%d" % i, [128, 4096], BF16) for i in range(2)]
    xb = [p.sbuf("xb%d" % i, [128, 1024], F32) for i in range(2)]
    tf = [p.sbuf("tf%d" % i, [128, 1024], F32) for i in range(2)]
    tb = [p.sbuf("tb%d" % i, [128, 1024], BF16) for i in range(2)]
    ot = [p.sbuf("ot%d" % i, [128, NT], F32) for i in range(2)]
    ps = [p.psum("ps%d" % i, [128, 512], F32) for i in range(8)]
    PI, PF = "pool", "sp"
    p.dma(PI, ones[:], ones_d, ones_b, writes=[ones_b])
    p.dma(PF, vc[:], vecs, vc_b, writes=[vc_b])
    p.dma(PI, wTs[:], wT, wTs_b, writes=[wTs_b])
    p.op("pool", lambda e: e.memset(EPSB[:], EPS), writes=[epsb_b])
    p.op("dve", lambda e: e.scalar_tensor_tensor(out=A1[:], in0=vc[:, 1, :], scalar=1.0, in1=vc[:, 2, :],
                                                 op0=ALU.add, op1=ALU.mult), reads=[vc_b], writes=[A1_b])
    xcnt = [0]

    def load_x(ap):
        i = xcnt[0] % 2
        xcnt[0] += 1
        t, b = xb[i]
        p.dma(PF, t[:], ap, b, writes=[b])
        return t, b

    fcnt = 0
    for ch in range(4):
        csl = slice(ch * 1024, (ch + 1) * 1024)
        for cg in range(32):
            t, b = load_x(xT[cg][:, csl])
            sq, sqb = tb[cg % 2]
            p.op("act", lambda e, t=t, sq=sq: e.activation(out=sq[:], in_=t[:], func=AF.Square), reads=[b], writes=[sqb])

            def mm(e, sq=sq, cg=cg):
                e.matmul(ps[6][0][:], lhsT=ones[:], rhs=sq[:, 0:512], start=(cg == 0), stop=(cg == 31))
                return e.matmul(ps[7][0][:], lhsT=ones[:], rhs=sq[:, 512:1024], start=(cg == 0), stop=(cg == 31))
            p.op("pe", mm, reads=[sqb, ones_b], writes=[ps[6][1], ps[7][1]])
        for h in range(2):
            sl = slice(h * 512, (h + 1) * 512)
            p.op("act", lambda e, h=h, sl=sl: e.activation(out=rstd[:, sl], in_=ps[6 + h][0][:], func=AF.Sqrt,
                                                          scale=1.0 / 4096, bias=EPSB[:, 0:1]),
                 reads=[ps[6 + h][1], epsb_b], writes=[rstd_b])
        p.op("dve", lambda e: e.reciprocal(out=rstd[:], in_=rstd[:]), reads=[rstd_b], writes=[rstd_b])
        for cg in range(32):
            t, b = load_x(xT[cg][:, csl])
            u, ub = tf[cg % 2]
            p.op("dve", lambda e, t=t, u=u: e.tensor_tensor(out=u[:], in0=t[:], in1=rstd[:], op=ALU.mult),
                 reads=[b, rstd_b], writes=[ub])
            p.op("act", lambda e, u=u, cg=cg: e.activation(out=big0[:, cg, :], in_=u[:], func=AF.Identity,
                                                          scale=A1[:, cg:cg + 1], bias=vc[:, 0, cg:cg + 1]),
                 reads=[ub, A1_b, vc_b], writes=[big0_b])
        for cf in range(16):
            w, wb = wf[fcnt % 2]
            pset = ps[2 * (fcnt % 2):2 * (fcnt % 2) + 2]
            o, ob = tf[fcnt % 2]
            fcnt += 1
            p.dma(PI, w[:], wF[cf], wb, writes=[wb])

            def mm(e, w=w, pset=pset):
                r = None
                for tt in range(2):
                    for kt in range(32):
                        r = e.matmul(pset[tt][0][:], lhsT=w[:, kt * 128:(kt + 1) * 128],
                                     rhs=big0[:, kt, tt * 512:(tt + 1) * 512], start=(kt == 0), stop=(kt == 31))
                return r
            p.op("pe", mm, reads=[wb, big0_b], writes=[pset[0][1], pset[1][1]])
            p.op("act", lambda e, o=o, pset=pset: e.activation(out=o[:, 0:512], in_=pset[0][0][:], func=AF.Copy),
                 reads=[pset[0][1]], writes=[ob])
            p.op("dve", lambda e, o=o, pset=pset: e.tensor_copy(out=o[:, 512:1024], in_=pset[1][0][:]),
                 reads=[pset[1][1], ob], writes=[ob])
            p.dma(PF, outF[cf][:, csl], o[:], ob, reads=[ob])
        for t8 in range(8):
            o, ob = ot[t8 % 2]
            for ni, (n0, nn) in enumerate(NTS):
                pt, ptb = ps[4 + (t8 * 3 + ni) % 2]

                def mm(e, pt=pt, t8=t8, n0=n0, nn=nn):
                    r = None
                    for kt in range(32):
                        r = e.matmul(pt[:, 0:nn], lhsT=big0[:, kt, t8 * 128:(t8 + 1) * 128], rhs=wTs[:, kt, n0:n0 + nn],
                                     start=(kt == 0), stop=(kt == 31))
                    return r
                p.op("pe", mm, reads=[big0_b, wTs_b], writes=[ptb])
                if ni % 2 == 0:
                    p.op("act", lambda e, o=o, pt=pt, n0=n0, nn=nn: e.activation(out=o[:, n0:n0 + nn], in_=pt[:, 0:nn], func=AF.Copy),
                         reads=[ptb, ob], writes=[ob])
                else:
                    p.op("dve", lambda e, o=o, pt=pt, n0=n0, nn=nn: e.tensor_copy(out=o[:, n0:n0 + nn], in_=pt[:, 0:nn]),
                         reads=[ptb, ob], writes=[ob])
            r0 = ch * 1024 + t8 * 128
            p.dma(PF, outT[r0:r0 + 128, :], o[:], ob, reads=[ob])
    p.finish("sp")
    p.emit()
    p.close()
    return nc


import math

SCALE = 128 ** -0.5
NEG = -30000.0
TWO_PI = 2 * math.pi


def build_A2():
    nc = bass.Bass("TRN2", target_bir_lowering=False)
    dt = nc.dram_tensor
    I = "ExternalInput"
    qT = dt("qT", [4, 128, 4096], F32, kind=I).ap()
    kT = dt("kT", [4, 128, 4096], F32, kind=I).ap()
    vs = dt("vs", [4096, 128], F32, kind=I).ap()
    vw = dt("vw", [4096, 128], F32, kind=I).ap()
    ng = dt("ng", [4096, 12], F32, kind=I).ap()
    pos = dt("pos", [32, 4096], I32, kind=I).ap()
    inv2 = dt("inv2", [32, 1], F32, kind=I).ap()
    pm = dt("pm", [32, 32], F32, kind=I).ap()
    cposT = dt("cposT", [2, 128, 32], F32, kind=I).ap()
    cw1 = dt("cw1", [2, 128, 32 * 128], F32, kind=I).ap()
    cw2 = dt("cw2", [2, 128, 128], F32, kind=I).ap()
    ident = dt("ident", [128, 128], F32, kind=I).ap()
    cmpmask = dt("cmpmask", [128, 2 * 4096], F32, kind=I).ap()
    ovt = dt("ovt", [128, 2, 64], F32, kind=I).ap()
    seltab = dt("seltab", [128, 32 * 64], F32, kind=I).ap()
    etab = dt("etab", [64, 4096], F32, kind=I).ap()
    diag = dt("diag", [128, 4 * 512], F32, kind=I).ap()
    band = dt("band", [128, 8 * 512], F32, kind=I).ap()
    onsa = dt("onsa", [4096, 512], F32, kind="ExternalOutput").ap()
    p = Prog(nc)
    PI, PF = "pool", "sp"
    QT, QT_b = p.sbuf("QT", [128, 4, 4096], BF16)
    KK = [p.sbuf("KK%d" % i, [128, 4096], BF16) for i in range(4)]
    (KC, KC_b), (VC, VC_b), (KS, KS_b), (KW, KW_b) = KK
    VS1, VS1_b = p.sbuf("VS1", [128, 32, 129], BF16)
    VW1, VW1_b = p.sbuf("VW1", [128, 32, 129], BF16)
    NG, NG_b = p.sbuf("NG", [128, 32, 12], F32)
    CM, CM_b = p.sbuf("CM", [128, 2 * 4096], BF16)
    ET, ET_b = p.sbuf("ET", [64, 4096], BF16)
    DG, DG_b = p.sbuf("DG", [128, 4 * 512], BF16)
    BD, BD_b = p.sbuf("BD", [128, 8 * 512], BF16)
    SELT, SELT_b = p.sbuf("SELT", [128, 32 * 64], F32)
    IDB, IDB_b = p.sbuf("IDB", [128, 128], BF16)
    PMb, PMb_b = p.sbuf("PMb", [32, 32], BF16)
    W1 = [p.sbuf("W1%d" % i, [128, 32 * 128], BF16) for i in range(2)]
    W2 = [p.sbuf("W2%d" % i, [128, 128], BF16) for i in range(2)]
    POSC = [p.sbuf("POSC%d" % i, [128, 32], BF16) for i in range(2)]
    KCT, KCT_b = p.sbuf("KCT", [128, 256], BF16)
    VC1, VC1_b = p.sbuf("VC1", [128, 2, 193], BF16)
    HID, HID_b = p.sbuf("HID", [128, 256], BF16)
    gt = [p.sbuf("gt%d" % i, [128, 256], F32) for i in range(3)]
    cb, cb_b = p.sbuf("cb", [128, 1], F32)
    POSI, POSI_b = p.sbuf("POSI", [32, 1024], I32)
    ANG, ANG_b = p.sbuf("ANG", [32, 1024], F32)
    COS, COS_b = p.sbuf("COS", [32, 4096], F32)
    SIN, SIN_b = p.sbuf("SIN", [32, 4096], F32)
    KI, KI_b = (POSI, POSI_b)
    iv, iv_b = p.sbuf("iv", [32, 1], F32)
    PT = [p.sbuf("PT%d" % i, [128, 512], BF16) for i in range(3)]
    OACC, OACC_b = p.sbuf("OACC", [128, 4, 512], F32)
    PSLC, PSLC_b = p.sbuf("PSLC", [128, 4, 64], F32)
    NMT, NMT_b = p.sbuf("NMT", [64, 512], BF16)
    sc1, sc1_b = p.sbuf("sc1", [128, 64], F32)
    sc2, sc2_b = p.sbuf("sc2", [128, 64], F32)
    nmk, nmk_b = p.sbuf("nmk", [128, 64], BF16)
    m8, m8_b = p.sbuf("m8", [128, 8], F32)
    rr = [p.sbuf("rr%d" % i, [128, 2], F32) for i in range(4)]
    rt = [p.sbuf("rt%d" % i, [32, 512], F32) for i in range(2)]
    psS = [p.psum("S%d" % i, [128, 512], F32) for i in range(2)]
    pv = [p.psum("pv%d" % i, [128, 512], F32) for i in range(4)]
    pm0, pm0_b = p.psum("m0", [128, 512], F32)
    pm1, pm1_b = p.psum("m1", [128, 1024], BF16)

    p.dma(PI, QT[:], qT.rearrange("j d t -> d j t"), QT_b, writes=[QT_b])
    for i in range(4):
        p.dma(PI, KK[i][0][:], kT[i], KK[i][1], writes=[KK[i][1]])
    p.dma(PI, VS1[:, :, 0:128], vs.rearrange("(kt p) d -> p kt d", p=128), VS1_b, writes=[VS1_b])
    p.dma(PI, VW1[:, :, 0:128], vw.rearrange("(kt p) d -> p kt d", p=128), VW1_b, writes=[VW1_b])
    p.op("pool", lambda e: e.memset(VS1[:, :, 128:129], 1.0), writes=[VS1_b])
    p.op("pool", lambda e: e.memset(VW1[:, :, 128:129], 1.0), writes=[VW1_b])
    p.dma(PF, NG[:], ng.rearrange("(tt p) c -> p tt c", p=128), NG_b, writes=[NG_b])
    p.dma(PI, CM[:], cmpmask, CM_b, writes=[CM_b])
    p.dma(PI, ET[:], etab, ET_b, writes=[ET_b])
    p.dma(PI, DG[:], diag, DG_b, writes=[DG_b])
    p.dma(PI, BD[:], band, BD_b, writes=[BD_b])
    p.dma(PF, SELT[:], seltab, SELT_b, writes=[SELT_b])
    p.dma(PI, IDB[:], ident, IDB_b, writes=[IDB_b])
    p.dma(PI, PMb[:], pm, PMb_b, writes=[PMb_b])
    for w in range(2):
        p.dma(PI, W1[w][0][:], cw1[w], W1[w][1], writes=[W1[w][1]])
        p.dma(PI, W2[w][0][:], cw2[w], W2[w][1], writes=[W2[w][1]])
        p.dma(PI, POSC[w][0][:], cposT[w], POSC[w][1], writes=[POSC[w][1]])
    p.dma(PI, VC1[:, :, 129:193], ovt, VC1_b, writes=[VC1_b])
    p.op("pool", lambda e: e.memset(VC1[:, :, 128:129], 1.0), writes=[VC1_b])
    p.dma(PF, iv[:], inv2, iv_b, writes=[iv_b])
    p.op("act", lambda e: e.activation(out=NG[:], in_=NG[:], func=AF.Sigmoid), reads=[NG_b], writes=[NG_b])

    for qq in range(4):
        qs = slice(qq * 1024, (qq + 1) * 1024)
        p.dma(PF, POSI[:], pos[:, qs], POSI_b, writes=[POSI_b])
        p.op("dve", lambda e: e.tensor_copy(out=ANG[:], in_=POSI[:]), reads=[POSI_b], writes=[ANG_b])
        p.op("dve", lambda e: e.tensor_scalar(out=ANG[:], in0=ANG[:], scalar1=iv[:, 0:1], scalar2=None, op0=ALU.mult),
             reads=[ANG_b, iv_b], writes=[ANG_b])
        for (TAB, TAB_b, shift) in ((SIN, SIN_b, 0.0), (COS, COS_b, math.pi / 2)):
            p.op("dve", lambda e, shift=shift: e.tensor_scalar(out=KI[:], in0=ANG[:], scalar1=shift, scalar2=1.0 / TWO_PI,
                                                              op0=ALU.add, op1=ALU.mult), reads=[ANG_b], writes=[KI_b])
            p.op("dve", lambda e, TAB=TAB, qs=qs: e.tensor_copy(out=TAB[:, qs], in_=KI[:]), reads=[KI_b], writes=[TAB_b])
            p.op("dve", lambda e, TAB=TAB, qs=qs: e.scalar_tensor_tensor(out=TAB[:, qs], in0=TAB[:, qs], scalar=-TWO_PI, in1=ANG[:],
                                                                 op0=ALU.mult, op1=ALU.add), reads=[TAB_b, ANG_b], writes=[TAB_b])
            p.op("dve", lambda e, TAB=TAB, shift=shift, qs=qs: e.tensor_scalar(out=TAB[:, qs], in0=TAB[:, qs], scalar1=shift, scalar2=3.1415925,
                                                                       op0=ALU.add, op1=ALU.min), reads=[TAB_b], writes=[TAB_b])
            p.op("dve", lambda e, TAB=TAB, qs=qs: e.tensor_scalar(out=TAB[:, qs], in0=TAB[:, qs], scalar1=-3.1415925, scalar2=None, op0=ALU.max),
                 reads=[TAB_b], writes=[TAB_b])
    for (TAB, TAB_b) in ((SIN, SIN_b), (COS, COS_b)):
        p.op("act", lambda e, TAB=TAB: e.activation(out=TAB[:], in_=TAB[:], func=AF.Sin), reads=[TAB_b], writes=[TAB_b])

    ropecnt = [0]

    def rope(X32, Xb, tsl):
        i = ropecnt[0] % 2
        ropecnt[0] += 1
        t1, t1b = rt[i]
        p.op("pe", lambda e: e.matmul(pm0[0:32, :], lhsT=PMb[:], rhs=X32, start=True, stop=True),
             reads=[Xb, PMb_b], writes=[pm0_b])
        p.op("dve", lambda e: e.tensor_tensor(out=t1[:], in0=X32, in1=COS[:, tsl], op=ALU.mult), reads=[Xb, COS_b], writes=[t1b])
        p.op("dve", lambda e: e.tensor_tensor(out=X32, in0=pm0[0:32, :], in1=SIN[:, tsl], op=ALU.mult),
             reads=[pm0_b, SIN_b, Xb], writes=[Xb])
        p.op("dve", lambda e: e.tensor_tensor(out=X32, in0=X32, in1=t1[:], op=ALU.add), reads=[Xb, t1b], writes=[Xb])

    for k in range(8):
        tsl = slice(k * 512, (k + 1) * 512)
        rope(KS[0:32, tsl], KS_b, tsl)
        rope(KW[0:32, tsl], KW_b, tsl)

    for w, (SRC, SRC_b) in enumerate(((KC, KC_b), (VC, VC_b))):
        W1t, W1b = W1[w]
        W2t, W2b = W2[w]
        PCt, PCb = POSC[w]

        def mb(e, W1t=W1t, PCt=PCt):
            r = None
            for l in range(32):
                r = e.matmul(pm0[:, 0:1], lhsT=W1t[:, l * 128:(l + 1) * 128], rhs=PCt[:, l:l + 1], start=(l == 0), stop=(l == 31))
            return r
        p.op("pe", mb, reads=[W1b, PCb], writes=[pm0_b])
        p.op("dve", lambda e: e.tensor_copy(out=cb[:], in_=pm0[:, 0:1]), reads=[pm0_b], writes=[cb_b])

        def mh(e, W1t=W1t, SRC=SRC):
            r = None
            for l in range(32):
                r = e.matmul(pm0[:, 0:255], lhsT=W1t[:, l * 128:(l + 1) * 128], rhs=SRC[:, l:l + 16 * 254 + 1:16],
                             start=(l == 0), stop=(l == 31))
            return r
        p.op("pe", mh, reads=[W1b, SRC_b], writes=[pm0_b])
        u, ub = gt[0]
        v_, vb = gt[1]
        w_, wb_ = gt[2]
        p.op("act", lambda e: e.activation(out=u[:, 0:255], in_=pm0[:, 0:255], func=AF.Identity, bias=cb[:, 0:1]),
             reads=[pm0_b, cb_b], writes=[ub])
        p.op("dve", lambda e: e.tensor_tensor(out=v_[:, 0:255], in0=u[:, 0:255], in1=u[:, 0:255], op=ALU.mult), reads=[ub], writes=[vb])
        p.op("dve", lambda e: e.tensor_scalar(out=v_[:, 0:255], in0=v_[:, 0:255], scalar1=0.044715, scalar2=1.0, op0=ALU.mult, op1=ALU.add),
             reads=[vb], writes=[vb])
        p.op("dve", lambda e: e.tensor_tensor(out=v_[:, 0:255], in0=v_[:, 0:255], in1=u[:, 0:255], op=ALU.mult), reads=[vb, ub], writes=[vb])
        p.op("act", lambda e: e.activation(out=w_[:, 0:255], in_=v_[:, 0:255], func=AF.Tanh, scale=0.7978845608028654),
             reads=[vb], writes=[wb_])
        p.op("dve", lambda e: e.tensor_scalar(out=w_[:, 0:255], in0=w_[:, 0:255], scalar1=0.5, scalar2=0.5, op0=ALU.mult, op1=ALU.add),
             reads=[wb_], writes=[wb_])
        p.op("pool", lambda e: e.memset(HID[:], 0.0), reads=[HID_b], writes=[HID_b])
        p.op("dve", lambda e: e.tensor_tensor(out=HID[:, 0:255], in0=w_[:, 0:255], in1=u[:, 0:255], op=ALU.mult),
             reads=[wb_, ub, HID_b], writes=[HID_b])
        if w == 0:
            p.op("pe", lambda e, W2t=W2t: e.matmul(pm0[:, 0:256], lhsT=W2t[:], rhs=HID[:], start=True, stop=True),
                 reads=[W2b, HID_b], writes=[pm0_b])
            p.op("act", lambda e: e.activation(out=KCT[:], in_=pm0[:, 0:256], func=AF.Copy), reads=[pm0_b], writes=[KCT_b])
        else:
            for nt in range(2):
                p.op("pe", lambda e, W2t=W2t, nt=nt: e.matmul(pm0[:, 0:128], lhsT=HID[:, nt * 128:(nt + 1) * 128], rhs=W2t[:],
                                                             start=True, stop=True), reads=[W2b, HID_b], writes=[pm0_b])
                p.op("act", lambda e, nt=nt: e.activation(out=VC1[:, nt, 0:128], in_=pm0[:, 0:128], func=AF.Copy),
                     reads=[pm0_b], writes=[VC1_b])

    cnt = {"s": 0, "pt": 0, "rr": 0}

    def branch(k, j, br, tiles, nv, first_branch):
        tsl = slice(k * 512, (k + 1) * 512)
        first = {}
        last = {}
        for ti, tl in enumerate(tiles):
            for ts in tl[3]:
                first.setdefault(ts, ti)
                last[ts] = ti
        for ti, (Kap, masks, V1ap, ts_list, rbufs) in enumerate(tiles):
            sp_, spb = psS[cnt["s"] % 2]
            cnt["s"] += 1
            pt, ptb = PT[cnt["pt"] % 3]
            cnt["pt"] += 1

            def ms(e, sp_=sp_, Kap=Kap, masks=masks):
                r = e.matmul(sp_[:], lhsT=Kap, rhs=QT[:, j, tsl], start=True, stop=(len(masks) == 0))
                for mi, (ml, mr) in enumerate(masks):
                    r = e.matmul(sp_[:], lhsT=ml, rhs=mr, start=False, stop=(mi == len(masks) - 1))
                return r
            p.op("pe", ms, reads=[QT_b] + rbufs, writes=[spb])
            p.op("act", lambda e, pt=pt, sp_=sp_: e.activation(out=pt[:], in_=sp_[:], func=AF.Exp, scale=SCALE),
                 reads=[spb], writes=[ptb])

            def mv(e, pt=pt, V1ap=V1ap, ts_list=ts_list, ti=ti):
                r = None
                for ts in ts_list:
                    r = e.matmul(pv[ts][0][:, 0:nv], lhsT=pt[:, ts * 128:(ts + 1) * 128], rhs=V1ap,
                                 start=(first[ts] == ti), stop=(last[ts] == ti))
                return r
            p.op("pe", mv, reads=[ptb] + rbufs, writes=[pv[ts][1] for ts in ts_list])
        for ts in range(4):
            if ts not in first:
                continue
            tt = 4 * k + ts
            r_, rb = rr[cnt["rr"] % 4]
            cnt["rr"] += 1
            pvt, pvb = pv[ts]
            p.op("dve", lambda e, r_=r_, pvt=pvt: e.tensor_scalar(out=r_[:, 0:1], in0=pvt[:, 128:129], scalar1=1e-30, scalar2=None, op0=ALU.add),
                 reads=[pvb], writes=[rb])
            p.op("dve", lambda e, r_=r_: e.reciprocal(out=r_[:, 0:1], in_=r_[:, 0:1]), reads=[rb], writes=[rb])
            p.op("dve", lambda e, r_=r_, tt=tt: e.tensor_tensor(out=r_[:, 1:2], in0=r_[:, 0:1], in1=NG[:, tt, br * 4 + j:br * 4 + j + 1], op=ALU.mult),
                 reads=[rb, NG_b], writes=[rb])
            osl = OACC[:, ts, j * 128:(j + 1) * 128]
            if first_branch:
                p.op("dve", lambda e, osl=osl, pvt=pvt, r_=r_: e.tensor_scalar(out=osl, in0=pvt[:, 0:128], scalar1=r_[:, 1:2], scalar2=None, op0=ALU.mult),
                     reads=[pvb, rb, OACC_b], writes=[OACC_b])
            else:
                p.op("dve", lambda e, osl=osl, pvt=pvt, r_=r_: e.scalar_tensor_tensor(out=osl, in0=pvt[:, 0:128], scalar=r_[:, 1:2], in1=osl,
                                                                                   op0=ALU.mult, op1=ALU.add),
                     reads=[pvb, rb, OACC_b], writes=[OACC_b])
            if br == 0:
                if j == 0:
                    p.op("dve", lambda e, ts=ts, pvt=pvt, r_=r_: e.tensor_scalar(out=PSLC[:, ts, :], in0=pvt[:, 129:193], scalar1=r_[:, 0:1], scalar2=None, op0=ALU.mult),
                         reads=[pvb, rb, PSLC_b], writes=[PSLC_b])
                else:
                    p.op("dve", lambda e, ts=ts, pvt=pvt, r_=r_: e.scalar_tensor_tensor(out=PSLC[:, ts, :], in0=pvt[:, 129:193], scalar=r_[:, 0:1], in1=PSLC[:, ts, :],
                                                                                      op0=ALU.mult, op1=ALU.add),
                         reads=[pvb, rb, PSLC_b], writes=[PSLC_b])

    onsa_v = onsa.rearrange("(tt p) c -> p tt c", p=128)
    for k in range(8):
        tsl = slice(k * 512, (k + 1) * 512)
        for j in range(4):
            tiles = []
            for nt in range(2 if k >= 4 else 1):
                partial = (nt == 0 and k <= 4) or (nt == 1)
                masks = [(IDB[:], CM[:, nt * 4096 + k * 512:nt * 4096 + (k + 1) * 512])] if partial else []
                tiles.append((KCT[:, nt * 128:(nt + 1) * 128], masks, VC1[:, nt, :], [0, 1, 2, 3], [KCT_b, VC1_b, IDB_b, CM_b]))
            branch(k, j, 0, tiles, 193, True)
        for ts in range(4):
            tt = 4 * k + ts
            p.op("dve", lambda e, ts=ts, tt=tt: e.tensor_tensor(out=sc1[:], in0=PSLC[:, ts, :], in1=SELT[:, tt * 64:(tt + 1) * 64], op=ALU.add),
                 reads=[PSLC_b, SELT_b], writes=[sc1_b])
            p.op("dve", lambda e: e.max(out=m8[:], in_=sc1[:]), reads=[sc1_b], writes=[m8_b])
            p.op("dve", lambda e: e.match_replace(out=sc2[:], in_to_replace=m8[:], in_values=sc1[:], imm_value=-3.0e38),
                 reads=[sc1_b, m8_b], writes=[sc2_b])
            p.op("dve", lambda e: e.max(out=m8[:], in_=sc2[:]), reads=[sc2_b], writes=[m8_b])
            p.op("dve", lambda e: e.tensor_scalar(out=sc2[:], in0=sc1[:], scalar1=m8[:, 7:8], scalar2=None, op0=ALU.is_ge),
                 reads=[sc1_b, m8_b], writes=[sc2_b])
            p.op("dve", lambda e: e.tensor_scalar(out=nmk[:], in0=sc2[:], scalar1=-NEG, scalar2=NEG, op0=ALU.mult, op1=ALU.add),
                 reads=[sc2_b], writes=[nmk_b])
            p.op("pe", lambda e: e.transpose(out=pm1[0:64, 0:128], in_=nmk[:], identity=IDB[:]), reads=[nmk_b, IDB_b], writes=[pm1_b])
            p.op("act", lambda e, ts=ts: e.activation(out=NMT[:, ts * 128:(ts + 1) * 128], in_=pm1[0:64, 0:128], func=AF.Copy),
                 reads=[pm1_b], writes=[NMT_b])
        for j in range(4):
            rope(QT[0:32, j, tsl], QT_b, tsl)
        for j in range(4):
            tiles = []
            for kt in range(4 * k + 4):
                a = kt - 4 * k
                masks = [(ET[:, kt * 128:(kt + 1) * 128], NMT[:])]
                if a >= 0:
                    masks.append((IDB[:], DG[:, a * 512:(a + 1) * 512]))
                ts_list = [ts for ts in range(4) if a < 0 or ts >= a]
                tiles.append((KS[:, kt * 128:(kt + 1) * 128], masks, VS1[:, kt, :], ts_list, [KS_b, VS1_b, ET_b, NMT_b, IDB_b, DG_b]))
            branch(k, j, 1, tiles, 129, False)
        for j in range(4):
            tiles = []
            for c in range(8):
                kt = 4 * k - 4 + c
                if kt < 0:
                    continue
                masks = [(IDB[:], BD[:, c * 512:(c + 1) * 512])]
                ts_list = [ts for ts in range(4) if ts <= c <= ts + 4]
                tiles.append((KW[:, kt * 128:(kt + 1) * 128], masks, VW1[:, kt, :], ts_list, [KW_b, VW1_b, IDB_b, BD_b]))
            branch(k, j, 2, tiles, 129, False)
        p.dma(PF, onsa_v[:, 4 * k:4 * k + 4, :], OACC[:], OACC_b, reads=[OACC_b])
    p.finish("sp")
    p.emit()
    p.close()
    return nc


EPS = 1e-6
C = 64
NCH = 64


def build_A3(layer, ST=('pre', 'loop', 'post'), NH=4, NCL=NCH):
    nc = bass.Bass("TRN2", target_bir_lowering=False)
    dt = nc.dram_tensor
    hqT = dt("hqT", [4, 128, 4096], F32, kind="ExternalInput").ap()
    hfT = dt("hfT", [4, 128, 4096], F32, kind="ExternalInput").ap()
    hi = dt("hi", [4096, 512], F32, kind="ExternalInput").ap()
    hg = dt("hg", [4096, 512], F32, kind="ExternalInput").ap()
    lbl = dt("lbl", [128, 2, 4], F32, kind="ExternalInput").ap()
    nw64 = dt("nw64", [64, 128], F32, kind="ExternalInput").ap()
    rmask = dt("rmask", [128, 4096], F32, kind="ExternalInput").ap()
    tri = dt("tri", [64, 64], F32, kind="ExternalInput").ap()
    ident = dt("ident", [128, 128], F32, kind="ExternalInput").ap()
    ohg = dt("ohg", [4096, 512], F32, kind="ExternalOutput").ap()
    p = Prog(nc)
    A, A_b = p.sbuf("A", [128, 4096], F32)
    B, B_b = p.sbuf("B", [128, 4096], F32)
    Cc, C_b = p.sbuf("C", [128, 4096], F32)
    D, D_b = p.sbuf("D", [128, 4096], F32)
    HQ, HQ_b = p.sbuf("HQ", [128, 4096], BF16)
    Q1, Q1_b = p.sbuf("Q1", [128, 4096], BF16)
    K1, K1_b = p.sbuf("K1", [128, 4096], BF16)
    Q2, Q2_b = p.sbuf("Q2", [128, 4096], BF16)
    K2, K2_b = p.sbuf("K2", [128, 4096], BF16)
    V, V_b = p.sbuf("V", [64, NCH, 128], BF16)
    G, G_b = p.sbuf("G", [64, NCH, 128], BF16)
    O, O_b = p.sbuf("O", [64, NCH, 128], F32)
    RM, RM_b = p.sbuf("RM", [128, 4096], BF16)
    lb, lb_b = p.sbuf("lb", [128, 2, 4], F32)
    lbv, lbv_b = p.sbuf("lbv", [128, 4], F32)
    oml, oml_b = p.sbuf("oml", [128, 4], F32)
    nw, nw_b = p.sbuf("nw", [64, 128], F32)
    trif, trif_b = p.sbuf("trif", [64, 64], F32)
    triu, triu_b = p.sbuf("triu", [64, 64], U8)
    idb, idb_b = p.sbuf("idb", [128, 128], BF16)
    EBL, EBL_b = p.sbuf("EBL", [128, NCH], F32)
    ssq, ssq_b = p.sbuf("ssq", [64, NCH], F32)
    epsb, epsb_b = p.sbuf("epsb", [64, 1], F32)
    Sf, Sf_b = p.sbuf("Sf", [128, 128], F32)
    Sbf = [p.sbuf("Sbf%d" % i, [128, 128], BF16) for i in range(2)]
    attS = [p.sbuf("attS%d" % i, [64, 64], BF16) for i in range(2)]
    khat = [p.sbuf("khat%d" % i, [64, 128], BF16) for i in range(2)]
    ps_att = [p.psum("att%d" % i, [128, 512], F32) for i in range(2)]
    ps_kh = [p.psum("kh%d" % i, [128, 1024], BF16) for i in range(2)]
    ps_o = [p.psum("o%d" % i, [128, 512], F32) for i in range(2)]
    ps_sn = [p.psum("sn%d" % i, [128, 512], F32) for i in range(2)]
    PI, PF = "pool", "sp"
    p.dma(PI, RM[:], rmask, RM_b, writes=[RM_b])
    p.dma(PI, idb[:], ident, idb_b, writes=[idb_b])
    p.dma(PF, trif[:], tri, trif_b, writes=[trif_b])
    p.dma(PF, lb[:], lbl, lb_b, writes=[lb_b])
    p.dma(PF, nw[:], nw64, nw_b, writes=[nw_b])
    p.op("dve", lambda e: e.tensor_copy(out=triu[:], in_=trif[:]), reads=[trif_b], writes=[triu_b])
    p.op("pool", lambda e: e.memset(epsb[:], EPS), writes=[epsb_b])
    for i in range(2):
        p.op("pool", lambda e, i=i: e.memset(attS[i][0][:], 0.0), writes=[attS[i][1]])
    p.op("dve", lambda e: e.tensor_tensor(out=lbv[:], in0=lb[:, 1, :], in1=lb[:, 0, :], op=ALU.subtract), reads=[lb_b], writes=[lbv_b])
    p.op("act", lambda e: e.activation(out=lbv[:], in_=lbv[:], func=AF.Sigmoid), reads=[lbv_b], writes=[lbv_b])
    p.op("dve", lambda e: e.tensor_scalar(out=lbv[:], in0=lbv[:], scalar1=float(layer), scalar2=None, op0=ALU.mult),
         reads=[lbv_b], writes=[lbv_b])
    p.op("dve", lambda e: e.tensor_scalar(out=oml[:], in0=lbv[:], scalar1=-1.0, scalar2=1.0, op0=ALU.mult, op1=ALU.add),
         reads=[lbv_b], writes=[oml_b])

    def v3(t):
        return t[:].rearrange("p (c j) -> p c j", j=C)

    hi_v = hi.rearrange("(c p) e -> p c e", p=C)
    hg_v = hg.rearrange("(c p) e -> p c e", p=C)
    ohg_v = ohg.rearrange("(c p) e -> p c e", p=C)
    for hh in range(NH):
        hs = slice(hh * 128, (hh + 1) * 128)
        p.dma(PF, A[:], hfT[hh], A_b, writes=[A_b])
        p.dma(PI, HQ[:], hqT[hh], HQ_b, writes=[HQ_b])
        p.dma(PI, V[:], hi_v[:, :, hs], V_b, writes=[V_b])
        p.dma(PI, G[:], hg_v[:, :, hs], G_b, writes=[G_b])
        if 'pre' in ST:
            p.op("act", lambda e: e.activation(out=A[:], in_=A[:], func=AF.Sigmoid), reads=[A_b], writes=[A_b])
            p.op("dve", lambda e, hh=hh: e.tensor_scalar(out=A[:], in0=A[:], scalar1=oml[:, hh:hh + 1], scalar2=lbv[:, hh:hh + 1],
                                                        op0=ALU.mult, op1=ALU.add), reads=[A_b, oml_b, lbv_b], writes=[A_b])
            p.op("act", lambda e: e.activation(out=B[:], in_=A[:], func=AF.Ln), reads=[A_b], writes=[B_b])
            p.op("dve", lambda e: e.tensor_scalar(out=A[:], in0=A[:], scalar1=-1.0, scalar2=1.0, op0=ALU.mult, op1=ALU.add),
                 reads=[A_b], writes=[A_b])
            p.op("dve", lambda e: e.tensor_tensor_scan(out=Cc[:], data0=RM[:], data1=B[:], initial=0.0, op0=ALU.mult, op1=ALU.add),
                 reads=[RM_b, B_b], writes=[C_b])
            p.op("dve", lambda e: e.tensor_tensor(out=v3(B), in0=v3(Cc), in1=v3(Cc)[:, :, 31:32].to_broadcast([128, NCH, C]),
                                                  op=ALU.subtract), reads=[C_b], writes=[B_b])
            p.op("act", lambda e: e.activation(out=D[:], in_=B[:], func=AF.Exp), reads=[B_b], writes=[D_b])
            p.op("dve", lambda e: e.tensor_tensor(out=Q1[:], in0=HQ[:], in1=D[:], op=ALU.mult), reads=[HQ_b, D_b], writes=[Q1_b])
            p.op("act", lambda e: e.activation(out=D[:], in_=B[:], func=AF.Exp, scale=-1.0), reads=[B_b], writes=[D_b])
            p.op("dve", lambda e: e.tensor_tensor(out=K1[:], in0=A[:], in1=D[:], op=ALU.mult), reads=[A_b, D_b], writes=[K1_b])
            p.op("act", lambda e: e.activation(out=D[:], in_=Cc[:], func=AF.Exp), reads=[C_b], writes=[D_b])
            p.op("dve", lambda e: e.tensor_tensor(out=Q2[:], in0=HQ[:], in1=D[:], op=ALU.mult), reads=[HQ_b, D_b], writes=[Q2_b])
            p.op("dve", lambda e: e.tensor_tensor(out=v3(B), in0=v3(Cc)[:, :, 63:64].to_broadcast([128, NCH, C]), in1=v3(Cc),
                                                  op=ALU.subtract), reads=[C_b], writes=[B_b])
            p.op("act", lambda e: e.activation(out=D[:], in_=B[:], func=AF.Exp), reads=[B_b], writes=[D_b])
            p.op("dve", lambda e: e.tensor_tensor(out=K2[:], in0=A[:], in1=D[:], op=ALU.mult), reads=[A_b, D_b], writes=[K2_b])
            p.op("act", lambda e: e.activation(out=EBL[:], in_=v3(Cc)[:, :, 63], func=AF.Exp), reads=[C_b], writes=[EBL_b])
        p.op("pool", lambda e: e.memset(Sf[:], 0.0), writes=[Sf_b])
        p.op("pool", lambda e: e.memset(Sbf[0][0][:], 0.0), writes=[Sbf[0][1]])
        for c in range(NCL if 'loop' in ST else 0):
            cs = slice(c * C, (c + 1) * C)
            i = c % 2
            pa, pab = ps_att[i]
            pk, pkb = ps_kh[i]
            po, pob = ps_o[i]
            pn, pnb = ps_sn[i]
            at, atb = attS[i]
            kh, khb = khat[i]
            sb_cur, sb_curb = Sbf[c % 2]
            sb_nxt, sb_nxtb = Sbf[(c + 1) % 2]
            p.op("pe", lambda e, pa=pa, cs=cs: e.matmul(pa[0:C, 0:C], lhsT=K1[:, cs], rhs=Q1[:, cs], start=True, stop=True),
                 reads=[K1_b, Q1_b], writes=[pab])
            p.op("dve", lambda e, at=at, pa=pa: e.copy_predicated(out=at[:], mask=triu[:], data=pa[0:C, 0:C]),
                 reads=[pab, triu_b, atb], writes=[atb])
            p.op("pe", lambda e, pk=pk, cs=cs: e.transpose(out=pk[0:C, 0:128], in_=K2[:, cs], identity=idb[:]),
                 reads=[K2_b, idb_b], writes=[pkb])
            p.op("act", lambda e, kh=kh, pk=pk: e.activation(out=kh[:], in_=pk[0:C, 0:128], func=AF.Copy),
                 reads=[pkb], writes=[khb])

            def mo(e, po=po, at=at, c=c, cs=cs, sb_cur=sb_cur):
                e.matmul(po[0:C, 0:128], lhsT=at[:], rhs=V[:, c, :], start=True, stop=False)
                return e.matmul(po[0:C, 0:128], lhsT=Q2[:, cs], rhs=sb_cur[:], start=False, stop=True)
            p.op("pe", mo, reads=[atb, V_b, Q2_b, sb_curb], writes=[pob])
            p.op("act", lambda e, po=po, c=c: e.activation(out=O[:, c, :], in_=po[0:C, 0:128], func=AF.Copy),
                 reads=[pob], writes=[O_b])
            p.op("pe", lambda e, pn=pn, kh=kh, c=c: e.matmul(pn[:, 0:128], lhsT=kh[:], rhs=V[:, c, :], start=True, stop=True),
                 reads=[khb, V_b], writes=[pnb])
            p.op("dve", lambda e, pn=pn, c=c: e.scalar_tensor_tensor(out=Sf[:], in0=Sf[:], scalar=EBL[:, c:c + 1], in1=pn[:, 0:128],
                                                                    op0=ALU.mult, op1=ALU.add),
                 reads=[Sf_b, EBL_b, pnb], writes=[Sf_b])
            p.op("act", lambda e, sb_nxt=sb_nxt: e.activation(out=sb_nxt[:], in_=Sf[:], func=AF.Copy),
                 reads=[Sf_b], writes=[sb_nxtb])
        if 'post' in ST:
            A64 = A[0:64, :].rearrange("p (c e) -> p c e", e=128)
            B64 = B[0:64, :].rearrange("p (c e) -> p c e", e=128)
            p.op("dve", lambda e: e.tensor_tensor(out=A64, in0=O[:, 0:32, :], in1=O[:, 0:32, :], op=ALU.mult), reads=[O_b], writes=[A_b])
            p.op("dve", lambda e: e.tensor_tensor(out=B64, in0=O[:, 32:64, :], in1=O[:, 32:64, :], op=ALU.mult), reads=[O_b], writes=[B_b])
            p.op("dve", lambda e: e.tensor_reduce(out=ssq[:, 0:32], in_=A64, axis=AX.X, op=ALU.add), reads=[A_b], writes=[ssq_b])
            p.op("dve", lambda e: e.tensor_reduce(out=ssq[:, 32:64], in_=B64, axis=AX.X, op=ALU.add), reads=[B_b, ssq_b], writes=[ssq_b])
            p.op("act", lambda e: e.activation(out=ssq[:], in_=ssq[:], func=AF.Sqrt, scale=1.0 / 128, bias=epsb[:, 0:1]),
                 reads=[ssq_b, epsb_b], writes=[ssq_b])
            p.op("dve", lambda e: e.reciprocal(out=ssq[:], in_=ssq[:]), reads=[ssq_b], writes=[ssq_b])
            p.op("dve", lambda e: e.tensor_tensor(out=O[:], in0=O[:], in1=ssq[:].unsqueeze(2).to_broadcast([64, NCH, 128]), op=ALU.mult),
                 reads=[O_b, ssq_b], writes=[O_b])
            p.op("dve", lambda e: e.tensor_tensor(out=O[:], in0=O[:], in1=nw[:].unsqueeze(1).to_broadcast([64, NCH, 128]), op=ALU.mult),
                 reads=[O_b, nw_b], writes=[O_b])
            p.op("act", lambda e: e.activation(out=A64, in_=G[:, 0:32, :], func=AF.Silu), reads=[G_b, A_b], writes=[A_b])
            p.op("act", lambda e: e.activation(out=B64, in_=G[:, 32:64, :], func=AF.Silu), reads=[G_b, B_b], writes=[B_b])
            p.op("dve", lambda e: e.tensor_tensor(out=O[:, 0:32, :], in0=O[:, 0:32, :], in1=A64, op=ALU.mult), reads=[O_b, A_b], writes=[O_b])
            p.op("dve", lambda e: e.tensor_tensor(out=O[:, 32:64, :], in0=O[:, 32:64, :], in1=B64, op=ALU.mult), reads=[O_b, B_b], writes=[O_b])
        p.dma(PF, ohg_v[:, :, hs], O[:], O_b, reads=[O_b])
    p.finish("sp")
    p.emit()
    p.close()
    return nc


T = 1024
EPS = 1e-6


def build_B(last):
    nc = bass.Bass("TRN2", target_bir_lowering=False)
    dt = nc.dram_tensor
    xT = dt("xT", [32, 128, T], F32, kind="ExternalInput").ap()
    vecs = dt("vecs", [128, 9, 32], F32, kind="ExternalInput").ap()
    wga = dt("wga", [32, 128, 32 * 128], F32, kind="ExternalInput").ap()
    wgb = dt("wgb", [32, 128, 32 * 128], F32, kind="ExternalInput").ap()
    onT = dt("onT", [16, 128, T], F32, kind="ExternalInput").ap()
    ohT = dt("ohT", [16, 128, T], F32, kind="ExternalInput").ap()
    wua = dt("wua", [32, 128, 16 * 128], F32, kind="ExternalInput").ap()
    wub = dt("wub", [32, 128, 16 * 128], F32, kind="ExternalInput").ap()
    wo = dt("wo", [32, 128, 32 * 128], F32, kind="ExternalInput").ap()
    w1 = dt("w1", [128, 128, 32 * 128], F32, kind="ExternalInput").ap()
    w2 = dt("w2", [8, 128, 128 * 512], F32, kind="ExternalInput").ap()
    ones_d = dt("ones", [128, 128], F32, kind="ExternalInput").ap()
    outT = dt("outT", [32, 128, T], F32, kind="ExternalOutput").ap()
    yT_s = dt("yT_s", [32, 128, T], BF16, kind="Internal").ap()
    x1T_s = dt("x1T_s", [32, 128, T], F32, kind="Internal").ap()
    aT_s = dt("aT_s", [128, 128, T], BF16, kind="Internal").ap()
    x2T_s = dt("x2T_s", [32, 128, T], F32, kind="Internal").ap() if last else None

    p = Prog(nc)
    ones, ones_b = p.sbuf("ones", [128, 128], BF16)
    vc, vc_b = p.sbuf("vc", [128, 9, 32], F32)
    A1, A1_b = p.sbuf("A1", [128, 32], F32)
    A2, A2_b = p.sbuf("A2", [128, 32], F32)
    rstd, rstd_b = p.sbuf("rstd", [128, T], F32)
    big0, big0_b = p.sbuf("big0", [128, 32, T], BF16)
    big1, big1_b = p.sbuf("big1", [128, 32, T], BF16)
    NXB = 2
    xb = [p.sbuf("xb%d" % i, [128, T], F32) for i in range(NXB)]
    tf = [p.sbuf("tf%d" % i, [128, T], F32) for i in range(2)]
    tb = [p.sbuf("tb%d" % i, [128, T], BF16) for i in range(2)]
    sg = [(tf[i % 2][0][:, (i // 2) * 512:(i // 2 + 1) * 512], tf[i % 2][1]) for i in range(4)]
    wA = [p.sbuf("wA%d" % i, [128, 4096], BF16) for i in range(2)]
    wB = [p.sbuf("wB%d" % i, [128, 4096], BF16) for i in range(2)]
    wC = [p.sbuf("wC%d" % i, [128, 2048], BF16) for i in range(2)]
    wD = [p.sbuf("wD%d" % i, [128, 2048], BF16) for i in range(2)]
    ab = [(tf[i][0][:].bitcast(BF16), tf[i][1]) for i in range(2)]
    ps = [p.psum("ps%d" % i, [128, 512], F32) for i in range(8)]

    vbufs = {}
    PI, PF = "pool", "sp"

    p.dma(PI, ones[:], ones_d, ones_b, writes=[ones_b])
    p.dma(PF, vc[:], vecs, vc_b, writes=[vc_b])
    p.op("dve", lambda e: e.scalar_tensor_tensor(out=A1[:], in0=vc[:, 1, :], scalar=1.0, in1=vc[:, 6, :],
                                                 op0=ALU.add, op1=ALU.mult), reads=[vc_b], writes=[A1_b])
    p.op("dve", lambda e: e.scalar_tensor_tensor(out=A2[:], in0=vc[:, 4, :], scalar=1.0, in1=vc[:, 7, :],
                                                 op0=ALU.add, op1=ALU.mult), reads=[vc_b], writes=[A2_b])

    xcnt = [0]

    def load_x(src_cg_ap, src_buf=None):
        i = xcnt[0] % NXB
        xcnt[0] += 1
        t, b = xb[i]
        p.dma(PF, t[:], src_cg_ap, b, reads=[src_buf] if src_buf else [], writes=[b])
        return t, b

    def stats_pass(src, src_bufs, ssA, ssB):
        for cg in range(32):
            t, b = load_x(src[cg], src_bufs[cg] if src_bufs else None)
            sq, sqb = tb[cg % 2]
            p.op("act", lambda e, t=t, sq=sq: e.activation(out=sq[:], in_=t[:], func=AF.Square),
                 reads=[b], writes=[sqb])

            def mm(e, sq=sq, cg=cg):
                e.matmul(ssA[0][:], lhsT=ones[:], rhs=sq[:, 0:512], start=(cg == 0), stop=(cg == 31))
                return e.matmul(ssB[0][:], lhsT=ones[:], rhs=sq[:, 512:1024], start=(cg == 0), stop=(cg == 31))
            p.op("pe", mm, reads=[sqb, ones_b], writes=[ssA[1], ssB[1]])

    def make_rstd(ssA, ssB):
        for h, ss in enumerate((ssA, ssB)):
            sl = slice(h * 512, (h + 1) * 512)
            p.op("act", lambda e, ss=ss, sl=sl: e.activation(out=rstd[:, sl], in_=ss[0][:], func=AF.Sqrt,
                                                            scale=1.0 / 4096, bias=EPSB[:, 0:1]),
                 reads=[ss[1], epsb_b], writes=[rstd_b])
        p.op("dve", lambda e: e.reciprocal(out=rstd[:], in_=rstd[:]), reads=[rstd_b], writes=[rstd_b])

    EPSB, epsb_b = p.sbuf("epsb", [128, 1], F32)
    p.op("pool", lambda e: e.memset(EPSB[:], EPS), writes=[epsb_b])

    def norm_apply(src, src_bufs, A, Bv, dst, dst_b):
        for cg in range(32):
            t, b = load_x(src[cg], src_bufs[cg] if src_bufs else None)
            u, ub = tf[cg % 2]
            p.op("dve", lambda e, t=t, u=u: e.tensor_tensor(out=u[:], in0=t[:], in1=rstd[:], op=ALU.mult),
                 reads=[b, rstd_b], writes=[ub])
            p.op("act", lambda e, u=u, cg=cg: e.activation(out=dst[:, cg, :], in_=u[:], func=AF.Identity,
                                                          scale=A[:, cg:cg + 1], bias=vc[:, Bv, cg:cg + 1]),
                 reads=[ub, A1_b, A2_b, vc_b], writes=[dst_b])

    stats_pass(xT, None, ps[0], ps[1])
    make_rstd(ps[0], ps[1])
    norm_apply(xT, None, A1, 0, big0, big0_b)
    p.dma(PI, big1[:, 0:16, :], onT.rearrange("k p t -> p k t"), big1_b, writes=[big1_b])
    p.dma(PI, big1[:, 16:32, :], ohT.rearrange("k p t -> p k t"), big1_b, writes=[big1_b])
    yb = [p.buf("yT%d" % cg) for cg in range(32)]
    for cg in range(32):
        (w_a, w_ab), (w_b, w_bb), (w_c, w_cb), (w_d, w_db) = wA[cg % 2], wB[cg % 2], wC[cg % 2], wD[cg % 2]
        p.dma(PI, w_a[:], wga[cg], w_ab, writes=[w_ab])
        p.dma(PI, w_b[:], wgb[cg], w_bb, writes=[w_bb])
        p.dma(PI, w_c[:], wua[cg], w_cb, writes=[w_cb])
        p.dma(PI, w_d[:], wub[cg], w_db, writes=[w_db])
        yt, ytb = tb[cg % 2]
        for tt in range(2):
            sl = slice(tt * 512, (tt + 1) * 512)
            pset = ps[4 * tt:4 * tt + 4]

            def mm(e, w_a=w_a, w_b=w_b, w_c=w_c, w_d=w_d, sl=sl, pset=pset):
                for kt in range(32):
                    e.matmul(pset[0][0][:], lhsT=w_a[:, kt * 128:(kt + 1) * 128], rhs=big0[:, kt, sl],
                             start=(kt == 0), stop=(kt == 31))
                for kt in range(32):
                    e.matmul(pset[1][0][:], lhsT=w_b[:, kt * 128:(kt + 1) * 128], rhs=big0[:, kt, sl],
                             start=(kt == 0), stop=(kt == 31))
                for k in range(16):
                    e.matmul(pset[2][0][:], lhsT=w_c[:, k * 128:(k + 1) * 128], rhs=big1[:, k, sl],
                             start=(k == 0), stop=(k == 15))
                r = None
                for k in range(16):
                    r = e.matmul(pset[3][0][:], lhsT=w_d[:, k * 128:(k + 1) * 128],
                                 rhs=big1[:, 16 + k, sl], start=(k == 0), stop=(k == 15))
                return r
            p.op("pe", mm, reads=[w_ab, w_bb, w_cb, w_db, big0_b, big1_b], writes=[q[1] for q in pset])
            sga, sgab = sg[2 * tt]
            sgb, sgbb = sg[2 * tt + 1]
            p.op("act", lambda e, sga=sga, pset=pset: e.activation(out=sga[:], in_=pset[0][0][:], func=AF.Sigmoid),
                 reads=[pset[0][1]], writes=[sgab])
            p.op("act", lambda e, sgb=sgb, pset=pset: e.activation(out=sgb[:], in_=pset[1][0][:], func=AF.Sigmoid),
                 reads=[pset[1][1]], writes=[sgbb])
            p.op("dve", lambda e, sga=sga, pset=pset: e.tensor_tensor(out=sga[:], in0=sga[:], in1=pset[2][0][:], op=ALU.mult),
                 reads=[sgab, pset[2][1]], writes=[sgab])
            p.op("dve", lambda e, sgb=sgb, pset=pset: e.tensor_tensor(out=sgb[:], in0=sgb[:], in1=pset[3][0][:], op=ALU.mult),
                 reads=[sgbb, pset[3][1]], writes=[sgbb])
            p.op("dve", lambda e, sga=sga, sgb=sgb, yt=yt, sl=sl: e.tensor_tensor(out=yt[:, sl], in0=sga[:], in1=sgb[:], op=ALU.add),
                 reads=[sgab, sgbb], writes=[ytb])
        p.dma(PF, yT_s[cg], yt[:], ytb, reads=[ytb], writes=[yb[cg]])
    p.dma(PF, big0[:], yT_s.rearrange("k p t -> p k t"), big0_b, reads=yb, writes=[big0_b])
    x1b = [p.buf("x1T%d" % cg) for cg in range(32)]
    for cg in range(32):
        wt, wtb = (wA + wB)[cg % 4]
        p.dma(PI, wt[:, 0:4096], wo[cg], wtb, writes=[wtb])
        pset = ps[2 * (cg % 2):2 * (cg % 2) + 2]

        def mm(e, wt=wt, pset=pset):
            r = None
            for tt in range(2):
                for kt in range(32):
                    r = e.matmul(pset[tt][0][:], lhsT=wt[:, kt * 128:(kt + 1) * 128],
                                 rhs=big0[:, kt, tt * 512:(tt + 1) * 512], start=(kt == 0), stop=(kt == 31))
            return r
        p.op("pe", mm, reads=[wtb, big0_b], writes=[pset[0][1], pset[1][1]])
        t, b = load_x(xT[cg])
        for tt in range(2):
            sl = slice(tt * 512, (tt + 1) * 512)
            p.op("dve", lambda e, t=t, sl=sl, pset=pset, tt=tt, cg=cg: e.scalar_tensor_tensor(
                out=t[:, sl], in0=pset[tt][0][:], scalar=vc[:, 2, cg:cg + 1], in1=t[:, sl], op0=ALU.mult, op1=ALU.add),
                reads=[pset[tt][1], b, vc_b], writes=[b])
        p.dma(PF, x1T_s[cg], t[:], b, reads=[b], writes=[x1b[cg]])
        sq, sqb = tb[cg % 2]
        p.op("act", lambda e, t=t, sq=sq: e.activation(out=sq[:], in_=t[:], func=AF.Square), reads=[b], writes=[sqb])

        def mm2(e, sq=sq, cg=cg):
            e.matmul(ps[4][0][:], lhsT=ones[:], rhs=sq[:, 0:512], start=(cg == 0), stop=(cg == 31))
            return e.matmul(ps[5][0][:], lhsT=ones[:], rhs=sq[:, 512:1024], start=(cg == 0), stop=(cg == 31))
        p.op("pe", mm2, reads=[sqb, ones_b], writes=[ps[4][1], ps[5][1]])
    make_rstd(ps[4], ps[5])
    norm_apply(x1T_s, x1b, A2, 3, big1, big1_b)
    ab_b = [p.buf("aT%d" % f) for f in range(128)]
    for fg in range(128):
        wt, wtb = (wA + wB)[fg % 4]
        p.dma(PI, wt[:, 0:4096], w1[fg], wtb, writes=[wtb])
        pset = ps[2 * (fg % 2):2 * (fg % 2) + 2]

        def mm(e, wt=wt, pset=pset):
            r = None
            for tt in range(2):
                for kt in range(32):
                    r = e.matmul(pset[tt][0][:], lhsT=wt[:, kt * 128:(kt + 1) * 128],
                                 rhs=big1[:, kt, tt * 512:(tt + 1) * 512], start=(kt == 0), stop=(kt == 31))
            return r
        p.op("pe", mm, reads=[wtb, big1_b], writes=[pset[0][1], pset[1][1]])
        r_, rb = tf[fg % 2]
        a_, a_b = tb[fg % 2]
        for tt in range(2):
            sl = slice(tt * 512, (tt + 1) * 512)
            p.op("act", lambda e, r_=r_, sl=sl, pset=pset, tt=tt: e.activation(out=r_[:, sl], in_=pset[tt][0][:], func=AF.Relu),
                 reads=[pset[tt][1]], writes=[rb])
            p.op("dve", lambda e, r_=r_, a_=a_, sl=sl, pset=pset, tt=tt: e.tensor_tensor(
                out=a_[:, sl], in0=r_[:, sl], in1=pset[tt][0][:], op=ALU.mult),
                reads=[rb, pset[tt][1]], writes=[a_b])
        p.dma(PF, aT_s[:, fg, :], a_[:], a_b, reads=[a_b], writes=[ab_b[fg]])
    dst = x2T_s if last else outT
    x2b = [p.buf("x2T%d" % cg) for cg in range(32)]
    FB = 8
    for db in range(8):
        for fc in range(128 // FB):
            wt, wtb = (wA + wB)[fc % 4]
            p.dma(PI, wt[:, 0:FB * 512], w2[db][:, fc * FB * 512:(fc + 1) * FB * 512], wtb, writes=[wtb])
            for f4 in range(FB // 2):
                f0 = fc * FB + f4 * 2
                at, atb = ab[(f0 // 2) % 2]
                p.dma(PF, at[:], aT_s[:, f0:f0 + 2, :].rearrange("p f t -> p (f t)"), atb, reads=ab_b[f0:f0 + 2], writes=[atb])

                def mm(e, wt=wt, at=at, f4=f4, f0=f0):
                    r = None
                    for fl in range(2):
                        fg = f0 + fl
                        wofs = (f4 * 2 + fl) * 512
                        for c4 in range(4):
                            for tt in range(2):
                                r = e.matmul(ps[c4 * 2 + tt][0][:], lhsT=wt[:, wofs + c4 * 128:wofs + (c4 + 1) * 128],
                                             rhs=at[:, fl * T + tt * 512:fl * T + (tt + 1) * 512], start=(fg == 0), stop=(fg == 127))
                    return r
                p.op("pe", mm, reads=[wtb, atb], writes=[q[1] for q in ps])
        for c4 in range(4):
            cg = db * 4 + c4
            t, b = load_x(x1T_s[cg], x1b[cg])
            for tt in range(2):
                sl = slice(tt * 512, (tt + 1) * 512)
                p.op("dve", lambda e, t=t, sl=sl, tt=tt, cg=cg, c4=c4: e.scalar_tensor_tensor(
                    out=t[:, sl], in0=ps[c4 * 2 + tt][0][:], scalar=vc[:, 5, cg:cg + 1], in1=t[:, sl],
                    op0=ALU.mult, op1=ALU.add), reads=[ps[c4 * 2 + tt][1], b, vc_b], writes=[b])
            p.dma(PF, dst[cg], t[:], b, reads=[b], writes=[x2b[cg]])
    if last:
        stats_pass(x2T_s, x2b, ps[0], ps[1])
        make_rstd(ps[0], ps[1])
        for cg in range(32):
            t, b = load_x(x2T_s[cg], x2b[cg])
            p.op("dve", lambda e, t=t: e.tensor_tensor(out=t[:], in0=t[:], in1=rstd[:], op=ALU.mult),
                 reads=[b, rstd_b], writes=[b])
            u, ub = tf[cg % 2]
            p.op("act", lambda e, t=t, u=u, cg=cg: e.activation(out=u[:], in_=t[:], func=AF.Identity, scale=vc[:, 8, cg:cg + 1]),
                 reads=[b, vc_b], writes=[ub])
            p.dma(PF, outT[cg], u[:], ub, reads=[ub])
    p.finish("sp")
    p.emit()
    p.close()
    return nc


def _lay_kc(w, nk):
    K, N = w.shape
    return np.ascontiguousarray(w.reshape(nk, 128, N // 128, 128).transpose(2, 1, 0, 3)).reshape(N // 128, 128, nk * 128)


def _run(nc, in_maps):
    res = run_bass_kernel_spmd(nc, in_maps, core_ids=list(range(8)))
    return res.results


def kernel(x, c, positions, ada_w, ada_b, norm_mix_w, w_in, nsa_cmp_pos, nsa_cmp_w1, nsa_cmp_w2,
           hgrn_lb_logits, hgrn_norm_w, w_up_a, w_up_b, w_out, norm_mlp_w, w_mlp1, w_mlp2, final_norm_w):
    A = lambda a: np.asarray(a)
    x, c, positions, ada_w, ada_b, norm_mix_w, w_in = A(x), A(c), A(positions), A(ada_w), A(ada_b), A(norm_mix_w), A(w_in)
    nsa_cmp_pos, nsa_cmp_w1, nsa_cmp_w2 = A(nsa_cmp_pos), A(nsa_cmp_w1), A(nsa_cmp_w2)
    hgrn_lb_logits, hgrn_norm_w, w_up_a, w_up_b, w_out = A(hgrn_lb_logits), A(hgrn_norm_w), A(w_up_a), A(w_up_b), A(w_out)
    norm_mlp_w, w_mlp1, w_mlp2, final_norm_w = A(norm_mlp_w), A(w_mlp1), A(w_mlp2), A(final_norm_w)
    S = 4096
    ones = np.ones((128, 128), f32)
    ncM = build_M()
    maps = []
    for i in range(8):
        l, j = i // 4, i % 4
        cols = slice(j * NCOL, (j + 1) * NCOL)
        maps.append(dict(c=c, adaw=np.ascontiguousarray(ada_w[l][:, cols]).reshape(128, 32, NCOL),
                         adab=np.ascontiguousarray(np.stack([ada_b[l][cols]] * 2))))
    r = _run(ncM, maps)
    mod = np.zeros((2, 2, 6 * 4096), f32)
    for i in range(8):
        l, j = i // 4, i % 4
        mod[l][:, j * NCOL:(j + 1) * NCOL] = r[i]["mod"]
    del maps
    consts = nsa_consts()
    rmask = np.ascontiguousarray(np.tile((np.arange(S) % 64 != 0).astype(f32)[None], (128, 1)))
    tri = np.triu(np.ones((64, 64), f32))
    ncA1 = build_A1()
    ncA2 = build_A2()
    O = dict(q=0, kc=2048, vc=2560, ks=3072, vs=3584, kw=4096, vw=4608, ng=5120, hq=5168, hf=7216, hi=9264, hg=11312, ga=13360, gb=17456)
    xcur = x
    for l in range(2):
        Wl = w_in[l]
        m6 = mod[l].reshape(2, 6, 4096)
        wF, wT = [], []
        for g in range(4):
            fc = np.concatenate([np.arange(O["q"] + g * 512, O["q"] + (g + 1) * 512)] +
                                [np.arange(O[n] + g * 128, O[n] + (g + 1) * 128) for n in ("kc", "vc", "ks", "kw")] +
                                [np.arange(O[n] + g * 512, O[n] + (g + 1) * 512) for n in ("hq", "hf")])
            tc = np.concatenate([np.arange(O[n] + g * 128, O[n] + (g + 1) * 128) for n in ("vs", "vw")] +
                                [np.arange(O[n] + g * 512, O[n] + (g + 1) * 512) for n in ("hi", "hg")] +
                                [np.array([O["ng"] + br * 16 + g * 4 + j for br in range(3) for j in range(4)])])
            wF.append(_lay_kc(np.ascontiguousarray(Wl[:, fc]), 32))
            wT.append(np.ascontiguousarray(np.ascontiguousarray(Wl[:, tc]).reshape(32, 128, NT).transpose(1, 0, 2)))
        xTb = [np.ascontiguousarray(xcur[b].T).reshape(32, 128, S) for b in range(2)]
        maps = []
        for i in range(8):
            b, g = i // 4, i % 4
            v3 = np.stack([m6[b, 0], m6[b, 1], norm_mix_w[l]])
            maps.append(dict(xT=xTb[b], vecs=np.ascontiguousarray(v3.reshape(3, 32, 128).transpose(2, 0, 1)),
                             wF=wF[g], wT=wT[g], ones=ones))
        rA1 = _run(ncA1, maps)
        del maps, wF, wT, xTb
        maps = []
        for i in range(8):
            b, g = i // 4, i % 4
            oF, oT = rA1[i]["outF"], rA1[i]["outT"]
            d = dict(qT=np.ascontiguousarray(oF[0:4]), kT=np.ascontiguousarray(oF[4:8]),
                     vs=np.ascontiguousarray(oT[:, 0:128]), vw=np.ascontiguousarray(oT[:, 128:256]),
                     ng=np.ascontiguousarray(oT[:, 1280:1292]),
                     pos=np.ascontiguousarray(np.tile(positions[b][None].astype(np.int32), (32, 1))),
                     cposT=np.ascontiguousarray(nsa_cmp_pos[l].transpose(0, 2, 1)),
                     cw1=np.ascontiguousarray(nsa_cmp_w1[l].transpose(0, 2, 1, 3)).reshape(2, 128, 32 * 128),
                     cw2=np.ascontiguousarray(nsa_cmp_w2[l]))
            d.update(consts)
            maps.append(d)
        rA2 = _run(ncA2, maps)
        del maps
        ncA3 = build_A3(l)
        maps = []
        for i in range(8):
            b, g = i // 4, i % 4
            oF, oT = rA1[i]["outF"], rA1[i]["outT"]
            lbl = np.ascontiguousarray(hgrn_lb_logits.reshape(2, 16, 128)[:, 4 * g:4 * g + 4, :].transpose(2, 0, 1))
            maps.append(dict(hqT=np.ascontiguousarray(oF[8:12]), hfT=np.ascontiguousarray(oF[12:16]),
                             hi=np.ascontiguousarray(oT[:, 256:768]), hg=np.ascontiguousarray(oT[:, 768:1280]),
                             lbl=lbl, nw64=np.ascontiguousarray(np.tile(hgrn_norm_w[l][None], (64, 1))),
                             rmask=rmask, tri=tri, ident=consts["ident"]))
        rA3 = _run(ncA3, maps)
        del maps, rA1
        on = [np.concatenate([rA2[b * 4 + g]["onsa"] for g in range(4)], axis=1) for b in range(2)]
        oh = [np.concatenate([rA3[b * 4 + g]["ohg"] for g in range(4)], axis=1) for b in range(2)]
        del rA2, rA3
        wga = _lay_kc(np.ascontiguousarray(Wl[:, O["ga"]:O["ga"] + 4096]), 32)
        wgb = _lay_kc(np.ascontiguousarray(Wl[:, O["gb"]:O["gb"] + 4096]), 32)
        wua = _lay_kc(w_up_a[l], 16)
        wub = _lay_kc(w_up_b[l], 16)
        wo = _lay_kc(w_out[l], 32)
        w1 = _lay_kc(w_mlp1[l], 32)
        w2 = np.ascontiguousarray(w_mlp2[l].reshape(128, 128, 8, 512).transpose(2, 1, 0, 3)).reshape(8, 128, 128 * 512)
        last = (l == 1)
        ncB = build_B(last)
        maps = []
        for i in range(8):
            b, s0 = i // 4, (i % 4) * 1024
            v9 = np.concatenate([m6[b], np.stack([norm_mix_w[l], norm_mlp_w[l], final_norm_w])], 0)
            maps.append(dict(xT=np.ascontiguousarray(xcur[b, s0:s0 + 1024].T).reshape(32, 128, 1024),
                             vecs=np.ascontiguousarray(v9.reshape(9, 32, 128).transpose(2, 0, 1)),
                             wga=wga, wgb=wgb,
                             onT=np.ascontiguousarray(on[b][s0:s0 + 1024].T).reshape(16, 128, 1024),
                             ohT=np.ascontiguousarray(oh[b][s0:s0 + 1024].T).reshape(16, 128, 1024),
                             wua=wua, wub=wub, wo=wo, w1=w1, w2=w2, ones=ones))
        rB = _run(ncB, maps)
        del maps, wga, wgb, wua, wub, wo, w1, w2
        xn = np.empty((2, S, 4096), f32)
        for i in range(8):
            b, s0 = i // 4, (i % 4) * 1024
            xn[b, s0:s0 + 1024] = rB[i]["outT"].reshape(4096, 1024).T
        del rB
        xcur = xn
    return xcur
```

```python
import contextlib
import numpy as np
import concourse.bass as bass
import concourse.mybir as mybir

F32 = mybir.dt.float32
BF16 = mybir.dt.bfloat16
I32 = mybir.dt.int32
U8 = mybir.dt.uint8
ALU = mybir.AluOpType
AF = mybir.ActivationFunctionType
AX = mybir.AxisListType


class Buf:
    __slots__ = ("name", "w", "r", "dsem")

    def __init__(self, name):
        self.name = name
        self.w = None
        self.r = []
        self.dsem = None


class Prog:
    ENG = ("sp", "act", "dve", "pool", "pe")

    def __init__(self, nc):
        self.nc = nc
        self.stack = contextlib.ExitStack()
        self.streams = {k: [] for k in self.ENG}
        self.sems = {}
        self.cnt = {}
        self.known = {k: {} for k in self.ENG}
        self.nbuf = 0
        for k in ("act", "dve", "pool", "pe"):
            self._newsem(k)

    def _newsem(self, key):
        s = self.stack.enter_context(self.nc.semaphore("s_" + key))
        self.sems[key] = s
        self.cnt[key] = 0
        return s

    def sbuf(self, name, shape, dtype):
        t = self.stack.enter_context(self.nc.sbuf_tensor("sb_" + name, list(shape), dtype))
        return t, Buf(name)

    def psum(self, name, shape, dtype=F32):
        t = self.stack.enter_context(self.nc.psum_tensor("ps_" + name, list(shape), dtype))
        return t, Buf(name)

    def buf(self, name):
        return Buf(name)

    def _need(self, eng, tok, deps):
        if tok is None:
            return
        k, v = tok
        if deps.get(k, 0) < v:
            deps[k] = v

    def _emit_waits(self, eng, deps):
        kn = self.known[eng]
        for k, v in deps.items():
            if k.startswith("d_"):
                v = max(v, self.cnt[k])
            if kn.get(k, 0) < v:
                kn[k] = v
                sem = self.sems[k]
                self.streams[eng].append(("wait", sem, v))

    def _deps(self, eng, reads, writes):
        deps = {}
        for b in reads:
            self._need(eng, b.w, deps)
        for b in writes:
            self._need(eng, b.w, deps)
            for t in b.r:
                self._need(eng, t, deps)
        self._emit_waits(eng, deps)

    def _commit(self, tok, reads, writes):
        for b in reads:
            b.r.append(tok)
            if len(b.r) > 64:
                m = {}
                for k, v in b.r:
                    if m.get(k, 0) < v:
                        m[k] = v
                b.r = list(m.items())
        for b in writes:
            b.w = tok
            b.r = []

    def op(self, eng, fn, reads=(), writes=()):
        self._deps(eng, reads, writes)
        self.cnt[eng] += 1
        tok = (eng, self.cnt[eng])
        self.streams[eng].append(("op", fn, self.sems[eng], 1))
        self._commit(tok, reads, writes)
        return tok

    def dma(self, eng, out, in_, sb, reads=(), writes=(), **kw):
        if sb.dsem is None:
            self.nbuf += 1
            sb.dsem = "d_%d_%s" % (self.nbuf, sb.name)
            self._newsem(sb.dsem)
        self._deps(eng, reads, writes)
        k = sb.dsem
        self.cnt[k] += 16
        tok = (k, self.cnt[k])
        self.streams[eng].append(("op", (lambda e: e.dma_start(out=out, in_=in_, **kw)), self.sems[k], 16))
        self._commit(tok, reads, writes)
        return tok

    def finish(self, eng="sp"):
        deps = {k: v for k, v in self.cnt.items() if v > 0}
        self._emit_waits(eng, deps)

    def emit(self):
        nc = self.nc
        with nc.Block() as block:
            decos = {"sp": block.sync, "act": block.scalar, "dve": block.vector,
                     "pool": block.gpsimd, "pe": block.tensor}
            for key in self.ENG:
                stream = self.streams[key]
                if not stream:
                    continue

                def body(e, stream=stream):
                    for it in stream:
                        if it[0] == "wait":
                            e.wait_ge(it[1], it[2])
                        else:
                            ins = it[1](e)
                            ins.then_inc(it[2], it[3])

                decos[key](body)

    def close(self):
        self.stack.close()


def simulate(prog):
    val = {id(s): 0 for s in prog.sems.values()}
    pos = {k: 0 for k in prog.ENG}
    progress = True
    while progress:
        progress = False
        for k in prog.ENG:
            st = prog.streams[k]
            while pos[k] < len(st):
                it = st[pos[k]]
                if it[0] == "wait":
                    if val[id(it[1])] >= it[2]:
                        pos[k] += 1
                        progress = True
                    else:
                        break
                else:
                    val[id(it[2])] += it[3]
                    pos[k] += 1
                    progress = True
    stuck = {k: (pos[k], len(prog.streams[k])) for k in prog.ENG if pos[k] < len(prog.streams[k])}
    return stuck


from concourse.bass_utils import run_bass_kernel_spmd
import math


f32 = np.float32
NEGV = -30000.0
def nsa_consts():
    S = 4096
    t = np.arange(S)
    c = {}
    c["ident"] = np.eye(128, dtype=f32)
    inv = (500000.0 ** (-np.arange(0, 32, 2, dtype=np.float32) / 32)).astype(f32)
    c["inv2"] = np.concatenate([inv, inv]).reshape(32, 1).astype(f32)
    pm = np.zeros((32, 32), f32)
    for d in range(16):
        pm[d + 16, d] = -1.0
        pm[d, d + 16] = 1.0
    c["pm"] = pm
    n = np.arange(256)
    end = n * 16 + 31
    vis = (end[:, None] <= t[None, :]) & (n[:, None] < 255)
    cm = np.where(vis, 0.0, NEGV).astype(f32).reshape(2, 128, S).transpose(1, 0, 2).reshape(128, 2 * S)
    c["cmpmask"] = np.ascontiguousarray(cm)
    cs = np.arange(255) * 16; ce = cs + 31
    ss = np.arange(64) * 64; se = ss + 63
    ov = np.maximum(np.minimum(ce[:, None], se[None, :]) - np.maximum(cs[:, None], ss[None, :]) + 1, 0).astype(f32)
    ov = np.concatenate([ov, np.zeros((1, 64), f32)], 0)
    c["ovt"] = np.ascontiguousarray(ov.reshape(2, 128, 64).transpose(1, 0, 2))
    blk = np.arange(64)
    cur = t // 64
    forced = (blk[None, :] == 0) | (blk[None, :] == cur[:, None]) | (blk[None, :] == cur[:, None] - 1)
    valid = blk[None, :] * 64 <= t[:, None]
    st = np.where(valid, np.where(forced, 1e6, 0.0), -1e9).astype(f32)
    c["seltab"] = np.ascontiguousarray(st.reshape(32, 128, 64).transpose(1, 0, 2)).reshape(128, 32 * 64)
    c["etab"] = (np.arange(S)[None, :] // 64 == blk[:, None]).astype(f32)
    p = np.arange(128); u = np.arange(512)
    dg = np.stack([np.where(128 * a + p[:, None] <= u[None, :], 0.0, NEGV) for a in range(4)], 1).astype(f32)
    c["diag"] = np.ascontiguousarray(dg).reshape(128, 4 * 512)
    bd = []
    for cc in range(8):
        key = 128 * (cc - 4) + p[:, None]
        ok = (key <= u[None, :]) & (key > u[None, :] - 512)
        bd.append(np.where(ok, 0.0, NEGV))
    c["band"] = np.ascontiguousarray(np.stack(bd, 1).astype(f32)).reshape(128, 8 * 512)
    return c


NCOL = 6144


def build_M():
    nc = bass.Bass("TRN2", target_bir_lowering=False)
    dt = nc.dram_tensor
    c = dt("c", [2, 4096], F32, kind="ExternalInput").ap()
    adaw = dt("adaw", [128, 32, NCOL], F32, kind="ExternalInput").ap()
    adab = dt("adab", [2, NCOL], F32, kind="ExternalInput").ap()
    mod = dt("mod", [2, NCOL], F32, kind="ExternalOutput").ap()
    p = Prog(nc)
    cT, cT_b = p.sbuf("cT", [128, 2, 32], F32)
    bias, bias_b = p.sbuf("bias", [2, NCOL], F32)
    res, res_b = p.sbuf("res", [2, NCOL], F32)
    wt = [p.sbuf("wt%d" % i, [128, 32, 512], F32) for i in range(2)]
    ps = [p.psum("ps%d" % i, [2, 512], F32) for i in range(2)]
    p.dma("sp", cT[:], c.rearrange("b (p kt) -> p b kt", kt=32), cT_b, writes=[cT_b])
    p.dma("sp", bias[:], adab, bias_b, writes=[bias_b])
    p.op("act", lambda e: e.activation(out=cT[:], in_=cT[:], func=AF.Silu), reads=[cT_b], writes=[cT_b])
    for ct in range(NCOL // 512):
        w, wb = wt[ct % 2]
        sl = slice(ct * 512, (ct + 1) * 512)
        p.dma("sp" if ct % 2 == 0 else "act", w[:], adaw[:, :, sl], wb, writes=[wb])
        pt, ptb = ps[ct % 2]

        def mm(e, w=w, pt=pt):
            r = None
            for kt in range(32):
                r = e.matmul(pt[:], lhsT=cT[:, :, kt], rhs=w[:, kt, :], start=(kt == 0), stop=(kt == 31))
            return r
        p.op("pe", mm, reads=[wb, cT_b], writes=[ptb])
        p.op("dve", lambda e, pt=pt, sl=sl: e.tensor_tensor(out=res[:, sl], in0=pt[:], in1=bias[:, sl], op=ALU.add),
             reads=[ptb, bias_b], writes=[res_b])
    p.dma("sp", mod, res[:], res_b, reads=[res_b])
    p.finish("sp")
    p.emit()
    p.close()
    return nc


EPS = 1e-6
NT = 1292
NTS = [(0, 512), (512, 512), (1024, 268)]


def build_A1():
    nc = bass.Bass("TRN2", target_bir_lowering=False)
    dt = nc.dram_tensor
    xT = dt("xT", [32, 128, 4096], F32, kind="ExternalInput").ap()
    vecs = dt("vecs", [128, 3, 32], F32, kind="ExternalInput").ap()
    wF = dt("wF", [16, 128, 4096], F32, kind="ExternalInput").ap()
    wT = dt("wT", [128, 32, NT], F32, kind="ExternalInput").ap()
    ones_d = dt("ones", [128, 128], F32, kind="ExternalInput").ap()
    outF = dt("outF", [16, 128, 4096], F32, kind="ExternalOutput").ap()
    outT = dt("outT", [4096, NT], F32, kind="ExternalOutput").ap()
    p = Prog(nc)
    ones, ones_b = p.sbuf("ones", [128, 128], BF16)
    vc, vc_b = p.sbuf("vc", [128, 3, 32], F32)
    A1, A1_b = p.sbuf("A1", [128, 32], F32)
    EPSB, epsb_b = p.sbuf("epsb", [128, 1], F32)
    rstd, rstd_b = p.sbuf("rstd", [128, 1024], F32)
    big0, big0_b = p.sbuf("big0", [128, 32, 1024], BF16)
    wTs, wTs_b = p.sbuf("wTs", [128, 32, NT], BF16)
    wf = [p.sbuf("wf%d" % i, [128, 4096], BF16) for i in range(2)]
    xb = [p.sbuf("xb%d" % i, [128, 1024], F32) for i in range(3)]
    tf = [p.sbuf("tf%d" % i, [128, 1024], F32) for i in range(2)]
    tb = [p.sbuf("tb%d" % i, [128, 1024], BF16) for i in range(2)]
    ot = [p.sbuf("ot%d" % i, [128, NT], F32) for i in range(2)]
    ps = [p.psum("ps%d" % i, [128, 512], F32) for i in range(8)]
    PI, PF = "pool", "sp"
    p.dma(PI, ones[:], ones_d, ones_b, writes=[ones_b])
    p.dma(PF, vc[:], vecs, vc_b, writes=[vc_b])
    p.dma(PI, wTs[:], wT, wTs_b, writes=[wTs_b])
    p.op("pool", lambda e: e.memset(EPSB[:], EPS), writes=[epsb_b])
    p.op("dve", lambda e: e.scalar_tensor_tensor(out=A1[:], in0=vc[:, 1, :], scalar=1.0, in1=vc[:, 2, :],
                                                 op0=ALU.add, op1=ALU.mult), reads=[vc_b], writes=[A1_b])
    xcnt = [0]

    def load_x(ap):
        i = xcnt[0] % 3
        xcnt[0] += 1
        t, b = xb[i]
        p.dma(PF, t[:], ap, b, writes=[b])
        return t, b

    fcnt = 0
    for ch in range(4):
        csl = slice(ch * 1024, (ch + 1) * 1024)
        for cg in range(32):
            t, b = load_x(xT[cg][:, csl])
            sq, sqb = tb[cg % 2]
            p.op("act", lambda e, t=t, sq=sq: e.activation(out=sq[:], in_=t[:], func=AF.Square), reads=[b], writes=[sqb])

            def mm(e, sq=sq, cg=cg):
                e.matmul(ps[6][0][:], lhsT=ones[:], rhs=sq[:, 0:512], start=(cg == 0), stop=(cg == 31))
                return e.matmul(ps[7][0][:], lhsT=ones[:], rhs=sq[:, 512:1024], start=(cg == 0), stop=(cg == 31))
            p.op("pe", mm, reads=[sqb, ones_b], writes=[ps[6][1], ps[7][1]])
        for h in range(2):
            sl = slice(h * 512, (h + 1) * 512)
            p.op("act", lambda e, h=h, sl=sl: e.activation(out=rstd[:, sl], in_=ps[6 + h][0][:], func=AF.Sqrt,
                                                          scale=1.0 / 4096, bias=EPSB[:, 0:1]),
                 reads=[ps[6 + h][1], epsb_b], writes=[rstd_b])
        p.op("dve", lambda e: e.reciprocal(out=rstd[:], in_=rstd[:]), reads=[rstd_b], writes=[rstd_b])
        for cg in range(32):
            t, b = load_x(xT[cg][:, csl])
            u, ub = tf[cg % 2]
            p.op("dve", lambda e, t=t, u=u: e.tensor_tensor(out=u[:], in0=t[:], in1=rstd[:], op=ALU.mult),
                 reads=[b, rstd_b], writes=[ub])
            p.op("act", lambda e, u=u, cg=cg: e.activation(out=big0[:, cg, :], in_=u[:], func=AF.Identity,
                                                          scale=A1[:, cg:cg + 1], bias=vc[:, 0, cg:cg + 1]),
                 reads=[ub, A1_b, vc_b], writes=[big0_b])
        for cf in range(16):
            w, wb = wf[fcnt % 2]
            pset = ps[2 * (fcnt % 2):2 * (fcnt % 2) + 2]
            o, ob = tf[fcnt % 2]
            fcnt += 1
            p.dma(PI, w[:], wF[cf], wb, writes=[wb])

            def mm(e, w=w, pset=pset):
                r = None
                for tt in range(2):
                    for kt in range(32):
                        r = e.matmul(pset[tt][0][:], lhsT=w[:, kt * 128:(kt + 1) * 128],
                                     rhs=big0[:, kt, tt * 512:(tt + 1) * 512], start=(kt == 0), stop=(kt == 31))
                return r
            p.op("pe", mm, reads=[wb, big0_b], writes=[pset[0][1], pset[1][1]])
            p.op("act", lambda e, o=o, pset=pset: e.activation(out=o[:, 0:512], in_=pset[0][0][:], func=AF.Copy),
                 reads=[pset[0][1]], writes=[ob])
            p.op("dve", lambda e, o=o, pset=pset: e.tensor_copy(out=o[:, 512:1024], in_=pset[1][0][:]),
                 reads=[pset[1][1], ob], writes=[ob])
            p.dma(PF, outF[cf][:, csl], o[:], ob, reads=[ob])
        for t8 in range(8):
            o, ob = ot[t8 % 2]
            for ni, (n0, nn) in enumerate(NTS):
                pt, ptb = ps[4 + (t8 * 3 + ni) % 2]

                def mm(e, pt=pt, t8=t8, n0=n0, nn=nn):
                    r = None
                    for kt in range(32):
                        r = e.matmul(pt[:, 0:nn], lhsT=big0[:, kt, t8 * 128:(t8 + 1) * 128], rhs=wTs[:, kt, n0:n0 + nn],
                                     start=(kt == 0), stop=(kt == 31))
                    return r
                p.op("pe", mm, reads=[big0_b, wTs_b], writes=[ptb])
                if ni % 2 == 0:
                    p.op("act", lambda e, o=o, pt=pt, n0=n0, nn=nn: e.activation(out=o[:, n0:n0 + nn], in_=pt[:, 0:nn], func=AF.Copy),
                         reads=[ptb, ob], writes=[ob])
                else:
                    p.op("dve", lambda e, o=o, pt=pt, n0=n0, nn=nn: e.tensor_copy(out=o[:, n0:n0 + nn], in_=pt[:, 0:nn]),
                         reads=[ptb, ob], writes=[ob])
            r0 = ch * 1024 + t8 * 128
            p.dma(PF, outT[r0:r0 + 128, :], o[:], ob, reads=[ob])
    p.finish("sp")
    p.emit()
    p.close()
    return nc


import math

SCALE = 128 ** -0.5
NEG = -30000.0
TWO_PI = 2 * math.pi


def build_A2():
    nc = bass.Bass("TRN2", target_bir_lowering=False)
    dt = nc.dram_tensor
    I = "ExternalInput"
    qT = dt("qT", [4, 128, 4096], F32, kind=I).ap()
    kT = dt("kT", [4, 128, 4096], F32, kind=I).ap()
    vs = dt("vs", [4096, 128], F32, kind=I).ap()
    vw = dt("vw", [4096, 128], F32, kind=I).ap()
    ng = dt("ng", [4096, 12], F32, kind=I).ap()
    pos = dt("pos", [32, 4096], I32, kind=I).ap()
    inv2 = dt("inv2", [32, 1], F32, kind=I).ap()
    pm = dt("pm", [32, 32], F32, kind=I).ap()
    cposT = dt("cposT", [2, 128, 32], F32, kind=I).ap()
    cw1 = dt("cw1", [2, 128, 32 * 128], F32, kind=I).ap()
    cw2 = dt("cw2", [2, 128, 128], F32, kind=I).ap()
    ident = dt("ident", [128, 128], F32, kind=I).ap()
    cmpmask = dt("cmpmask", [128, 2 * 4096], F32, kind=I).ap()
    ovt = dt("ovt", [128, 2, 64], F32, kind=I).ap()
    seltab = dt("seltab", [128, 32 * 64], F32, kind=I).ap()
    etab = dt("etab", [64, 4096], F32, kind=I).ap()
    diag = dt("diag", [128, 4 * 512], F32, kind=I).ap()
    band = dt("band", [128, 8 * 512], F32, kind=I).ap()
    onsa = dt("onsa", [4096, 512], F32, kind="ExternalOutput").ap()
    p = Prog(nc)
    PI, PF = "pool", "sp"
    QT, QT_b = p.sbuf("QT", [128, 4, 4096], BF16)
    KK = [p.sbuf("KK%d" % i, [128, 4096], BF16) for i in range(4)]
    (KC, KC_b), (VC, VC_b), (KS, KS_b), (KW, KW_b) = KK
    VS1, VS1_b = p.sbuf("VS1", [128, 32, 129], BF16)
    VW1, VW1_b = p.sbuf("VW1", [128, 32, 129], BF16)
    NG, NG_b = p.sbuf("NG", [128, 32, 12], F32)
    CM, CM_b = p.sbuf("CM", [128, 2 * 4096], BF16)
    ET, ET_b = p.sbuf("ET", [64, 4096], BF16)
    DG, DG_b = p.sbuf("DG", [128, 4 * 512], BF16)
    BD, BD_b = p.sbuf("BD", [128, 8 * 512], BF16)
    SELT, SELT_b = p.sbuf("SELT", [128, 32 * 64], F32)
    IDB, IDB_b = p.sbuf("IDB", [128, 128], BF16)
    PMb, PMb_b = p.sbuf("PMb", [32, 32], BF16)
    W1 = [p.sbuf("W1%d" % i, [128, 32 * 128], BF16) for i in range(2)]
    W2 = [p.sbuf("W2%d" % i, [128, 128], BF16) for i in range(2)]
    POSC = [p.sbuf("POSC%d" % i, [128, 32], BF16) for i in range(2)]
    KCT, KCT_b = p.sbuf("KCT", [128, 256], BF16)
    VC1, VC1_b = p.sbuf("VC1", [128, 2, 193], BF16)
    HID, HID_b = p.sbuf("HID", [128, 256], BF16)
    gt = [p.sbuf("gt%d" % i, [128, 256], F32) for i in range(3)]
    cb, cb_b = p.sbuf("cb", [128, 1], F32)
    POSI, POSI_b = p.sbuf("POSI", [32, 1024], I32)
    ANG, ANG_b = p.sbuf("ANG", [32, 1024], F32)
    COS, COS_b = p.sbuf("COS", [32, 4096], F32)
    SIN, SIN_b = p.sbuf("SIN", [32, 4096], F32)
    KI, KI_b = (POSI, POSI_b)
    iv, iv_b = p.sbuf("iv", [32, 1], F32)
    PT = [p.sbuf("PT%d" % i, [128, 512], BF16) for i in range(3)]
    OACC, OACC_b = p.sbuf("OACC", [128, 4, 512], F32)
    PSLC, PSLC_b = p.sbuf("PSLC", [128, 4, 64], F32)
    NMT, NMT_b = p.sbuf("NMT", [64, 512], BF16)
    sc1, sc1_b = p.sbuf("sc1", [128, 64], F32)
    sc2, sc2_b = p.sbuf("sc2", [128, 64], F32)
    nmk, nmk_b = p.sbuf("nmk", [128, 64], BF16)
    m8, m8_b = p.sbuf("m8", [128, 8], F32)
    rr = [p.sbuf("rr%d" % i, [128, 2], F32) for i in range(4)]
    rt = [p.sbuf("rt%d" % i, [32, 512], F32) for i in range(2)]
    psS = [p.psum("S%d" % i, [128, 512], F32) for i in range(2)]
    pv = [p.psum("pv%d" % i, [128, 512], F32) for i in range(4)]
    pm0, pm0_b = p.psum("m0", [128, 512], F32)
    pm1, pm1_b = p.psum("m1", [128, 1024], BF16)

    p.dma(PI, QT[:], qT.rearrange("j d t -> d j t"), QT_b, writes=[QT_b])
    for i in range(4):
        p.dma(PI, KK[i][0][:], kT[i], KK[i][1], writes=[KK[i][1]])
    p.dma(PI, VS1[:, :, 0:128], vs.rearrange("(kt p) d -> p kt d", p=128), VS1_b, writes=[VS1_b])
    p.dma(PI, VW1[:, :, 0:128], vw.rearrange("(kt p) d -> p kt d", p=128), VW1_b, writes=[VW1_b])
    p.op("pool", lambda e: e.memset(VS1[:, :, 128:129], 1.0), writes=[VS1_b])
    p.op("pool", lambda e: e.memset(VW1[:, :, 128:129], 1.0), writes=[VW1_b])
    p.dma(PF, NG[:], ng.rearrange("(tt p) c -> p tt c", p=128), NG_b, writes=[NG_b])
    p.dma(PI, CM[:], cmpmask, CM_b, writes=[CM_b])
    p.dma(PI, ET[:], etab, ET_b, writes=[ET_b])
    p.dma(PI, DG[:], diag, DG_b, writes=[DG_b])
    p.dma(PI, BD[:], band, BD_b, writes=[BD_b])
    p.dma(PF, SELT[:], seltab, SELT_b, writes=[SELT_b])
    p.dma(PI, IDB[:], ident, IDB_b, writes=[IDB_b])
    p.dma(PI, PMb[:], pm, PMb_b, writes=[PMb_b])
    for w in range(2):
        p.dma(PI, W1[w][0][:], cw1[w], W1[w][1], writes=[W1[w][1]])
        p.dma(PI, W2[w][0][:], cw2[w], W2[w][1], writes=[W2[w][1]])
        p.dma(PI, POSC[w][0][:], cposT[w], POSC[w][1], writes=[POSC[w][1]])
    p.dma(PI, VC1[:, :, 129:193], ovt, VC1_b, writes=[VC1_b])
    p.op("pool", lambda e: e.memset(VC1[:, :, 128:129], 1.0), writes=[VC1_b])
    p.dma(PF, iv[:], inv2, iv_b, writes=[iv_b])
    p.op("act", lambda e: e.activation(out=NG[:], in_=NG[:], func=AF.Sigmoid), reads=[NG_b], writes=[NG_b])

    for qq in range(4):
        qs = slice(qq * 1024, (qq + 1) * 1024)
        p.dma(PF, POSI[:], pos[:, qs], POSI_b, writes=[POSI_b])
        p.op("dve", lambda e: e.tensor_copy(out=ANG[:], in_=POSI[:]), reads=[POSI_b], writes=[ANG_b])
        p.op("dve", lambda e: e.tensor_scalar(out=ANG[:], in0=ANG[:], scalar1=iv[:, 0:1], scalar2=None, op0=ALU.mult),
             reads=[ANG_b, iv_b], writes=[ANG_b])
        for (TAB, TAB_b, shift) in ((SIN, SIN_b, 0.0), (COS, COS_b, math.pi / 2)):
            p.op("dve", lambda e, shift=shift: e.tensor_scalar(out=KI[:], in0=ANG[:], scalar1=shift, scalar2=1.0 / TWO_PI,
                                                              op0=ALU.add, op1=ALU.mult), reads=[ANG_b], writes=[KI_b])
            p.op("dve", lambda e, TAB=TAB, qs=qs: e.tensor_copy(out=TAB[:, qs], in_=KI[:]), reads=[KI_b], writes=[TAB_b])
            p.op("dve", lambda e, TAB=TAB, qs=qs: e.scalar_tensor_tensor(out=TAB[:, qs], in0=TAB[:, qs], scalar=-TWO_PI, in1=ANG[:],
                                                                 op0=ALU.mult, op1=ALU.add), reads=[TAB_b, ANG_b], writes=[TAB_b])
            p.op("dve", lambda e, TAB=TAB, shift=shift, qs=qs: e.tensor_scalar(out=TAB[:, qs], in0=TAB[:, qs], scalar1=shift, scalar2=3.1415925,
                                                                       op0=ALU.add, op1=ALU.min), reads=[TAB_b], writes=[TAB_b])
            p.op("dve", lambda e, TAB=TAB, qs=qs: e.tensor_scalar(out=TAB[:, qs], in0=TAB[:, qs], scalar1=-3.1415925, scalar2=None, op0=ALU.max),
                 reads=[TAB_b], writes=[TAB_b])
    for (TAB, TAB_b) in ((SIN, SIN_b), (COS, COS_b)):
        p.op("act", lambda e, TAB=TAB: e.activation(out=TAB[:], in_=TAB[:], func=AF.Sin), reads=[TAB_b], writes=[TAB_b])

    ropecnt = [0]

    def rope(X32, Xb, tsl):
        i = ropecnt[0] % 2
        ropecnt[0] += 1
        t1, t1b = rt[i]
        p.op("pe", lambda e: e.matmul(pm0[0:32, :], lhsT=PMb[:], rhs=X32, start=True, stop=True),
             reads=[Xb, PMb_b], writes=[pm0_b])
        p.op("dve", lambda e: e.tensor_tensor(out=t1[:], in0=X32, in1=COS[:, tsl], op=ALU.mult), reads=[Xb, COS_b], writes=[t1b])
        p.op("dve", lambda e: e.tensor_tensor(out=X32, in0=pm0[0:32, :], in1=SIN[:, tsl], op=ALU.mult),
             reads=[pm0_b, SIN_b, Xb], writes=[Xb])
        p.op("dve", lambda e: e.tensor_tensor(out=X32, in0=X32, in1=t1[:], op=ALU.add), reads=[Xb, t1b], writes=[Xb])

    for k in range(8):
        tsl = slice(k * 512, (k + 1) * 512)
        rope(KS[0:32, tsl], KS_b, tsl)
        rope(KW[0:32, tsl], KW_b, tsl)

    for w, (SRC, SRC_b) in enumerate(((KC, KC_b), (VC, VC_b))):
        W1t, W1b = W1[w]
        W2t, W2b = W2[w]
        PCt, PCb = POSC[w]

        def mb(e, W1t=W1t, PCt=PCt):
            r = None
            for l in range(32):
                r = e.matmul(pm0[:, 0:1], lhsT=W1t[:, l * 128:(l + 1) * 128], rhs=PCt[:, l:l + 1], start=(l == 0), stop=(l == 31))
            return r
        p.op("pe", mb, reads=[W1b, PCb], writes=[pm0_b])
        p.op("dve", lambda e: e.tensor_copy(out=cb[:], in_=pm0[:, 0:1]), reads=[pm0_b], writes=[cb_b])

        def mh(e, W1t=W1t, SRC=SRC):
            r = None
            for l in range(32):
                r = e.matmul(pm0[:, 0:255], lhsT=W1t[:, l * 128:(l + 1) * 128], rhs=SRC[:, l:l + 16 * 254 + 1:16],
                             start=(l == 0), stop=(l == 31))
            return r
        p.op("pe", mh, reads=[W1b, SRC_b], writes=[pm0_b])
        u, ub = gt[0]
        v_, vb = gt[1]
        w_, wb_ = gt[2]
        p.op("act", lambda e: e.activation(out=u[:, 0:255], in_=pm0[:, 0:255], func=AF.Identity, bias=cb[:, 0:1]),
             reads=[pm0_b, cb_b], writes=[ub])
        p.op("dve", lambda e: e.tensor_tensor(out=v_[:, 0:255], in0=u[:, 0:255], in1=u[:, 0:255], op=ALU.mult), reads=[ub], writes=[vb])
        p.op("dve", lambda e: e.tensor_scalar(out=v_[:, 0:255], in0=v_[:, 0:255], scalar1=0.044715, scalar2=1.0, op0=ALU.mult, op1=ALU.add),
             reads=[vb], writes=[vb])
        p.op("dve", lambda e: e.tensor_tensor(out=v_[:, 0:255], in0=v_[:, 0:255], in1=u[:, 0:255], op=ALU.mult), reads=[vb, ub], writes=[vb])
        p.op("act", lambda e: e.activation(out=w_[:, 0:255], in_=v_[:, 0:255], func=AF.Tanh, scale=0.7978845608028654),
             reads=[vb], writes=[wb_])
        p.op("dve", lambda e: e.tensor_scalar(out=w_[:, 0:255], in0=w_[:, 0:255], scalar1=0.5, scalar2=0.5, op0=ALU.mult, op1=ALU.add),
             reads=[wb_], writes=[wb_])
        p.op("pool", lambda e: e.memset(HID[:], 0.0), reads=[HID_b], writes=[HID_b])
        p.op("dve", lambda e: e.tensor_tensor(out=HID[:, 0:255], in0=w_[:, 0:255], in1=u[:, 0:255], op=ALU.mult),
             reads=[wb_, ub, HID_b], writes=[HID_b])
        if w == 0:
            p.op("pe", lambda e, W2t=W2t: e.matmul(pm0[:, 0:256], lhsT=W2t[:], rhs=HID[:], start=True, stop=True),
                 reads=[W2b, HID_b], writes=[pm0_b])
            p.op("act", lambda e: e.activation(out=KCT[:], in_=pm0[:, 0:256], func=AF.Copy), reads=[pm0_b], writes=[KCT_b])
        else:
            for nt in range(2):
                p.op("pe", lambda e, W2t=W2t, nt=nt: e.matmul(pm0[:, 0:128], lhsT=HID[:, nt * 128:(nt + 1) * 128], rhs=W2t[:],
                                                             start=True, stop=True), reads=[W2b, HID_b], writes=[pm0_b])
                p.op("act", lambda e, nt=nt: e.activation(out=VC1[:, nt, 0:128], in_=pm0[:, 0:128], func=AF.Copy),
                     reads=[pm0_b], writes=[VC1_b])

    cnt = {"s": 0, "pt": 0, "rr": 0}

    def branch(k, j, br, tiles, nv, first_branch):
        tsl = slice(k * 512, (k + 1) * 512)
        first = {}
        last = {}
        for ti, tl in enumerate(tiles):
            for ts in tl[3]:
                first.setdefault(ts, ti)
                last[ts] = ti
        staged = {}

        def emit_S(ti):
            Kap, masks, V1ap, ts_list, rbufs = tiles[ti]
            sp_, spb = psS[cnt["s"] % 2]
            cnt["s"] += 1
            pt, ptb = PT[cnt["pt"] % 3]
            cnt["pt"] += 1

            def ms(e, sp_=sp_, Kap=Kap, masks=masks):
                r = e.matmul(sp_[:], lhsT=Kap, rhs=QT[:, j, tsl], start=True, stop=(len(masks) == 0))
                for mi, (ml, mr) in enumerate(masks):
                    r = e.matmul(sp_[:], lhsT=ml, rhs=mr, start=False, stop=(mi == len(masks) - 1))
                return r
            p.op("pe", ms, reads=[QT_b] + rbufs, writes=[spb])
            p.op("act", lambda e, pt=pt, sp_=sp_: e.activation(out=pt[:], in_=sp_[:], func=AF.Exp, scale=SCALE),
                 reads=[spb], writes=[ptb])
            staged[ti] = (pt, ptb)

        def emit_PV(ti):
            Kap, masks, V1ap, ts_list, rbufs = tiles[ti]
            pt, ptb = staged.pop(ti)

            def mv(e, pt=pt, V1ap=V1ap, ts_list=ts_list, ti=ti):
                r = None
                for ts in ts_list:
                    r = e.matmul(pv[ts][0][:, 0:nv], lhsT=pt[:, ts * 128:(ts + 1) * 128], rhs=V1ap,
                                 start=(first[ts] == ti), stop=(last[ts] == ti))
                return r
            p.op("pe", mv, reads=[ptb] + rbufs, writes=[pv[ts][1] for ts in ts_list])

        emit_S(0)
        for ti in range(len(tiles)):
            if ti + 1 < len(tiles):
                emit_S(ti + 1)
            emit_PV(ti)
        for ts in range(4):
            if ts not in first:
                continue
            tt = 4 * k + ts
            r_, rb = rr[cnt["rr"] % 4]
            cnt["rr"] += 1
            pvt, pvb = pv[ts]
            p.op("dve", lambda e, r_=r_, pvt=pvt: e.tensor_scalar(out=r_[:, 0:1], in0=pvt[:, 128:129], scalar1=1e-30, scalar2=None, op0=ALU.add),
                 reads=[pvb], writes=[rb])
            p.op("dve", lambda e, r_=r_: e.reciprocal(out=r_[:, 0:1], in_=r_[:, 0:1]), reads=[rb], writes=[rb])
            p.op("dve", lambda e, r_=r_, tt=tt: e.tensor_tensor(out=r_[:, 1:2], in0=r_[:, 0:1], in1=NG[:, tt, br * 4 + j:br * 4 + j + 1], op=ALU.mult),
                 reads=[rb, NG_b], writes=[rb])
            osl = OACC[:, ts, j * 128:(j + 1) * 128]
            if first_branch:
                p.op("dve", lambda e, osl=osl, pvt=pvt, r_=r_: e.tensor_scalar(out=osl, in0=pvt[:, 0:128], scalar1=r_[:, 1:2], scalar2=None, op0=ALU.mult),
                     reads=[pvb, rb, OACC_b], writes=[OACC_b])
            else:
                p.op("dve", lambda e, osl=osl, pvt=pvt, r_=r_: e.scalar_tensor_tensor(out=osl, in0=pvt[:, 0:128], scalar=r_[:, 1:2], in1=osl,
                                                                                   op0=ALU.mult, op1=ALU.add),
                     reads=[pvb, rb, OACC_b], writes=[OACC_b])
            if br == 0:
                if j == 0:
                    p.op("dve", lambda e, ts=ts, pvt=pvt, r_=r_: e.tensor_scalar(out=PSLC[:, ts, :], in0=pvt[:, 129:193], scalar1=r_[:, 0:1], scalar2=None, op0=ALU.mult),
                         reads=[pvb, rb, PSLC_b], writes=[PSLC_b])
                else:
                    p.op("dve", lambda e, ts=ts, pvt=pvt, r_=r_: e.scalar_tensor_tensor(out=PSLC[:, ts, :], in0=pvt[:, 129:193], scalar=r_[:, 0:1], in1=PSLC[:, ts, :],
                                                                                      op0=ALU.mult, op1=ALU.add),
                         reads=[pvb, rb, PSLC_b], writes=[PSLC_b])

    onsa_v = onsa.rearrange("(tt p) c -> p tt c", p=128)
    for k in range(8):
        tsl = slice(k * 512, (k + 1) * 512)
        for j in range(4):
            tiles = []
            for nt in range(2 if k >= 4 else 1):
                partial = (nt == 0 and k <= 4) or (nt == 1)
                masks = [(IDB[:], CM[:, nt * 4096 + k * 512:nt * 4096 + (k + 1) * 512])] if partial else []
                tiles.append((KCT[:, nt * 128:(nt + 1) * 128], masks, VC1[:, nt, :], [0, 1, 2, 3], [KCT_b, VC1_b, IDB_b, CM_b]))
            branch(k, j, 0, tiles, 193, True)
        for ts in range(4):
            tt = 4 * k + ts
            p.op("dve", lambda e, ts=ts, tt=tt: e.tensor_tensor(out=sc1[:], in0=PSLC[:, ts, :], in1=SELT[:, tt * 64:(tt + 1) * 64], op=ALU.add),
                 reads=[PSLC_b, SELT_b], writes=[sc1_b])
            p.op("dve", lambda e: e.max(out=m8[:], in_=sc1[:]), reads=[sc1_b], writes=[m8_b])
            p.op("dve", lambda e: e.match_replace(out=sc2[:], in_to_replace=m8[:], in_values=sc1[:], imm_value=-3.0e38),
                 reads=[sc1_b, m8_b], writes=[sc2_b])
            p.op("dve", lambda e: e.max(out=m8[:], in_=sc2[:]), reads=[sc2_b], writes=[m8_b])
            p.op("dve", lambda e: e.tensor_scalar(out=sc2[:], in0=sc1[:], scalar1=m8[:, 7:8], scalar2=None, op0=ALU.is_ge),
                 reads=[sc1_b, m8_b], writes=[sc2_b])
            p.op("dve", lambda e: e.tensor_scalar(out=nmk[:], in0=sc2[:], scalar1=-NEG, scalar2=NEG, op0=ALU.mult, op1=ALU.add),
                 reads=[sc2_b], writes=[nmk_b])
            p.op("pe", lambda e: e.transpose(out=pm1[0:64, 0:128], in_=nmk[:], identity=IDB[:]), reads=[nmk_b, IDB_b], writes=[pm1_b])
            p.op("act", lambda e, ts=ts: e.activation(out=NMT[:, ts * 128:(ts + 1) * 128], in_=pm1[0:64, 0:128], func=AF.Copy),
                 reads=[pm1_b], writes=[NMT_b])
        for j in range(4):
            rope(QT[0:32, j, tsl], QT_b, tsl)
        for j in range(4):
            tiles = []
            for kt in range(4 * k + 4):
                a = kt - 4 * k
                masks = [(ET[:, kt * 128:(kt + 1) * 128], NMT[:])]
                if a >= 0:
                    masks.append((IDB[:], DG[:, a * 512:(a + 1) * 512]))
                ts_list = [ts for ts in range(4) if a < 0 or ts >= a]
                tiles.append((KS[:, kt * 128:(kt + 1) * 128], masks, VS1[:, kt, :], ts_list, [KS_b, VS1_b, ET_b, NMT_b, IDB_b, DG_b]))
            branch(k, j, 1, tiles, 129, False)
        for j in range(4):
            tiles = []
            for c in range(8):
                kt = 4 * k - 4 + c
                if kt < 0:
                    continue
                masks = [(IDB[:], BD[:, c * 512:(c + 1) * 512])]
                ts_list = [ts for ts in range(4) if ts <= c <= ts + 4]
                tiles.append((KW[:, kt * 128:(kt + 1) * 128], masks, VW1[:, kt, :], ts_list, [KW_b, VW1_b, IDB_b, BD_b]))
            branch(k, j, 2, tiles, 129, False)
        p.dma(PF, onsa_v[:, 4 * k:4 * k + 4, :], OACC[:], OACC_b, reads=[OACC_b])
    p.finish("sp")
    p.emit()
    p.close()
    return nc


EPS = 1e-6
C = 64
NCH = 64


def build_A3(layer, ST=('pre', 'loop', 'post'), NH=4, NCL=NCH):
    nc = bass.Bass("TRN2", target_bir_lowering=False)
    dt = nc.dram_tensor
    hqT = dt("hqT", [4, 128, 4096], F32, kind="ExternalInput").ap()
    hfT = dt("hfT", [4, 128, 4096], F32, kind="ExternalInput").ap()
    hi = dt("hi", [4096, 512], F32, kind="ExternalInput").ap()
    hg = dt("hg", [4096, 512], F32, kind="ExternalInput").ap()
    lbl = dt("lbl", [128, 2, 4], F32, kind="ExternalInput").ap()
    nw64 = dt("nw64", [64, 128], F32, kind="ExternalInput").ap()
    rmask = dt("rmask", [128, 4096], F32, kind="ExternalInput").ap()
    tri = dt("tri", [64, 64], F32, kind="ExternalInput").ap()
    ident = dt("ident", [128, 128], F32, kind="ExternalInput").ap()
    ohg = dt("ohg", [4096, 512], F32, kind="ExternalOutput").ap()
    p = Prog(nc)
    A, A_b = p.sbuf("A", [128, 4096], F32)
    B, B_b = p.sbuf("B", [128, 4096], F32)
    Cc, C_b = p.sbuf("C", [128, 4096], F32)
    D, D_b = p.sbuf("D", [128, 4096], F32)
    HQ, HQ_b = p.sbuf("HQ", [128, 4096], BF16)
    Q1, Q1_b = p.sbuf("Q1", [128, 4096], BF16)
    K1, K1_b = p.sbuf("K1", [128, 4096], BF16)
    Q2, Q2_b = p.sbuf("Q2", [128, 4096], BF16)
    K2, K2_b = p.sbuf("K2", [128, 4096], BF16)
    V, V_b = p.sbuf("V", [64, NCH, 128], BF16)
    G, G_b = p.sbuf("G", [64, NCH, 128], BF16)
    O, O_b = p.sbuf("O", [64, NCH, 128], F32)
    RM, RM_b = p.sbuf("RM", [128, 4096], BF16)
    lb, lb_b = p.sbuf("lb", [128, 2, 4], F32)
    lbv, lbv_b = p.sbuf("lbv", [128, 4], F32)
    oml, oml_b = p.sbuf("oml", [128, 4], F32)
    nw, nw_b = p.sbuf("nw", [64, 128], F32)
    trif, trif_b = p.sbuf("trif", [64, 64], F32)
    triu, triu_b = p.sbuf("triu", [64, 64], U8)
    idb, idb_b = p.sbuf("idb", [128, 128], BF16)
    EBL, EBL_b = p.sbuf("EBL", [128, NCH], F32)
    ssq, ssq_b = p.sbuf("ssq", [64, NCH], F32)
    epsb, epsb_b = p.sbuf("epsb", [64, 1], F32)
    Sf, Sf_b = p.sbuf("Sf", [128, 128], F32)
    Sbf = [p.sbuf("Sbf%d" % i, [128, 128], BF16) for i in range(2)]
    attS = [p.sbuf("attS%d" % i, [64, 64], BF16) for i in range(2)]
    khat = [p.sbuf("khat%d" % i, [64, 128], BF16) for i in range(2)]
    ps_att = [p.psum("att%d" % i, [128, 512], F32) for i in range(2)]
    ps_kh = [p.psum("kh%d" % i, [128, 1024], BF16) for i in range(2)]
    ps_o = [p.psum("o%d" % i, [128, 512], F32) for i in range(2)]
    ps_sn = [p.psum("sn%d" % i, [128, 512], F32) for i in range(2)]
    PI, PF = "pool", "sp"
    p.dma(PI, RM[:], rmask, RM_b, writes=[RM_b])
    p.dma(PI, idb[:], ident, idb_b, writes=[idb_b])
    p.dma(PF, trif[:], tri, trif_b, writes=[trif_b])
    p.dma(PF, lb[:], lbl, lb_b, writes=[lb_b])
    p.dma(PF, nw[:], nw64, nw_b, writes=[nw_b])
    p.op("dve", lambda e: e.tensor_copy(out=triu[:], in_=trif[:]), reads=[trif_b], writes=[triu_b])
    p.op("pool", lambda e: e.memset(epsb[:], EPS), writes=[epsb_b])
    for i in range(2):
        p.op("pool", lambda e, i=i: e.memset(attS[i][0][:], 0.0), writes=[attS[i][1]])
    p.op("dve", lambda e: e.tensor_tensor(out=lbv[:], in0=lb[:, 1, :], in1=lb[:, 0, :], op=ALU.subtract), reads=[lb_b], writes=[lbv_b])
    p.op("act", lambda e: e.activation(out=lbv[:], in_=lbv[:], func=AF.Sigmoid), reads=[lbv_b], writes=[lbv_b])
    p.op("dve", lambda e: e.tensor_scalar(out=lbv[:], in0=lbv[:], scalar1=float(layer), scalar2=None, op0=ALU.mult),
         reads=[lbv_b], writes=[lbv_b])
    p.op("dve", lambda e: e.tensor_scalar(out=oml[:], in0=lbv[:], scalar1=-1.0, scalar2=1.0, op0=ALU.mult, op1=ALU.add),
         reads=[lbv_b], writes=[oml_b])

    def v3(t):
        return t[:].rearrange("p (c j) -> p c j", j=C)

    hi_v = hi.rearrange("(c p) e -> p c e", p=C)
    hg_v = hg.rearrange("(c p) e -> p c e", p=C)
    ohg_v = ohg.rearrange("(c p) e -> p c e", p=C)
    for hh in range(NH):
        hs = slice(hh * 128, (hh + 1) * 128)
        p.dma(PF, A[:], hfT[hh], A_b, writes=[A_b])
        p.dma(PI, HQ[:], hqT[hh], HQ_b, writes=[HQ_b])
        p.dma(PI, V[:], hi_v[:, :, hs], V_b, writes=[V_b])
        p.dma(PI, G[:], hg_v[:, :, hs], G_b, writes=[G_b])
        if 'pre' in ST:
            p.op("act", lambda e: e.activation(out=A[:], in_=A[:], func=AF.Sigmoid), reads=[A_b], writes=[A_b])
            p.op("dve", lambda e, hh=hh: e.tensor_scalar(out=A[:], in0=A[:], scalar1=oml[:, hh:hh + 1], scalar2=lbv[:, hh:hh + 1],
                                                        op0=ALU.mult, op1=ALU.add), reads=[A_b, oml_b, lbv_b], writes=[A_b])
            p.op("act", lambda e: e.activation(out=B[:], in_=A[:], func=AF.Ln), reads=[A_b], writes=[B_b])
            p.op("dve", lambda e: e.tensor_scalar(out=A[:], in0=A[:], scalar1=-1.0, scalar2=1.0, op0=ALU.mult, op1=ALU.add),
                 reads=[A_b], writes=[A_b])
            p.op("dve", lambda e: e.tensor_tensor_scan(out=Cc[:], data0=RM[:], data1=B[:], initial=0.0, op0=ALU.mult, op1=ALU.add),
                 reads=[RM_b, B_b], writes=[C_b])
            p.op("dve", lambda e: e.tensor_tensor(out=v3(B), in0=v3(Cc), in1=v3(Cc)[:, :, 31:32].to_broadcast([128, NCH, C]),
                                                  op=ALU.subtract), reads=[C_b], writes=[B_b])
            p.op("act", lambda e: e.activation(out=D[:], in_=B[:], func=AF.Exp), reads=[B_b], writes=[D_b])
            p.op("dve", lambda e: e.tensor_tensor(out=Q1[:], in0=HQ[:], in1=D[:], op=ALU.mult), reads=[HQ_b, D_b], writes=[Q1_b])
            p.op("act", lambda e: e.activation(out=D[:], in_=B[:], func=AF.Exp, scale=-1.0), reads=[B_b], writes=[D_b])
            p.op("dve", lambda e: e.tensor_tensor(out=K1[:], in0=A[:], in1=D[:], op=ALU.mult), reads=[A_b, D_b], writes=[K1_b])
            p.op("act", lambda e: e.activation(out=D[:], in_=Cc[:], func=AF.Exp), reads=[C_b], writes=[D_b])
            p.op("dve", lambda e: e.tensor_tensor(out=Q2[:], in0=HQ[:], in1=D[:], op=ALU.mult), reads=[HQ_b, D_b], writes=[Q2_b])
            p.op("dve", lambda e: e.tensor_tensor(out=v3(B), in0=v3(Cc)[:, :, 63:64].to_broadcast([128, NCH, C]), in1=v3(Cc),
                                                  op=ALU.subtract), reads=[C_b], writes=[B_b])
            p.op("act", lambda e: e.activation(out=D[:], in_=B[:], func=AF.Exp), reads=[B_b], writes=[D_b])
            p.op("dve", lambda e: e.tensor_tensor(out=K2[:], in0=A[:], in1=D[:], op=ALU.mult), reads=[A_b, D_b], writes=[K2_b])
            p.op("act", lambda e: e.activation(out=EBL[:], in_=v3(Cc)[:, :, 63], func=AF.Exp), reads=[C_b], writes=[EBL_b])
        p.op("pool", lambda e: e.memset(Sf[:], 0.0), writes=[Sf_b])
        p.op("pool", lambda e: e.memset(Sbf[0][0][:], 0.0), writes=[Sbf[0][1]])
        for c in range(NCL if 'loop' in ST else 0):
            cs = slice(c * C, (c + 1) * C)
            i = c % 2
            pa, pab = ps_att[i]
            pk, pkb = ps_kh[i]
            po, pob = ps_o[i]
            pn, pnb = ps_sn[i]
            at, atb = attS[i]
            kh, khb = khat[i]
            sb_cur, sb_curb = Sbf[c % 2]
            sb_nxt, sb_nxtb = Sbf[(c + 1) % 2]
            p.op("pe", lambda e, pa=pa, cs=cs: e.matmul(pa[0:C, 0:C], lhsT=K1[:, cs], rhs=Q1[:, cs], start=True, stop=True),
                 reads=[K1_b, Q1_b], writes=[pab])
            p.op("dve", lambda e, at=at, pa=pa: e.copy_predicated(out=at[:], mask=triu[:], data=pa[0:C, 0:C]),
                 reads=[pab, triu_b, atb], writes=[atb])
            p.op("pe", lambda e, pk=pk, cs=cs: e.transpose(out=pk[0:C, 0:128], in_=K2[:, cs], identity=idb[:]),
                 reads=[K2_b, idb_b], writes=[pkb])
            p.op("act", lambda e, kh=kh, pk=pk: e.activation(out=kh[:], in_=pk[0:C, 0:128], func=AF.Copy),
                 reads=[pkb], writes=[khb])

            def mo(e, po=po, at=at, c=c, cs=cs, sb_cur=sb_cur):
                e.matmul(po[0:C, 0:128], lhsT=at[:], rhs=V[:, c, :], start=True, stop=False)
                return e.matmul(po[0:C, 0:128], lhsT=Q2[:, cs], rhs=sb_cur[:], start=False, stop=True)
            p.op("pe", mo, reads=[atb, V_b, Q2_b, sb_curb], writes=[pob])
            p.op("act", lambda e, po=po, c=c: e.activation(out=O[:, c, :], in_=po[0:C, 0:128], func=AF.Copy),
                 reads=[pob], writes=[O_b])
            p.op("pe", lambda e, pn=pn, kh=kh, c=c: e.matmul(pn[:, 0:128], lhsT=kh[:], rhs=V[:, c, :], start=True, stop=True),
                 reads=[khb, V_b], writes=[pnb])
            p.op("dve", lambda e, pn=pn, c=c: e.scalar_tensor_tensor(out=Sf[:], in0=Sf[:], scalar=EBL[:, c:c + 1], in1=pn[:, 0:128],
                                                                    op0=ALU.mult, op1=ALU.add),
                 reads=[Sf_b, EBL_b, pnb], writes=[Sf_b])
            p.op("act", lambda e, sb_nxt=sb_nxt: e.activation(out=sb_nxt[:], in_=Sf[:], func=AF.Copy),
                 reads=[Sf_b], writes=[sb_nxtb])
        if 'post' in ST:
            A64 = A[0:64, :].rearrange("p (c e) -> p c e", e=128)
            B64 = B[0:64, :].rearrange("p (c e) -> p c e", e=128)
            p.op("dve", lambda e: e.tensor_tensor(out=A64, in0=O[:, 0:32, :], in1=O[:, 0:32, :], op=ALU.mult), reads=[O_b], writes=[A_b])
            p.op("dve", lambda e: e.tensor_tensor(out=B64, in0=O[:, 32:64, :], in1=O[:, 32:64, :], op=ALU.mult), reads=[O_b], writes=[B_b])
            p.op("dve", lambda e: e.tensor_reduce(out=ssq[:, 0:32], in_=A64, axis=AX.X, op=ALU.add), reads=[A_b], writes=[ssq_b])
            p.op("dve", lambda e: e.tensor_reduce(out=ssq[:, 32:64], in_=B64, axis=AX.X, op=ALU.add), reads=[B_b, ssq_b], writes=[ssq_b])
            p.op("act", lambda e: e.activation(out=ssq[:], in_=ssq[:], func=AF.Sqrt, scale=1.0 / 128, bias=epsb[:, 0:1]),
                 reads=[ssq_b, epsb_b], writes=[ssq_b])
            p.op("dve", lambda e: e.reciprocal(out=ssq[:], in_=ssq[:]), reads=[ssq_b], writes=[ssq_b])
            p.op("dve", lambda e: e.tensor_tensor(out=O[:], in0=O[:], in1=ssq[:].unsqueeze(2).to_broadcast([64, NCH, 128]), op=ALU.mult),
                 reads=[O_b, ssq_b], writes=[O_b])
            p.op("dve", lambda e: e.tensor_tensor(out=O[:], in0=O[:], in1=nw[:].unsqueeze(1).to_broadcast([64, NCH, 128]), op=ALU.mult),
                 reads=[O_b, nw_b], writes=[O_b])
            p.op("act", lambda e: e.activation(out=A64, in_=G[:, 0:32, :], func=AF.Silu), reads=[G_b, A_b], writes=[A_b])
            p.op("act", lambda e: e.activation(out=B64, in_=G[:, 32:64, :], func=AF.Silu), reads=[G_b, B_b], writes=[B_b])
            p.op("dve", lambda e: e.tensor_tensor(out=O[:, 0:32, :], in0=O[:, 0:32, :], in1=A64, op=ALU.mult), reads=[O_b, A_b], writes=[O_b])
            p.op("dve", lambda e: e.tensor_tensor(out=O[:, 32:64, :], in0=O[:, 32:64, :], in1=B64, op=ALU.mult), reads=[O_b, B_b], writes=[O_b])
        p.dma(PF, ohg_v[:, :, hs], O[:], O_b, reads=[O_b])
    p.finish("sp")
    p.emit()
    p.close()
    return nc


T = 1024
EPS = 1e-6


def build_B(last):
    nc = bass.Bass("TRN2", target_bir_lowering=False)
    dt = nc.dram_tensor
    xT = dt("xT", [32, 128, T], F32, kind="ExternalInput").ap()
    vecs = dt("vecs", [128, 9, 32], F32, kind="ExternalInput").ap()
    wga = dt("wga", [32, 128, 32 * 128], F32, kind="ExternalInput").ap()
    wgb = dt("wgb", [32, 128, 32 * 128], F32, kind="ExternalInput").ap()
    onT = dt("onT", [16, 128, T], F32, kind="ExternalInput").ap()
    ohT = dt("ohT", [16, 128, T], F32, kind="ExternalInput").ap()
    wua = dt("wua", [32, 128, 16 * 128], F32, kind="ExternalInput").ap()
    wub = dt("wub", [32, 128, 16 * 128], F32, kind="ExternalInput").ap()
    wo = dt("wo", [32, 128, 32 * 128], F32, kind="ExternalInput").ap()
    w1 = dt("w1", [128, 128, 32 * 128], F32, kind="ExternalInput").ap()
    w2 = dt("w2", [8, 128, 128 * 512], F32, kind="ExternalInput").ap()
    ones_d = dt("ones", [128, 128], F32, kind="ExternalInput").ap()
    outT = dt("outT", [32, 128, T], F32, kind="ExternalOutput").ap()
    yT_s = dt("yT_s", [32, 128, T], BF16, kind="Internal").ap()
    x1T_s = dt("x1T_s", [32, 128, T], F32, kind="Internal").ap()
    aT_s = dt("aT_s", [128, 128, T], BF16, kind="Internal").ap()
    x2T_s = dt("x2T_s", [32, 128, T], F32, kind="Internal").ap() if last else None

    p = Prog(nc)
    ones, ones_b = p.sbuf("ones", [128, 128], BF16)
    vc, vc_b = p.sbuf("vc", [128, 9, 32], F32)
    A1, A1_b = p.sbuf("A1", [128, 32], F32)
    A2, A2_b = p.sbuf("A2", [128, 32], F32)
    rstd, rstd_b = p.sbuf("rstd", [128, T], F32)
    big0, big0_b = p.sbuf("big0", [128, 32, T], BF16)
    big1, big1_b = p.sbuf("big1", [128, 32, T], BF16)
    NXB = 3
    xb = [p.sbuf("xb%d" % i, [128, T], F32) for i in range(NXB)]
    tf = [p.sbuf("tf%d" % i, [128, T], F32) for i in range(2)]
    tb = [p.sbuf("tb%d" % i, [128, T], BF16) for i in range(2)]
    sg = [(tf[i % 2][0][:, (i // 2) * 512:(i // 2 + 1) * 512], tf[i % 2][1]) for i in range(4)]
    wA = [p.sbuf("wA%d" % i, [128, 4096], BF16) for i in range(2)]
    wB = [p.sbuf("wB%d" % i, [128, 4096], BF16) for i in range(2)]
    wC = [p.sbuf("wC%d" % i, [128, 2048], BF16) for i in range(2)]
    wD = [p.sbuf("wD%d" % i, [128, 2048], BF16) for i in range(2)]
    ab = [(tf[i][0][:].bitcast(BF16), tf[i][1]) for i in range(2)]
    ps = [p.psum("ps%d" % i, [128, 512], F32) for i in range(8)]

    vbufs = {}
    PI, PF = "pool", "sp"

    p.dma(PI, ones[:], ones_d, ones_b, writes=[ones_b])
    p.dma(PF, vc[:], vecs, vc_b, writes=[vc_b])
    p.op("dve", lambda e: e.scalar_tensor_tensor(out=A1[:], in0=vc[:, 1, :], scalar=1.0, in1=vc[:, 6, :],
                                                 op0=ALU.add, op1=ALU.mult), reads=[vc_b], writes=[A1_b])
    p.op("dve", lambda e: e.scalar_tensor_tensor(out=A2[:], in0=vc[:, 4, :], scalar=1.0, in1=vc[:, 7, :],
                                                 op0=ALU.add, op1=ALU.mult), reads=[vc_b], writes=[A2_b])

    xcnt = [0]

    def load_x(src_cg_ap, src_buf=None):
        i = xcnt[0] % NXB
        xcnt[0] += 1
        t, b = xb[i]
        p.dma(PF, t[:], src_cg_ap, b, reads=[src_buf] if src_buf else [], writes=[b])
        return t, b

    def stats_pass(src, src_bufs, ssA, ssB):
        for cg in range(32):
            t, b = load_x(src[cg], src_bufs[cg] if src_bufs else None)
            sq, sqb = tb[cg % 2]
            p.op("act", lambda e, t=t, sq=sq: e.activation(out=sq[:], in_=t[:], func=AF.Square),
                 reads=[b], writes=[sqb])

            def mm(e, sq=sq, cg=cg):
                e.matmul(ssA[0][:], lhsT=ones[:], rhs=sq[:, 0:512], start=(cg == 0), stop=(cg == 31))
                return e.matmul(ssB[0][:], lhsT=ones[:], rhs=sq[:, 512:1024], start=(cg == 0), stop=(cg == 31))
            p.op("pe", mm, reads=[sqb, ones_b], writes=[ssA[1], ssB[1]])

    def make_rstd(ssA, ssB):
        for h, ss in enumerate((ssA, ssB)):
            sl = slice(h * 512, (h + 1) * 512)
            p.op("act", lambda e, ss=ss, sl=sl: e.activation(out=rstd[:, sl], in_=ss[0][:], func=AF.Sqrt,
                                                            scale=1.0 / 4096, bias=EPSB[:, 0:1]),
                 reads=[ss[1], epsb_b], writes=[rstd_b])
        p.op("dve", lambda e: e.reciprocal(out=rstd[:], in_=rstd[:]), reads=[rstd_b], writes=[rstd_b])

    EPSB, epsb_b = p.sbuf("epsb", [128, 1], F32)
    p.op("pool", lambda e: e.memset(EPSB[:], EPS), writes=[epsb_b])

    def norm_apply(src, src_bufs, A, Bv, dst, dst_b):
        for cg in range(32):
            t, b = load_x(src[cg], src_bufs[cg] if src_bufs else None)
            u, ub = tf[cg % 2]
            p.op("dve", lambda e, t=t, u=u: e.tensor_tensor(out=u[:], in0=t[:], in1=rstd[:], op=ALU.mult),
                 reads=[b, rstd_b], writes=[ub])
            p.op("act", lambda e, u=u, cg=cg: e.activation(out=dst[:, cg, :], in_=u[:], func=AF.Identity,
                                                          scale=A[:, cg:cg + 1], bias=vc[:, Bv, cg:cg + 1]),
                 reads=[ub, A1_b, A2_b, vc_b], writes=[dst_b])

    stats_pass(xT, None, ps[0], ps[1])
    make_rstd(ps[0], ps[1])
    norm_apply(xT, None, A1, 0, big0, big0_b)
    p.dma(PI, big1[:, 0:16, :], onT.rearrange("k p t -> p k t"), big1_b, writes=[big1_b])
    p.dma(PI, big1[:, 16:32, :], ohT.rearrange("k p t -> p k t"), big1_b, writes=[big1_b])
    yb = [p.buf("yT%d" % cg) for cg in range(32)]
    for cg in range(32):
        (w_a, w_ab), (w_b, w_bb), (w_c, w_cb), (w_d, w_db) = wA[cg % 2], wB[cg % 2], wC[cg % 2], wD[cg % 2]
        p.dma(PI, w_a[:], wga[cg], w_ab, writes=[w_ab])
        p.dma(PI, w_b[:], wgb[cg], w_bb, writes=[w_bb])
        p.dma(PI, w_c[:], wua[cg], w_cb, writes=[w_cb])
        p.dma(PI, w_d[:], wub[cg], w_db, writes=[w_db])
        yt, ytb = tb[cg % 2]
        for tt in range(2):
            sl = slice(tt * 512, (tt + 1) * 512)
            pset = ps[4 * tt:4 * tt + 4]

            def mm(e, w_a=w_a, w_b=w_b, w_c=w_c, w_d=w_d, sl=sl, pset=pset):
                for kt in range(32):
                    e.matmul(pset[0][0][:], lhsT=w_a[:, kt * 128:(kt + 1) * 128], rhs=big0[:, kt, sl],
                             start=(kt == 0), stop=(kt == 31))
                for kt in range(32):
                    e.matmul(pset[1][0][:], lhsT=w_b[:, kt * 128:(kt + 1) * 128], rhs=big0[:, kt, sl],
                             start=(kt == 0), stop=(kt == 31))
                for k in range(16):
                    e.matmul(pset[2][0][:], lhsT=w_c[:, k * 128:(k + 1) * 128], rhs=big1[:, k, sl],
                             start=(k == 0), stop=(k == 15))
                r = None
                for k in range(16):
                    r = e.matmul(pset[3][0][:], lhsT=w_d[:, k * 128:(k + 1) * 128],
                                 rhs=big1[:, 16 + k, sl], start=(k == 0), stop=(k == 15))
                return r
            p.op("pe", mm, reads=[w_ab, w_bb, w_cb, w_db, big0_b, big1_b], writes=[q[1] for q in pset])
            sga, sgab = sg[2 * tt]
            sgb, sgbb = sg[2 * tt + 1]
            p.op("act", lambda e, sga=sga, pset=pset: e.activation(out=sga[:], in_=pset[0][0][:], func=AF.Sigmoid),
                 reads=[pset[0][1]], writes=[sgab])
            p.op("act", lambda e, sgb=sgb, pset=pset: e.activation(out=sgb[:], in_=pset[1][0][:], func=AF.Sigmoid),
                 reads=[pset[1][1]], writes=[sgbb])
            p.op("dve", lambda e, sga=sga, pset=pset: e.tensor_tensor(out=sga[:], in0=sga[:], in1=pset[2][0][:], op=ALU.mult),
                 reads=[sgab, pset[2][1]], writes=[sgab])
            p.op("dve", lambda e, sgb=sgb, pset=pset: e.tensor_tensor(out=sgb[:], in0=sgb[:], in1=pset[3][0][:], op=ALU.mult),
                 reads=[sgbb, pset[3][1]], writes=[sgbb])
            p.op("dve", lambda e, sga=sga, sgb=sgb, yt=yt, sl=sl: e.tensor_tensor(out=yt[:, sl], in0=sga[:], in1=sgb[:], op=ALU.add),
                 reads=[sgab, sgbb], writes=[ytb])
        p.dma(PF, yT_s[cg], yt[:], ytb, reads=[ytb], writes=[yb[cg]])
    p.dma(PF, big0[:], yT_s.rearrange("k p t -> p k t"), big0_b, reads=yb, writes=[big0_b])
    x1b = [p.buf("x1T%d" % cg) for cg in range(32)]
    for cg in range(32):
        wt, wtb = (wA + wB)[cg % 4]
        p.dma(PI, wt[:, 0:4096], wo[cg], wtb, writes=[wtb])
        pset = ps[2 * (cg % 2):2 * (cg % 2) + 2]

        def mm(e, wt=wt, pset=pset):
            r = None
            for tt in range(2):
                for kt in range(32):
                    r = e.matmul(pset[tt][0][:], lhsT=wt[:, kt * 128:(kt + 1) * 128],
                                 rhs=big0[:, kt, tt * 512:(tt + 1) * 512], start=(kt == 0), stop=(kt == 31))
            return r
        p.op("pe", mm, reads=[wtb, big0_b], writes=[pset[0][1], pset[1][1]])
        t, b = load_x(xT[cg])
        for tt in range(2):
            sl = slice(tt * 512, (tt + 1) * 512)
            p.op("dve", lambda e, t=t, sl=sl, pset=pset, tt=tt, cg=cg: e.scalar_tensor_tensor(
                out=t[:, sl], in0=pset[tt][0][:], scalar=vc[:, 2, cg:cg + 1], in1=t[:, sl], op0=ALU.mult, op1=ALU.add),
                reads=[pset[tt][1], b, vc_b], writes=[b])
        p.dma(PF, x1T_s[cg], t[:], b, reads=[b], writes=[x1b[cg]])
        sq, sqb = tb[cg % 2]
        p.op("act", lambda e, t=t, sq=sq: e.activation(out=sq[:], in_=t[:], func=AF.Square), reads=[b], writes=[sqb])

        def mm2(e, sq=sq, cg=cg):
            e.matmul(ps[4][0][:], lhsT=ones[:], rhs=sq[:, 0:512], start=(cg == 0), stop=(cg == 31))
            return e.matmul(ps[5][0][:], lhsT=ones[:], rhs=sq[:, 512:1024], start=(cg == 0), stop=(cg == 31))
        p.op("pe", mm2, reads=[sqb, ones_b], writes=[ps[4][1], ps[5][1]])
    make_rstd(ps[4], ps[5])
    norm_apply(x1T_s, x1b, A2, 3, big1, big1_b)
    ab_b = [p.buf("aT%d" % f) for f in range(128)]
    for fg in range(128):
        wt, wtb = (wA + wB)[fg % 4]
        p.dma(PI, wt[:, 0:4096], w1[fg], wtb, writes=[wtb])
        pset = ps[2 * (fg % 2):2 * (fg % 2) + 2]

        def mm(e, wt=wt, pset=pset):
            r = None
            for tt in range(2):
                for kt in range(32):
                    r = e.matmul(pset[tt][0][:], lhsT=wt[:, kt * 128:(kt + 1) * 128],
                                 rhs=big1[:, kt, tt * 512:(tt + 1) * 512], start=(kt == 0), stop=(kt == 31))
            return r
        p.op("pe", mm, reads=[wtb, big1_b], writes=[pset[0][1], pset[1][1]])
        r_, rb = tf[fg % 2]
        a_, a_b = tb[fg % 2]
        for tt in range(2):
            sl = slice(tt * 512, (tt + 1) * 512)
            p.op("act", lambda e, r_=r_, sl=sl, pset=pset, tt=tt: e.activation(out=r_[:, sl], in_=pset[tt][0][:], func=AF.Relu),
                 reads=[pset[tt][1]], writes=[rb])
            p.op("dve", lambda e, r_=r_, a_=a_, sl=sl, pset=pset, tt=tt: e.tensor_tensor(
                out=a_[:, sl], in0=r_[:, sl], in1=pset[tt][0][:], op=ALU.mult),
                reads=[rb, pset[tt][1]], writes=[a_b])
        p.dma(PF, aT_s[:, fg, :], a_[:], a_b, reads=[a_b], writes=[ab_b[fg]])
    dst = x2T_s if last else outT
    x2b = [p.buf("x2T%d" % cg) for cg in range(32)]
    FB = 8
    NSL = 8
    aslot = []
    for i in range(NSL):
        sb_ = p.buf("aslot%d" % i)
        sb_.w = big0_b.w
        sb_.r = list(big0_b.r)
        aslot.append((big0[:, 4 * i:4 * i + 4, :], sb_))
    acnt = 0
    for db in range(8):
        for fc in range(128 // FB):
            wt, wtb = (wA + wB)[fc % 4]
            p.dma(PI, wt[:, 0:FB * 512], w2[db][:, fc * FB * 512:(fc + 1) * FB * 512], wtb, writes=[wtb])
            for f4 in range(FB // 4):
                f0 = fc * FB + f4 * 4
                at, atb = aslot[acnt % NSL]
                acnt += 1
                p.dma(PF, at, aT_s[:, f0:f0 + 4, :], atb, reads=ab_b[f0:f0 + 4], writes=[atb])

                def mm(e, wt=wt, at=at, f4=f4, f0=f0):
                    r = None
                    for fl in range(4):
                        fg = f0 + fl
                        wofs = (f4 * 4 + fl) * 512
                        for c4 in range(4):
                            for tt in range(2):
                                r = e.matmul(ps[c4 * 2 + tt][0][:], lhsT=wt[:, wofs + c4 * 128:wofs + (c4 + 1) * 128],
                                             rhs=at[:, fl, tt * 512:(tt + 1) * 512], start=(fg == 0), stop=(fg == 127))
                    return r
                p.op("pe", mm, reads=[wtb, atb], writes=[q[1] for q in ps])
        for c4 in range(4):
            cg = db * 4 + c4
            t, b = load_x(x1T_s[cg], x1b[cg])
            for tt in range(2):
                sl = slice(tt * 512, (tt + 1) * 512)
                p.op("dve", lambda e, t=t, sl=sl, tt=tt, cg=cg, c4=c4: e.scalar_tensor_tensor(
                    out=t[:, sl], in0=ps[c4 * 2 + tt][0][:], scalar=vc[:, 5, cg:cg + 1], in1=t[:, sl],
                    op0=ALU.mult, op1=ALU.add), reads=[ps[c4 * 2 + tt][1], b, vc_b], writes=[b])
            p.dma(PF, dst[cg], t[:], b, reads=[b], writes=[x2b[cg]])
    if last:
        stats_pass(x2T_s, x2b, ps[0], ps[1])
        make_rstd(ps[0], ps[1])
        for cg in range(32):
            t, b = load_x(x2T_s[cg], x2b[cg])
            p.op("dve", lambda e, t=t: e.tensor_tensor(out=t[:], in0=t[:], in1=rstd[:], op=ALU.mult),
                 reads=[b, rstd_b], writes=[b])
            u, ub = tf[cg % 2]
            p.op("act", lambda e, t=t, u=u, cg=cg: e.activation(out=u[:], in_=t[:], func=AF.Identity, scale=vc[:, 8, cg:cg + 1]),
                 reads=[b, vc_b], writes=[ub])
            p.dma(PF, outT[cg], u[:], ub, reads=[ub])
    p.finish("sp")
    p.emit()
    p.close()
    return nc


def _lay_kc(w, nk):
    K, N = w.shape
    return np.ascontiguousarray(w.reshape(nk, 128, N // 128, 128).transpose(2, 1, 0, 3)).reshape(N // 128, 128, nk * 128)


def _run(nc, in_maps):
    res = run_bass_kernel_spmd(nc, in_maps, core_ids=list(range(8)))
    return res.results


def kernel(x, c, positions, ada_w, ada_b, norm_mix_w, w_in, nsa_cmp_pos, nsa_cmp_w1, nsa_cmp_w2,
           hgrn_lb_logits, hgrn_norm_w, w_up_a, w_up_b, w_out, norm_mlp_w, w_mlp1, w_mlp2, final_norm_w):
    A = lambda a: np.asarray(a)
    x, c, positions, ada_w, ada_b, norm_mix_w, w_in = A(x), A(c), A(positions), A(ada_w), A(ada_b), A(norm_mix_w), A(w_in)
    nsa_cmp_pos, nsa_cmp_w1, nsa_cmp_w2 = A(nsa_cmp_pos), A(nsa_cmp_w1), A(nsa_cmp_w2)
    hgrn_lb_logits, hgrn_norm_w, w_up_a, w_up_b, w_out = A(hgrn_lb_logits), A(hgrn_norm_w), A(w_up_a), A(w_up_b), A(w_out)
    norm_mlp_w, w_mlp1, w_mlp2, final_norm_w = A(norm_mlp_w), A(w_mlp1), A(w_mlp2), A(final_norm_w)
    S = 4096
    ones = np.ones((128, 128), f32)
    ncM = build_M()
    maps = []
    for i in range(8):
        l, j = i // 4, i % 4
        cols = slice(j * NCOL, (j + 1) * NCOL)
        maps.append(dict(c=c, adaw=np.ascontiguousarray(ada_w[l][:, cols]).reshape(128, 32, NCOL),
                         adab=np.ascontiguousarray(np.stack([ada_b[l][cols]] * 2))))
    r = _run(ncM, maps)
    mod = np.zeros((2, 2, 6 * 4096), f32)
    for i in range(8):
        l, j = i // 4, i % 4
        mod[l][:, j * NCOL:(j + 1) * NCOL] = r[i]["mod"]
    del maps
    consts = nsa_consts()
    rmask = np.ascontiguousarray(np.tile((np.arange(S) % 64 != 0).astype(f32)[None], (128, 1)))
    tri = np.triu(np.ones((64, 64), f32))
    ncA1 = build_A1()
    ncA2 = build_A2()
    O = dict(q=0, kc=2048, vc=2560, ks=3072, vs=3584, kw=4096, vw=4608, ng=5120, hq=5168, hf=7216, hi=9264, hg=11312, ga=13360, gb=17456)
    xcur = x
    for l in range(2):
        Wl = w_in[l]
        m6 = mod[l].reshape(2, 6, 4096)
        wF, wT = [], []
        for g in range(4):
            fc = np.concatenate([np.arange(O["q"] + g * 512, O["q"] + (g + 1) * 512)] +
                                [np.arange(O[n] + g * 128, O[n] + (g + 1) * 128) for n in ("kc", "vc", "ks", "kw")] +
                                [np.arange(O[n] + g * 512, O[n] + (g + 1) * 512) for n in ("hq", "hf")])
            tc = np.concatenate([np.arange(O[n] + g * 128, O[n] + (g + 1) * 128) for n in ("vs", "vw")] +
                                [np.arange(O[n] + g * 512, O[n] + (g + 1) * 512) for n in ("hi", "hg")] +
                                [np.array([O["ng"] + br * 16 + g * 4 + j for br in range(3) for j in range(4)])])
            wF.append(_lay_kc(np.ascontiguousarray(Wl[:, fc]), 32))
            wT.append(np.ascontiguousarray(np.ascontiguousarray(Wl[:, tc]).reshape(32, 128, NT).transpose(1, 0, 2)))
        xTb = [np.ascontiguousarray(xcur[b].T).reshape(32, 128, S) for b in range(2)]
        maps = []
        for i in range(8):
            b, g = i // 4, i % 4
            v3 = np.stack([m6[b, 0], m6[b, 1], norm_mix_w[l]])
            maps.append(dict(xT=xTb[b], vecs=np.ascontiguousarray(v3.reshape(3, 32, 128).transpose(2, 0, 1)),
                             wF=wF[g], wT=wT[g], ones=ones))
        rA1 = _run(ncA1, maps)
        del maps, wF, wT, xTb
        maps = []
        for i in range(8):
            b, g = i // 4, i % 4
            oF, oT = rA1[i]["outF"], rA1[i]["outT"]
            d = dict(qT=np.ascontiguousarray(oF[0:4]), kT=np.ascontiguousarray(oF[4:8]),
                     vs=np.ascontiguousarray(oT[:, 0:128]), vw=np.ascontiguousarray(oT[:, 128:256]),
                     ng=np.ascontiguousarray(oT[:, 1280:1292]),
                     pos=np.ascontiguousarray(np.tile(positions[b][None].astype(np.int32), (32, 1))),
                     cposT=np.ascontiguousarray(nsa_cmp_pos[l].transpose(0, 2, 1)),
                     cw1=np.ascontiguousarray(nsa_cmp_w1[l].transpose(0, 2, 1, 3)).reshape(2, 128, 32 * 128),
                     cw2=np.ascontiguousarray(nsa_cmp_w2[l]))
            d.update(consts)
            maps.append(d)
        rA2 = _run(ncA2, maps)
        del maps
        ncA3 = build_A3(l)
        maps = []
        for i in range(8):
            b, g = i // 4, i % 4
            oF, oT = rA1[i]["outF"], rA1[i]["outT"]
            lbl = np.ascontiguousarray(hgrn_lb_logits.reshape(2, 16, 128)[:, 4 * g:4 * g + 4, :].transpose(2, 0, 1))
            maps.append(dict(hqT=np.ascontiguousarray(oF[8:12]), hfT=np.ascontiguousarray(oF[12:16]),
                             hi=np.ascontiguousarray(oT[:, 256:768]), hg=np.ascontiguousarray(oT[:, 768:1280]),
                             lbl=lbl, nw64=np.ascontiguousarray(np.tile(hgrn_norm_w[l][None], (64, 1))),
                             rmask=rmask, tri=tri, ident=consts["ident"]))
        rA3 = _run(ncA3, maps)
        del maps, rA1
        on = [np.concatenate([rA2[b * 4 + g]["onsa"] for g in range(4)], axis=1) for b in range(2)]
        oh = [np.concatenate([rA3[b * 4 + g]["ohg"] for g in range(4)], axis=1) for b in range(2)]
        del rA2, rA3
        wga = _lay_kc(np.ascontiguousarray(Wl[:, O["ga"]:O["ga"] + 4096]), 32)
        wgb = _lay_kc(np.ascontiguousarray(Wl[:, O["gb"]:O["gb"] + 4096]), 32)
        wua = _lay_kc(w_up_a[l], 16)
        wub = _lay_kc(w_up_b[l], 16)
        wo = _lay_kc(w_out[l], 32)
        w1 = _lay_kc(w_mlp1[l], 32)
        w2 = np.ascontiguousarray(w_mlp2[l].reshape(128, 128, 8, 512).transpose(2, 1, 0, 3)).reshape(8, 128, 128 * 512)
        last = (l == 1)
        ncB = build_B(last)
        maps = []
        for i in range(8):
            b, s0 = i // 4, (i % 4) * 1024
            v9 = np.concatenate([m6[b], np.stack([norm_mix_w[l], norm_mlp_w[l], final_norm_w])], 0)
            maps.append(dict(xT=np.ascontiguousarray(xcur[b, s0:s0 + 1024].T).reshape(32, 128, 1024),
                             vecs=np.ascontiguousarray(v9.reshape(9, 32, 128).transpose(2, 0, 1)),
                             wga=wga, wgb=wgb,
                             onT=np.ascontiguousarray(on[b][s0:s0 + 1024].T).reshape(16, 128, 1024),
                             ohT=np.ascontiguousarray(oh[b][s0:s0 + 1024].T).reshape(16, 128, 1024),
                             wua=wua, wub=wub, wo=wo, w1=w1, w2=w2, ones=ones))
        rB = _run(ncB, maps)
        del maps, wga, wgb, wua, wub, wo, w1, w2
        xn = np.empty((2, S, 4096), f32)
        for i in range(8):
            b, s0 = i // 4, (i % 4) * 1024
            xn[b, s0:s0 + 1024] = rB[i]["outT"].reshape(4096, 1024).T
        del rB
        xcur = xn
    return xcur
```

```python
import contextlib
import numpy as np
import concourse.bass as bass
import concourse.mybir as mybir

F32 = mybir.dt.float32
BF16 = mybir.dt.bfloat16
I32 = mybir.dt.int32
U8 = mybir.dt.uint8
ALU = mybir.AluOpType
AF = mybir.ActivationFunctionType
AX = mybir.AxisListType


class Buf:
    __slots__ = ("name", "w", "r", "dsem")

    def __init__(self, name):
        self.name = name
        self.w = None
        self.r = []
        self.dsem = None


class Prog:
    ENG = ("sp", "act", "dve", "pool", "pe")

    def __init__(self, nc):
        self.nc = nc
        self.stack = contextlib.ExitStack()
        self.streams = {k: [] for k in self.ENG}
        self.sems = {}
        self.cnt = {}
        self.known = {k: {} for k in self.ENG}
        self.nbuf = 0
        for k in ("act", "dve", "pool", "pe"):
            self._newsem(k)

    def _newsem(self, key):
        s = self.stack.enter_context(self.nc.semaphore("s_" + key))
        self.sems[key] = s
        self.cnt[key] = 0
        return s

    def sbuf(self, name, shape, dtype):
        t = self.stack.enter_context(self.nc.sbuf_tensor("sb_" + name, list(shape), dtype))
        return t, Buf(name)

    def psum(self, name, shape, dtype=F32):
        t = self.stack.enter_context(self.nc.psum_tensor("ps_" + name, list(shape), dtype))
        return t, Buf(name)

    def buf(self, name):
        return Buf(name)

    def _need(self, eng, tok, deps):
        if tok is None:
            return
        k, v = tok
        if deps.get(k, 0) < v:
            deps[k] = v

    def _emit_waits(self, eng, deps):
        kn = self.known[eng]
        for k, v in deps.items():
            if k.startswith("d_"):
                v = max(v, self.cnt[k])
            if kn.get(k, 0) < v:
                kn[k] = v
                sem = self.sems[k]
                self.streams[eng].append(("wait", sem, v))

    def _deps(self, eng, reads, writes):
        deps = {}
        for b in reads:
            self._need(eng, b.w, deps)
        for b in writes:
            self._need(eng, b.w, deps)
            for t in b.r:
                self._need(eng, t, deps)
        self._emit_waits(eng, deps)

    def _commit(self, tok, reads, writes):
        for b in reads:
            b.r.append(tok)
            if len(b.r) > 64:
                m = {}
                for k, v in b.r:
                    if m.get(k, 0) < v:
                        m[k] = v
                b.r = list(m.items())
        for b in writes:
            b.w = tok
            b.r = []

    def op(self, eng, fn, reads=(), writes=()):
        self._deps(eng, reads, writes)
        self.cnt[eng] += 1
        tok = (eng, self.cnt[eng])
        self.streams[eng].append(("op", fn, self.sems[eng], 1))
        self._commit(tok, reads, writes)
        return tok

    def dma(self, eng, out, in_, sb, reads=(), writes=(), **kw):
        if sb.dsem is None:
            self.nbuf += 1
            sb.dsem = "d_%d_%s" % (self.nbuf, sb.name)
            self._newsem(sb.dsem)
        self._deps(eng, reads, writes)
        k = sb.dsem
        self.cnt[k] += 16
        tok = (k, self.cnt[k])
        self.streams[eng].append(("op", (lambda e: e.dma_start(out=out, in_=in_, **kw)), self.sems[k], 16))
        self._commit(tok, reads, writes)
        return tok

    def finish(self, eng="sp"):
        deps = {k: v for k, v in self.cnt.items() if v > 0}
        self._emit_waits(eng, deps)

    def emit(self):
        nc = self.nc
        with nc.Block() as block:
            decos = {"sp": block.sync, "act": block.scalar, "dve": block.vector,
                     "pool": block.gpsimd, "pe": block.tensor}
            for key in self.ENG:
                stream = self.streams[key]
                if not stream:
                    continue

                def body(e, stream=stream):
                    for it in stream:
                        if it[0] == "wait":
                            e.wait_ge(it[1], it[2])
                        else:
                            ins = it[1](e)
                            ins.then_inc(it[2], it[3])

                decos[key](body)

    def close(self):
        self.stack.close()


def simulate(prog):
    val = {id(s): 0 for s in prog.sems.values()}
    pos = {k: 0 for k in prog.ENG}
    progress = True
    while progress:
        progress = False
        for k in prog.ENG:
            st = prog.streams[k]
            while pos[k] < len(st):
                it = st[pos[k]]
                if it[0] == "wait":
                    if val[id(it[1])] >= it[2]:
                        pos[k] += 1
                        progress = True
                    else:
                        break
                else:
                    val[id(it[2])] += it[3]
                    pos[k] += 1
                    progress = True
    stuck = {k: (pos[k], len(prog.streams[k])) for k in prog.ENG if pos[k] < len(prog.streams[k])}
    return stuck


from concourse.bass_utils import run_bass_kernel_spmd
import math


f32 = np.float32
NEGV = -30000.0
def nsa_consts():
    S = 4096
    t = np.arange(S)
    c = {}
    c["ident"] = np.eye(128, dtype=f32)
    inv = (500000.0 ** (-np.arange(0, 32, 2, dtype=np.float32) / 32)).astype(f32)
    c["inv2"] = np.concatenate([inv, inv]).reshape(32, 1).astype(f32)
    pm = np.zeros((32, 32), f32)
    for d in range(16):
        pm[d + 16, d] = -1.0
        pm[d, d + 16] = 1.0
    c["pm"] = pm
    n = np.arange(256)
    end = n * 16 + 31
    vis = (end[:, None] <= t[None, :]) & (n[:, None] < 255)
    cm = np.where(vis, 0.0, NEGV).astype(f32).reshape(2, 128, S).transpose(1, 0, 2).reshape(128, 2 * S)
    c["cmpmask"] = np.ascontiguousarray(cm)
    cs = np.arange(255) * 16; ce = cs + 31
    ss = np.arange(64) * 64; se = ss + 63
    ov = np.maximum(np.minimum(ce[:, None], se[None, :]) - np.maximum(cs[:, None], ss[None, :]) + 1, 0).astype(f32)
    ov = np.concatenate([ov, np.zeros((1, 64), f32)], 0)
    c["ovt"] = np.ascontiguousarray(ov.reshape(2, 128, 64).transpose(1, 0, 2))
    blk = np.arange(64)
    cur = t // 64
    forced = (blk[None, :] == 0) | (blk[None, :] == cur[:, None]) | (blk[None, :] == cur[:, None] - 1)
    valid = blk[None, :] * 64 <= t[:, None]
    st = np.where(valid, np.where(forced, 1e6, 0.0), -1e9).astype(f32)
    c["seltab"] = np.ascontiguousarray(st.reshape(32, 128, 64).transpose(1, 0, 2)).reshape(128, 32 * 64)
    c["etab"] = (np.arange(S)[None, :] // 64 == blk[:, None]).astype(f32)
    p = np.arange(128); u = np.arange(512)
    dg = np.stack([np.where(128 * a + p[:, None] <= u[None, :], 0.0, NEGV) for a in range(4)], 1).astype(f32)
    c["diag"] = np.ascontiguousarray(dg).reshape(128, 4 * 512)
    bd = []
    for cc in range(8):
        key = 128 * (cc - 4) + p[:, None]
        ok = (key <= u[None, :]) & (key > u[None, :] - 512)
        bd.append(np.where(ok, 0.0, NEGV))
    c["band"] = np.ascontiguousarray(np.stack(bd, 1).astype(f32)).reshape(128, 8 * 512)
    return c


NCOL = 6144


def build_M():
    nc = bass.Bass("TRN2", target_bir_lowering=False)
    dt = nc.dram_tensor
    c = dt("c", [2, 4096], F32, kind="ExternalInput").ap()
    adaw = dt("adaw", [128, 32, NCOL], F32, kind="ExternalInput").ap()
    adab = dt("adab", [2, NCOL], F32, kind="ExternalInput").ap()
    mod = dt("mod", [2, NCOL], F32, kind="ExternalOutput").ap()
    p = Prog(nc)
    cT, cT_b = p.sbuf("cT", [128, 2, 32], F32)
    bias, bias_b = p.sbuf("bias", [2, NCOL], F32)
    res, res_b = p.sbuf("res", [2, NCOL], F32)
    wt = [p.sbuf("wt%d" % i, [128, 32, 512], F32) for i in range(2)]
    ps = [p.psum("ps%d" % i, [2, 512], F32) for i in range(2)]
    p.dma("sp", cT[:], c.rearrange("b (p kt) -> p b kt", kt=32), cT_b, writes=[cT_b])
    p.dma("sp", bias[:], adab, bias_b, writes=[bias_b])
    p.op("act", lambda e: e.activation(out=cT[:], in_=cT[:], func=AF.Silu), reads=[cT_b], writes=[cT_b])
    for ct in range(NCOL // 512):
        w, wb = wt[ct % 2]
        sl = slice(ct * 512, (ct + 1) * 512)
        p.dma("sp" if ct % 2 == 0 else "act", w[:], adaw[:, :, sl], wb, writes=[wb])
        pt, ptb = ps[ct % 2]

        def mm(e, w=w, pt=pt):
            r = None
            for kt in range(32):
                r = e.matmul(pt[:], lhsT=cT[:, :, kt], rhs=w[:, kt, :], start=(kt == 0), stop=(kt == 31))
            return r
        p.op("pe", mm, reads=[wb, cT_b], writes=[ptb])
        p.op("dve", lambda e, pt=pt, sl=sl: e.tensor_tensor(out=res[:, sl], in0=pt[:], in1=bias[:, sl], op=ALU.add),
             reads=[ptb, bias_b], writes=[res_b])
    p.dma("sp", mod, res[:], res_b, reads=[res_b])
    p.finish("sp")
    p.emit()
    p.close()
    return nc


EPS = 1e-6
NT = 1292
NTS = [(0, 512), (512, 512), (1024, 268)]


def build_A1():
    nc = bass.Bass("TRN2", target_bir_lowering=False)
    dt = nc.dram_tensor
    xT = dt("xT", [32, 128, 4096], F32, kind="ExternalInput").ap()
    vecs = dt("vecs", [128, 3, 32], F32, kind="ExternalInput").ap()
    wF = dt("wF", [16, 128, 4096], F32, kind="ExternalInput").ap()
    wT = dt("wT", [128, 32, NT], F32, kind="ExternalInput").ap()
    ones_d = dt("ones", [128, 128], F32, kind="ExternalInput").ap()
    outF = dt("outF", [16, 128, 4096], F32, kind="ExternalOutput").ap()
    outT = dt("outT", [4096, NT], F32, kind="ExternalOutput").ap()
    p = Prog(nc)
    ones, ones_b = p.sbuf("ones", [128, 128], BF16)
    vc, vc_b = p.sbuf("vc", [128, 3, 32], F32)
    A1, A1_b = p.sbuf("A1", [128, 32], F32)
    EPSB, epsb_b = p.sbuf("epsb", [128, 1], F32)
    rstd, rstd_b = p.sbuf("rstd", [128, 1024], F32)
    big0, big0_b = p.sbuf("big0", [128, 32, 1024], BF16)
    wTs, wTs_b = p.sbuf("wTs", [128, 32, NT], BF16)
    wf = [p.sbuf("wf%d" % i, [128, 4096], BF16) for i in range(2)]
    xb = [p.sbuf("xb%d" % i, [128, 1024], F32) for i in range(3)]
    tf = [p.sbuf("tf%d" % i, [128, 1024], F32) for i in range(2)]
    tb = [p.sbuf("tb%d" % i, [128, 1024], BF16) for i in range(2)]
    ot = [p.sbuf("ot%d" % i, [128, NT], F32) for i in range(2)]
    ps = [p.psum("ps%d" % i, [128, 512], F32) for i in range(8)]
    PI, PF = "pool", "sp"
    p.dma(PI, ones[:], ones_d, ones_b, writes=[ones_b])
    p.dma(PF, vc[:], vecs, vc_b, writes=[vc_b])
    p.dma(PI, wTs[:], wT, wTs_b, writes=[wTs_b])
    p.op("pool", lambda e: e.memset(EPSB[:], EPS), writes=[epsb_b])
    p.op("dve", lambda e: e.scalar_tensor_tensor(out=A1[:], in0=vc[:, 1, :], scalar=1.0, in1=vc[:, 2, :],
                                                 op0=ALU.add, op1=ALU.mult), reads=[vc_b], writes=[A1_b])
    xcnt = [0]

    def load_x(ap):
        i = xcnt[0] % 3
        xcnt[0] += 1
        t, b = xb[i]
        p.dma(PF, t[:], ap, b, writes=[b])
        return t, b

    fcnt = 0
    for ch in range(4):
        csl = slice(ch * 1024, (ch + 1) * 1024)
        for cg in range(32):
            t, b = load_x(xT[cg][:, csl])
            sq, sqb = tb[cg % 2]
            p.op("act", lambda e, t=t, sq=sq: e.activation(out=sq[:], in_=t[:], func=AF.Square), reads=[b], writes=[sqb])
            p.op("dve", lambda e, t=t, cg=cg: e.tensor_copy(out=big0[:, cg, :], in_=t[:]), reads=[b, big0_b], writes=[big0_b])

            def mm(e, sq=sq, cg=cg):
                e.matmul(ps[6][0][:], lhsT=ones[:], rhs=sq[:, 0:512], start=(cg == 0), stop=(cg == 31))
                return e.matmul(ps[7][0][:], lhsT=ones[:], rhs=sq[:, 512:1024], start=(cg == 0), stop=(cg == 31))
            p.op("pe", mm, reads=[sqb, ones_b], writes=[ps[6][1], ps[7][1]])
        for h in range(2):
            sl = slice(h * 512, (h + 1) * 512)
            p.op("act", lambda e, h=h, sl=sl: e.activation(out=rstd[:, sl], in_=ps[6 + h][0][:], func=AF.Sqrt,
                                                          scale=1.0 / 4096, bias=EPSB[:, 0:1]),
                 reads=[ps[6 + h][1], epsb_b], writes=[rstd_b])
        p.op("dve", lambda e: e.reciprocal(out=rstd[:], in_=rstd[:]), reads=[rstd_b], writes=[rstd_b])
        for cg in range(32):
            u, ub = tf[cg % 2]
            p.op("dve", lambda e, u=u, cg=cg: e.tensor_tensor(out=u[:], in0=big0[:, cg, :], in1=rstd[:], op=ALU.mult),
                 reads=[big0_b, rstd_b], writes=[ub])
            p.op("act", lambda e, u=u, cg=cg: e.activation(out=big0[:, cg, :], in_=u[:], func=AF.Identity,
                                                          scale=A1[:, cg:cg + 1], bias=vc[:, 0, cg:cg + 1]),
                 reads=[ub, A1_b, vc_b], writes=[big0_b])
        for cf in range(16):
            w, wb = wf[fcnt % 2]
            pset = ps[2 * (fcnt % 2):2 * (fcnt % 2) + 2]
            o, ob = tf[fcnt % 2]
            fcnt += 1
            p.dma(PI, w[:], wF[cf], wb, writes=[wb])

            def mm(e, w=w, pset=pset):
                r = None
                for tt in range(2):
                    for kt in range(32):
                        r = e.matmul(pset[tt][0][:], lhsT=w[:, kt * 128:(kt + 1) * 128],
                                     rhs=big0[:, kt, tt * 512:(tt + 1) * 512], start=(kt == 0), stop=(kt == 31))
                return r
            p.op("pe", mm, reads=[wb, big0_b], writes=[pset[0][1], pset[1][1]])
            p.op("act", lambda e, o=o, pset=pset: e.activation(out=o[:, 0:512], in_=pset[0][0][:], func=AF.Copy),
                 reads=[pset[0][1]], writes=[ob])
            p.op("dve", lambda e, o=o, pset=pset: e.tensor_copy(out=o[:, 512:1024], in_=pset[1][0][:]),
                 reads=[pset[1][1], ob], writes=[ob])
            p.dma(PF, outF[cf][:, csl], o[:], ob, reads=[ob])
        for t8 in range(8):
            o, ob = ot[t8 % 2]
            for ni, (n0, nn) in enumerate(NTS):
                pt, ptb = ps[4 + (t8 * 3 + ni) % 2]

                def mm(e, pt=pt, t8=t8, n0=n0, nn=nn):
                    r = None
                    for kt in range(32):
                        r = e.matmul(pt[:, 0:nn], lhsT=big0[:, kt, t8 * 128:(t8 + 1) * 128], rhs=wTs[:, kt, n0:n0 + nn],
                                     start=(kt == 0), stop=(kt == 31))
                    return r
                p.op("pe", mm, reads=[big0_b, wTs_b], writes=[ptb])
                if ni % 2 == 0:
                    p.op("act", lambda e, o=o, pt=pt, n0=n0, nn=nn: e.activation(out=o[:, n0:n0 + nn], in_=pt[:, 0:nn], func=AF.Copy),
                         reads=[ptb, ob], writes=[ob])
                else:
                    p.op("dve", lambda e, o=o, pt=pt, n0=n0, nn=nn: e.tensor_copy(out=o[:, n0:n0 + nn], in_=pt[:, 0:nn]),
                         reads=[ptb, ob], writes=[ob])
            r0 = ch * 1024 + t8 * 128
            p.dma(PF, outT[r0:r0 + 128, :], o[:], ob, reads=[ob])
    p.finish("sp")
    p.emit()
    p.close()
    return nc


import math

SCALE = 128 ** -0.5
NEG = -30000.0
TWO_PI = 2 * math.pi


def build_A2():
    nc = bass.Bass("TRN2", target_bir_lowering=False)
    dt = nc.dram_tensor
    I = "ExternalInput"
    qT = dt("qT", [4, 128, 4096], F32, kind=I).ap()
    kT = dt("kT", [4, 128, 4096], F32, kind=I).ap()
    vs = dt("vs", [4096, 128], F32, kind=I).ap()
    vw = dt("vw", [4096, 128], F32, kind=I).ap()
    ng = dt("ng", [4096, 12], F32, kind=I).ap()
    pos = dt("pos", [32, 4096], I32, kind=I).ap()
    inv2 = dt("inv2", [32, 1], F32, kind=I).ap()
    pm = dt("pm", [32, 32], F32, kind=I).ap()
    cposT = dt("cposT", [2, 128, 32], F32, kind=I).ap()
    cw1 = dt("cw1", [2, 128, 32 * 128], F32, kind=I).ap()
    cw2 = dt("cw2", [2, 128, 128], F32, kind=I).ap()
    ident = dt("ident", [128, 128], F32, kind=I).ap()
    cmpmask = dt("cmpmask", [128, 2 * 4096], F32, kind=I).ap()
    ovt = dt("ovt", [128, 2, 64], F32, kind=I).ap()
    seltab = dt("seltab", [128, 32 * 64], F32, kind=I).ap()
    etab = dt("etab", [64, 4096], F32, kind=I).ap()
    diag = dt("diag", [128, 4 * 512], F32, kind=I).ap()
    band = dt("band", [128, 8 * 512], F32, kind=I).ap()
    onsa = dt("onsa", [4096, 512], F32, kind="ExternalOutput").ap()
    p = Prog(nc)
    PI, PF = "pool", "sp"
    QT, QT_b = p.sbuf("QT", [128, 4, 4096], BF16)
    KK = [p.sbuf("KK%d" % i, [128, 4096], BF16) for i in range(4)]
    (KC, KC_b), (VC, VC_b), (KS, KS_b), (KW, KW_b) = KK
    VS1, VS1_b = p.sbuf("VS1", [128, 32, 129], BF16)
    VW1, VW1_b = p.sbuf("VW1", [128, 32, 129], BF16)
    NG, NG_b = p.sbuf("NG", [128, 32, 12], F32)
    CM, CM_b = p.sbuf("CM", [128, 2 * 4096], BF16)
    ET, ET_b = p.sbuf("ET", [64, 4096], BF16)
    DG, DG_b = p.sbuf("DG", [128, 4 * 512], BF16)
    BD, BD_b = p.sbuf("BD", [128, 8 * 512], BF16)
    SELT, SELT_b = p.sbuf("SELT", [128, 32 * 64], F32)
    IDB, IDB_b = p.sbuf("IDB", [128, 128], BF16)
    PMb, PMb_b = p.sbuf("PMb", [32, 32], BF16)
    W1 = [p.sbuf("W1%d" % i, [128, 32 * 128], BF16) for i in range(2)]
    W2 = [p.sbuf("W2%d" % i, [128, 128], BF16) for i in range(2)]
    POSC = [p.sbuf("POSC%d" % i, [128, 32], BF16) for i in range(2)]
    KCT, KCT_b = p.sbuf("KCT", [128, 256], BF16)
    VC1, VC1_b = p.sbuf("VC1", [128, 2, 193], BF16)
    HID, HID_b = p.sbuf("HID", [128, 256], BF16)
    gt = [p.sbuf("gt%d" % i, [128, 256], F32) for i in range(3)]
    cb, cb_b = p.sbuf("cb", [128, 1], F32)
    POSI, POSI_b = p.sbuf("POSI", [32, 1024], I32)
    ANG, ANG_b = p.sbuf("ANG", [32, 1024], F32)
    COS, COS_b = p.sbuf("COS", [32, 4096], F32)
    SIN, SIN_b = p.sbuf("SIN", [32, 4096], F32)
    KI, KI_b = (POSI, POSI_b)
    iv, iv_b = p.sbuf("iv", [32, 1], F32)
    PT = [p.sbuf("PT%d" % i, [128, 512], BF16) for i in range(3)]
    OACC, OACC_b = p.sbuf("OACC", [128, 4, 512], F32)
    PSLC, PSLC_b = p.sbuf("PSLC", [128, 4, 64], F32)
    NMT, NMT_b = p.sbuf("NMT", [64, 512], BF16)
    sc1, sc1_b = p.sbuf("sc1", [128, 64], F32)
    sc2, sc2_b = p.sbuf("sc2", [128, 64], F32)
    nmk, nmk_b = p.sbuf("nmk", [128, 64], BF16)
    m8, m8_b = p.sbuf("m8", [128, 8], F32)
    rr = [p.sbuf("rr%d" % i, [128, 2], F32) for i in range(4)]
    rt = [p.sbuf("rt%d" % i, [32, 512], F32) for i in range(2)]
    psS = [p.psum("S%d" % i, [128, 512], F32) for i in range(2)]
    pv = [p.psum("pv%d" % i, [128, 512], F32) for i in range(4)]
    pm0, pm0_b = p.psum("m0", [128, 512], F32)
    pm1, pm1_b = p.psum("m1", [128, 1024], BF16)

    p.dma(PI, QT[:], qT.rearrange("j d t -> d j t"), QT_b, writes=[QT_b])
    for i in range(4):
        p.dma(PI, KK[i][0][:], kT[i], KK[i][1], writes=[KK[i][1]])
    p.dma(PI, VS1[:, :, 0:128], vs.rearrange("(kt p) d -> p kt d", p=128), VS1_b, writes=[VS1_b])
    p.dma(PI, VW1[:, :, 0:128], vw.rearrange("(kt p) d -> p kt d", p=128), VW1_b, writes=[VW1_b])
    p.op("pool", lambda e: e.memset(VS1[:, :, 128:129], 1.0), writes=[VS1_b])
    p.op("pool", lambda e: e.memset(VW1[:, :, 128:129], 1.0), writes=[VW1_b])
    p.dma(PF, NG[:], ng.rearrange("(tt p) c -> p tt c", p=128), NG_b, writes=[NG_b])
    p.dma(PI, CM[:], cmpmask, CM_b, writes=[CM_b])
    p.dma(PI, ET[:], etab, ET_b, writes=[ET_b])
    p.dma(PI, DG[:], diag, DG_b, writes=[DG_b])
    p.dma(PI, BD[:], band, BD_b, writes=[BD_b])
    p.dma(PF, SELT[:], seltab, SELT_b, writes=[SELT_b])
    p.dma(PI, IDB[:], ident, IDB_b, writes=[IDB_b])
    p.dma(PI, PMb[:], pm, PMb_b, writes=[PMb_b])
    for w in range(2):
        p.dma(PI, W1[w][0][:], cw1[w], W1[w][1], writes=[W1[w][1]])
        p.dma(PI, W2[w][0][:], cw2[w], W2[w][1], writes=[W2[w][1]])
        p.dma(PI, POSC[w][0][:], cposT[w], POSC[w][1], writes=[POSC[w][1]])
    p.dma(PI, VC1[:, :, 129:193], ovt, VC1_b, writes=[VC1_b])
    p.op("pool", lambda e: e.memset(VC1[:, :, 128:129], 1.0), writes=[VC1_b])
    p.dma(PF, iv[:], inv2, iv_b, writes=[iv_b])
    p.op("act", lambda e: e.activation(out=NG[:], in_=NG[:], func=AF.Sigmoid), reads=[NG_b], writes=[NG_b])

    for qq in range(4):
        qs = slice(qq * 1024, (qq + 1) * 1024)
        p.dma(PF, POSI[:], pos[:, qs], POSI_b, writes=[POSI_b])
        p.op("dve", lambda e: e.tensor_copy(out=ANG[:], in_=POSI[:]), reads=[POSI_b], writes=[ANG_b])
        p.op("dve", lambda e: e.tensor_scalar(out=ANG[:], in0=ANG[:], scalar1=iv[:, 0:1], scalar2=None, op0=ALU.mult),
             reads=[ANG_b, iv_b], writes=[ANG_b])
        for (TAB, TAB_b, shift) in ((SIN, SIN_b, 0.0), (COS, COS_b, math.pi / 2)):
            p.op("dve", lambda e, shift=shift: e.tensor_scalar(out=KI[:], in0=ANG[:], scalar1=shift, scalar2=1.0 / TWO_PI,
                                                              op0=ALU.add, op1=ALU.mult), reads=[ANG_b], writes=[KI_b])
            p.op("dve", lambda e, TAB=TAB, qs=qs: e.tensor_copy(out=TAB[:, qs], in_=KI[:]), reads=[KI_b], writes=[TAB_b])
            p.op("dve", lambda e, TAB=TAB, qs=qs: e.scalar_tensor_tensor(out=TAB[:, qs], in0=TAB[:, qs], scalar=-TWO_PI, in1=ANG[:],
                                                                 op0=ALU.mult, op1=ALU.add), reads=[TAB_b, ANG_b], writes=[TAB_b])
            p.op("dve", lambda e, TAB=TAB, shift=shift, qs=qs: e.tensor_scalar(out=TAB[:, qs], in0=TAB[:, qs], scalar1=shift, scalar2=3.1415925,
                                                                       op0=ALU.add, op1=ALU.min), reads=[TAB_b], writes=[TAB_b])
            p.op("dve", lambda e, TAB=TAB, qs=qs: e.tensor_scalar(out=TAB[:, qs], in0=TAB[:, qs], scalar1=-3.1415925, scalar2=None, op0=ALU.max),
                 reads=[TAB_b], writes=[TAB_b])
    for (TAB, TAB_b) in ((SIN, SIN_b), (COS, COS_b)):
        p.op("act", lambda e, TAB=TAB: e.activation(out=TAB[:], in_=TAB[:], func=AF.Sin), reads=[TAB_b], writes=[TAB_b])

    ropecnt = [0]

    def rope(X32, Xb, tsl):
        i = ropecnt[0] % 2
        ropecnt[0] += 1
        t1, t1b = rt[i]
        p.op("pe", lambda e: e.matmul(pm0[0:32, :], lhsT=PMb[:], rhs=X32, start=True, stop=True),
             reads=[Xb, PMb_b], writes=[pm0_b])
        p.op("dve", lambda e: e.tensor_tensor(out=t1[:], in0=X32, in1=COS[:, tsl], op=ALU.mult), reads=[Xb, COS_b], writes=[t1b])
        p.op("dve", lambda e: e.tensor_tensor(out=X32, in0=pm0[0:32, :], in1=SIN[:, tsl], op=ALU.mult),
             reads=[pm0_b, SIN_b, Xb], writes=[Xb])
        p.op("dve", lambda e: e.tensor_tensor(out=X32, in0=X32, in1=t1[:], op=ALU.add), reads=[Xb, t1b], writes=[Xb])

    for k in range(8):
        tsl = slice(k * 512, (k + 1) * 512)
        rope(KS[0:32, tsl], KS_b, tsl)
        rope(KW[0:32, tsl], KW_b, tsl)

    for w, (SRC, SRC_b) in enumerate(((KC, KC_b), (VC, VC_b))):
        W1t, W1b = W1[w]
        W2t, W2b = W2[w]
        PCt, PCb = POSC[w]

        def mb(e, W1t=W1t, PCt=PCt):
            r = None
            for l in range(32):
                r = e.matmul(pm0[:, 0:1], lhsT=W1t[:, l * 128:(l + 1) * 128], rhs=PCt[:, l:l + 1], start=(l == 0), stop=(l == 31))
            return r
        p.op("pe", mb, reads=[W1b, PCb], writes=[pm0_b])
        p.op("dve", lambda e: e.tensor_copy(out=cb[:], in_=pm0[:, 0:1]), reads=[pm0_b], writes=[cb_b])

        def mh(e, W1t=W1t, SRC=SRC):
            r = None
            for l in range(32):
                r = e.matmul(pm0[:, 0:255], lhsT=W1t[:, l * 128:(l + 1) * 128], rhs=SRC[:, l:l + 16 * 254 + 1:16],
                             start=(l == 0), stop=(l == 31))
            return r
        p.op("pe", mh, reads=[W1b, SRC_b], writes=[pm0_b])
        u, ub = gt[0]
        v_, vb = gt[1]
        w_, wb_ = gt[2]
        p.op("act", lambda e: e.activation(out=u[:, 0:255], in_=pm0[:, 0:255], func=AF.Identity, bias=cb[:, 0:1]),
             reads=[pm0_b, cb_b], writes=[ub])
        p.op("dve", lambda e: e.tensor_tensor(out=v_[:, 0:255], in0=u[:, 0:255], in1=u[:, 0:255], op=ALU.mult), reads=[ub], writes=[vb])
        p.op("dve", lambda e: e.tensor_scalar(out=v_[:, 0:255], in0=v_[:, 0:255], scalar1=0.044715, scalar2=1.0, op0=ALU.mult, op1=ALU.add),
             reads=[vb], writes=[vb])
        p.op("dve", lambda e: e.tensor_tensor(out=v_[:, 0:255], in0=v_[:, 0:255], in1=u[:, 0:255], op=ALU.mult), reads=[vb, ub], writes=[vb])
        p.op("act", lambda e: e.activation(out=w_[:, 0:255], in_=v_[:, 0:255], func=AF.Tanh, scale=0.7978845608028654),
             reads=[vb], writes=[wb_])
        p.op("dve", lambda e: e.tensor_scalar(out=w_[:, 0:255], in0=w_[:, 0:255], scalar1=0.5, scalar2=0.5, op0=ALU.mult, op1=ALU.add),
             reads=[wb_], writes=[wb_])
        p.op("pool", lambda e: e.memset(HID[:], 0.0), reads=[HID_b], writes=[HID_b])
        p.op("dve", lambda e: e.tensor_tensor(out=HID[:, 0:255], in0=w_[:, 0:255], in1=u[:, 0:255], op=ALU.mult),
             reads=[wb_, ub, HID_b], writes=[HID_b])
        if w == 0:
            p.op("pe", lambda e, W2t=W2t: e.matmul(pm0[:, 0:256], lhsT=W2t[:], rhs=HID[:], start=True, stop=True),
                 reads=[W2b, HID_b], writes=[pm0_b])
            p.op("act", lambda e: e.activation(out=KCT[:], in_=pm0[:, 0:256], func=AF.Copy), reads=[pm0_b], writes=[KCT_b])
        else:
            for nt in range(2):
                p.op("pe", lambda e, W2t=W2t, nt=nt: e.matmul(pm0[:, 0:128], lhsT=HID[:, nt * 128:(nt + 1) * 128], rhs=W2t[:],
                                                             start=True, stop=True), reads=[W2b, HID_b], writes=[pm0_b])
                p.op("act", lambda e, nt=nt: e.activation(out=VC1[:, nt, 0:128], in_=pm0[:, 0:128], func=AF.Copy),
                     reads=[pm0_b], writes=[VC1_b])

    cnt = {"s": 0, "pt": 0, "rr": 0}

    def branch(k, j, br, tiles, nv, first_branch):
        tsl = slice(k * 512, (k + 1) * 512)
        first = {}
        last = {}
        for ti, tl in enumerate(tiles):
            for ts in tl[3]:
                first.setdefault(ts, ti)
                last[ts] = ti
        staged = {}

        def emit_S(ti):
            Kap, masks, V1ap, ts_list, rbufs = tiles[ti]
            sp_, spb = psS[cnt["s"] % 2]
            cnt["s"] += 1
            pt, ptb = PT[cnt["pt"] % 3]
            cnt["pt"] += 1

            def ms(e, sp_=sp_, Kap=Kap, masks=masks):
                r = e.matmul(sp_[:], lhsT=Kap, rhs=QT[:, j, tsl], start=True, stop=(len(masks) == 0))
                for mi, (ml, mr) in enumerate(masks):
                    r = e.matmul(sp_[:], lhsT=ml, rhs=mr, start=False, stop=(mi == len(masks) - 1))
                return r
            p.op("pe", ms, reads=[QT_b] + rbufs, writes=[spb])
            p.op("act", lambda e, pt=pt, sp_=sp_: e.activation(out=pt[:], in_=sp_[:], func=AF.Exp, scale=SCALE),
                 reads=[spb], writes=[ptb])
            staged[ti] = (pt, ptb)

        def emit_PV(ti):
            Kap, masks, V1ap, ts_list, rbufs = tiles[ti]
            pt, ptb = staged.pop(ti)

            def mv(e, pt=pt, V1ap=V1ap, ts_list=ts_list, ti=ti):
                r = None
                for ts in ts_list:
                    r = e.matmul(pv[ts][0][:, 0:nv], lhsT=pt[:, ts * 128:(ts + 1) * 128], rhs=V1ap,
                                 start=(first[ts] == ti), stop=(last[ts] == ti))
                return r
            p.op("pe", mv, reads=[ptb] + rbufs, writes=[pv[ts][1] for ts in ts_list])

        emit_S(0)
        for ti in range(len(tiles)):
            if ti + 1 < len(tiles):
                emit_S(ti + 1)
            emit_PV(ti)
        for ts in range(4):
            if ts not in first:
                continue
            tt = 4 * k + ts
            r_, rb = rr[cnt["rr"] % 4]
            cnt["rr"] += 1
            pvt, pvb = pv[ts]
            p.op("dve", lambda e, r_=r_, pvt=pvt: e.tensor_scalar(out=r_[:, 0:1], in0=pvt[:, 128:129], scalar1=1e-30, scalar2=None, op0=ALU.add),
                 reads=[pvb], writes=[rb])
            p.op("dve", lambda e, r_=r_: e.reciprocal(out=r_[:, 0:1], in_=r_[:, 0:1]), reads=[rb], writes=[rb])
            p.op("dve", lambda e, r_=r_, tt=tt: e.tensor_tensor(out=r_[:, 1:2], in0=r_[:, 0:1], in1=NG[:, tt, br * 4 + j:br * 4 + j + 1], op=ALU.mult),
                 reads=[rb, NG_b], writes=[rb])
            osl = OACC[:, ts, j * 128:(j + 1) * 128]
            if first_branch:
                p.op("dve", lambda e, osl=osl, pvt=pvt, r_=r_: e.tensor_scalar(out=osl, in0=pvt[:, 0:128], scalar1=r_[:, 1:2], scalar2=None, op0=ALU.mult),
                     reads=[pvb, rb, OACC_b], writes=[OACC_b])
            else:
                p.op("dve", lambda e, osl=osl, pvt=pvt, r_=r_: e.scalar_tensor_tensor(out=osl, in0=pvt[:, 0:128], scalar=r_[:, 1:2], in1=osl,
                                                                                   op0=ALU.mult, op1=ALU.add),
                     reads=[pvb, rb, OACC_b], writes=[OACC_b])
            if br == 0:
                if j == 0:
                    p.op("dve", lambda e, ts=ts, pvt=pvt, r_=r_: e.tensor_scalar(out=PSLC[:, ts, :], in0=pvt[:, 129:193], scalar1=r_[:, 0:1], scalar2=None, op0=ALU.mult),
                         reads=[pvb, rb, PSLC_b], writes=[PSLC_b])
                else:
                    p.op("dve", lambda e, ts=ts, pvt=pvt, r_=r_: e.scalar_tensor_tensor(out=PSLC[:, ts, :], in0=pvt[:, 129:193], scalar=r_[:, 0:1], in1=PSLC[:, ts, :],
                                                                                      op0=ALU.mult, op1=ALU.add),
                         reads=[pvb, rb, PSLC_b], writes=[PSLC_b])

    onsa_v = onsa.rearrange("(tt p) c -> p tt c", p=128)
    for k in range(8):
        tsl = slice(k * 512, (k + 1) * 512)
        for j in range(4):
            tiles = []
            for nt in range(2 if k >= 4 else 1):
                partial = (nt == 0 and k <= 4) or (nt == 1)
                masks = [(IDB[:], CM[:, nt * 4096 + k * 512:nt * 4096 + (k + 1) * 512])] if partial else []
                tiles.append((KCT[:, nt * 128:(nt + 1) * 128], masks, VC1[:, nt, :], [0, 1, 2, 3], [KCT_b, VC1_b, IDB_b, CM_b]))
            branch(k, j, 0, tiles, 193, True)
        for ts in range(4):
            tt = 4 * k + ts
            p.op("dve", lambda e, ts=ts, tt=tt: e.tensor_tensor(out=sc1[:], in0=PSLC[:, ts, :], in1=SELT[:, tt * 64:(tt + 1) * 64], op=ALU.add),
                 reads=[PSLC_b, SELT_b], writes=[sc1_b])
            p.op("dve", lambda e: e.max(out=m8[:], in_=sc1[:]), reads=[sc1_b], writes=[m8_b])
            p.op("dve", lambda e: e.match_replace(out=sc2[:], in_to_replace=m8[:], in_values=sc1[:], imm_value=-3.0e38),
                 reads=[sc1_b, m8_b], writes=[sc2_b])
            p.op("dve", lambda e: e.max(out=m8[:], in_=sc2[:]), reads=[sc2_b], writes=[m8_b])
            p.op("dve", lambda e: e.tensor_scalar(out=sc2[:], in0=sc1[:], scalar1=m8[:, 7:8], scalar2=None, op0=ALU.is_ge),
                 reads=[sc1_b, m8_b], writes=[sc2_b])
            p.op("dve", lambda e: e.tensor_scalar(out=nmk[:], in0=sc2[:], scalar1=-NEG, scalar2=NEG, op0=ALU.mult, op1=ALU.add),
                 reads=[sc2_b], writes=[nmk_b])
            p.op("pe", lambda e: e.transpose(out=pm1[0:64, 0:128], in_=nmk[:], identity=IDB[:]), reads=[nmk_b, IDB_b], writes=[pm1_b])
            p.op("act", lambda e, ts=ts: e.activation(out=NMT[:, ts * 128:(ts + 1) * 128], in_=pm1[0:64, 0:128], func=AF.Copy),
                 reads=[pm1_b], writes=[NMT_b])
        for j in range(4):
            rope(QT[0:32, j, tsl], QT_b, tsl)
        for j in range(4):
            tiles = []
            for kt in range(4 * k + 4):
                a = kt - 4 * k
                masks = [(ET[:, kt * 128:(kt + 1) * 128], NMT[:])]
                if a >= 0:
                    masks.append((IDB[:], DG[:, a * 512:(a + 1) * 512]))
                ts_list = [ts for ts in range(4) if a < 0 or ts >= a]
                tiles.append((KS[:, kt * 128:(kt + 1) * 128], masks, VS1[:, kt, :], ts_list, [KS_b, VS1_b, ET_b, NMT_b, IDB_b, DG_b]))
            branch(k, j, 1, tiles, 129, False)
        for j in range(4):
            tiles = []
            for c in range(8):
                kt = 4 * k - 4 + c
                if kt < 0:
                    continue
                masks = [(IDB[:], BD[:, c * 512:(c + 1) * 512])]
                ts_list = [ts for ts in range(4) if ts <= c <= ts + 4]
                tiles.append((KW[:, kt * 128:(kt + 1) * 128], masks, VW1[:, kt, :], ts_list, [KW_b, VW1_b, IDB_b, BD_b]))
            branch(k, j, 2, tiles, 129, False)
        p.dma(PF, onsa_v[:, 4 * k:4 * k + 4, :], OACC[:], OACC_b, reads=[OACC_b])
    p.finish("sp")
    p.emit()
    p.close()
    return nc


EPS = 1e-6
C = 64
NCH = 64


def build_A3(layer, ST=('pre', 'loop', 'post'), NH=4, NCL=NCH):
    nc = bass.Bass("TRN2", target_bir_lowering=False)
    dt = nc.dram_tensor
    hqT = dt("hqT", [4, 128, 4096], F32, kind="ExternalInput").ap()
    hfT = dt("hfT", [4, 128, 4096], F32, kind="ExternalInput").ap()
    hi = dt("hi", [4096, 512], F32, kind="ExternalInput").ap()
    hg = dt("hg", [4096, 512], F32, kind="ExternalInput").ap()
    lbl = dt("lbl", [128, 2, 4], F32, kind="ExternalInput").ap()
    nw64 = dt("nw64", [64, 128], F32, kind="ExternalInput").ap()
    rmask = dt("rmask", [128, 4096], F32, kind="ExternalInput").ap()
    tri = dt("tri", [64, 64], F32, kind="ExternalInput").ap()
    ident = dt("ident", [128, 128], F32, kind="ExternalInput").ap()
    ohg = dt("ohg", [4096, 512], F32, kind="ExternalOutput").ap()
    p = Prog(nc)
    A, A_b = p.sbuf("A", [128, 4096], F32)
    B, B_b = p.sbuf("B", [128, 4096], F32)
    Cc, C_b = p.sbuf("C", [128, 4096], F32)
    D, D_b = p.sbuf("D", [128, 4096], F32)
    HQ, HQ_b = p.sbuf("HQ", [128, 4096], BF16)
    Q1, Q1_b = p.sbuf("Q1", [128, 4096], BF16)
    K1, K1_b = p.sbuf("K1", [128, 4096], BF16)
    Q2, Q2_b = p.sbuf("Q2", [128, 4096], BF16)
    K2, K2_b = p.sbuf("K2", [128, 4096], BF16)
    V, V_b = p.sbuf("V", [64, NCH, 128], BF16)
    G, G_b = p.sbuf("G", [64, NCH, 128], BF16)
    O, O_b = p.sbuf("O", [64, NCH, 128], F32)
    RM, RM_b = p.sbuf("RM", [128, 4096], BF16)
    lb, lb_b = p.sbuf("lb", [128, 2, 4], F32)
    lbv, lbv_b = p.sbuf("lbv", [128, 4], F32)
    oml, oml_b = p.sbuf("oml", [128, 4], F32)
    nw, nw_b = p.sbuf("nw", [64, 128], F32)
    trif, trif_b = p.sbuf("trif", [64, 64], F32)
    triu, triu_b = p.sbuf("triu", [64, 64], U8)
    idb, idb_b = p.sbuf("idb", [128, 128], BF16)
    EBL, EBL_b = p.sbuf("EBL", [128, NCH], F32)
    ssq, ssq_b = p.sbuf("ssq", [64, NCH], F32)
    epsb, epsb_b = p.sbuf("epsb", [64, 1], F32)
    Sf, Sf_b = p.sbuf("Sf", [128, 128], F32)
    Sbf = [p.sbuf("Sbf%d" % i, [128, 128], BF16) for i in range(2)]
    attS = [p.sbuf("attS%d" % i, [64, 64], BF16) for i in range(2)]
    khat = [p.sbuf("khat%d" % i, [64, 128], BF16) for i in range(2)]
    ps_att = [p.psum("att%d" % i, [128, 512], F32) for i in range(2)]
    ps_kh = [p.psum("kh%d" % i, [128, 1024], BF16) for i in range(2)]
    ps_o = [p.psum("o%d" % i, [128, 512], F32) for i in range(2)]
    ps_sn = [p.psum("sn%d" % i, [128, 512], F32) for i in range(2)]
    PI, PF = "pool", "sp"
    p.dma(PI, RM[:], rmask, RM_b, writes=[RM_b])
    p.dma(PI, idb[:], ident, idb_b, writes=[idb_b])
    p.dma(PF, trif[:], tri, trif_b, writes=[trif_b])
    p.dma(PF, lb[:], lbl, lb_b, writes=[lb_b])
    p.dma(PF, nw[:], nw64, nw_b, writes=[nw_b])
    p.op("dve", lambda e: e.tensor_copy(out=triu[:], in_=trif[:]), reads=[trif_b], writes=[triu_b])
    p.op("pool", lambda e: e.memset(epsb[:], EPS), writes=[epsb_b])
    for i in range(2):
        p.op("pool", lambda e, i=i: e.memset(attS[i][0][:], 0.0), writes=[attS[i][1]])
    p.op("dve", lambda e: e.tensor_tensor(out=lbv[:], in0=lb[:, 1, :], in1=lb[:, 0, :], op=ALU.subtract), reads=[lb_b], writes=[lbv_b])
    p.op("act", lambda e: e.activation(out=lbv[:], in_=lbv[:], func=AF.Sigmoid), reads=[lbv_b], writes=[lbv_b])
    p.op("dve", lambda e: e.tensor_scalar(out=lbv[:], in0=lbv[:], scalar1=float(layer), scalar2=None, op0=ALU.mult),
         reads=[lbv_b], writes=[lbv_b])
    p.op("dve", lambda e: e.tensor_scalar(out=oml[:], in0=lbv[:], scalar1=-1.0, scalar2=1.0, op0=ALU.mult, op1=ALU.add),
         reads=[lbv_b], writes=[oml_b])

    def v3(t):
        return t[:].rearrange("p (c j) -> p c j", j=C)

    hi_v = hi.rearrange("(c p) e -> p c e", p=C)
    hg_v = hg.rearrange("(c p) e -> p c e", p=C)
    ohg_v = ohg.rearrange("(c p) e -> p c e", p=C)
    for hh in range(NH):
        hs = slice(hh * 128, (hh + 1) * 128)
        p.dma(PF, A[:], hfT[hh], A_b, writes=[A_b])
        p.dma(PI, HQ[:], hqT[hh], HQ_b, writes=[HQ_b])
        p.dma(PI, V[:], hi_v[:, :, hs], V_b, writes=[V_b])
        p.dma(PI, G[:], hg_v[:, :, hs], G_b, writes=[G_b])
        if 'pre' in ST:
            p.op("act", lambda e: e.activation(out=A[:], in_=A[:], func=AF.Sigmoid), reads=[A_b], writes=[A_b])
            p.op("dve", lambda e, hh=hh: e.tensor_scalar(out=A[:], in0=A[:], scalar1=oml[:, hh:hh + 1], scalar2=lbv[:, hh:hh + 1],
                                                        op0=ALU.mult, op1=ALU.add), reads=[A_b, oml_b, lbv_b], writes=[A_b])
            p.op("act", lambda e: e.activation(out=B[:], in_=A[:], func=AF.Ln), reads=[A_b], writes=[B_b])
            p.op("dve", lambda e: e.tensor_scalar(out=A[:], in0=A[:], scalar1=-1.0, scalar2=1.0, op0=ALU.mult, op1=ALU.add),
                 reads=[A_b], writes=[A_b])
            p.op("dve", lambda e: e.tensor_tensor_scan(out=Cc[:], data0=RM[:], data1=B[:], initial=0.0, op0=ALU.mult, op1=ALU.add),
                 reads=[RM_b, B_b], writes=[C_b])
            p.op("dve", lambda e: e.tensor_tensor(out=v3(B), in0=v3(Cc), in1=v3(Cc)[:, :, 31:32].to_broadcast([128, NCH, C]),
                                                  op=ALU.subtract), reads=[C_b], writes=[B_b])
            p.op("act", lambda e: e.activation(out=D[:], in_=B[:], func=AF.Exp), reads=[B_b], writes=[D_b])
            p.op("dve", lambda e: e.tensor_tensor(out=Q1[:], in0=HQ[:], in1=D[:], op=ALU.mult), reads=[HQ_b, D_b], writes=[Q1_b])
            p.op("act", lambda e: e.activation(out=D[:], in_=B[:], func=AF.Exp, scale=-1.0), reads=[B_b], writes=[D_b])
            p.op("dve", lambda e: e.tensor_tensor(out=K1[:], in0=A[:], in1=D[:], op=ALU.mult), reads=[A_b, D_b], writes=[K1_b])
            p.op("act", lambda e: e.activation(out=D[:], in_=Cc[:], func=AF.Exp), reads=[C_b], writes=[D_b])
            p.op("dve", lambda e: e.tensor_tensor(out=Q2[:], in0=HQ[:], in1=D[:], op=ALU.mult), reads=[HQ_b, D_b], writes=[Q2_b])
            p.op("dve", lambda e: e.tensor_tensor(out=v3(B), in0=v3(Cc)[:, :, 63:64].to_broadcast([128, NCH, C]), in1=v3(Cc),
                                                  op=ALU.subtract), reads=[C_b], writes=[B_b])
            p.op("act", lambda e: e.activation(out=D[:], in_=B[:], func=AF.Exp), reads=[B_b], writes=[D_b])
            p.op("dve", lambda e: e.tensor_tensor(out=K2[:], in0=A[:], in1=D[:], op=ALU.mult), reads=[A_b, D_b], writes=[K2_b])
            p.op("act", lambda e: e.activation(out=EBL[:], in_=v3(Cc)[:, :, 63], func=AF.Exp), reads=[C_b], writes=[EBL_b])
        p.op("pool", lambda e: e.memset(Sf[:], 0.0), writes=[Sf_b])
        p.op("pool", lambda e: e.memset(Sbf[0][0][:], 0.0), writes=[Sbf[0][1]])
        for c in range(NCL if 'loop' in ST else 0):
            cs = slice(c * C, (c + 1) * C)
            i = c % 2
            pa, pab = ps_att[i]
            pk, pkb = ps_kh[i]
            po, pob = ps_o[i]
            pn, pnb = ps_sn[i]
            at, atb = attS[i]
            kh, khb = khat[i]
            sb_cur, sb_curb = Sbf[c % 2]
            sb_nxt, sb_nxtb = Sbf[(c + 1) % 2]
            p.op("pe", lambda e, pa=pa, cs=cs: e.matmul(pa[0:C, 0:C], lhsT=K1[:, cs], rhs=Q1[:, cs], start=True, stop=True),
                 reads=[K1_b, Q1_b], writes=[pab])
            p.op("dve", lambda e, at=at, pa=pa: e.copy_predicated(out=at[:], mask=triu[:], data=pa[0:C, 0:C]),
                 reads=[pab, triu_b, atb], writes=[atb])
            p.op("pe", lambda e, pk=pk, cs=cs: e.transpose(out=pk[0:C, 0:128], in_=K2[:, cs], identity=idb[:]),
                 reads=[K2_b, idb_b], writes=[pkb])
            p.op("act", lambda e, kh=kh, pk=pk: e.activation(out=kh[:], in_=pk[0:C, 0:128], func=AF.Copy),
                 reads=[pkb], writes=[khb])

            def mo(e, po=po, at=at, c=c, cs=cs, sb_cur=sb_cur):
                e.matmul(po[0:C, 0:128], lhsT=at[:], rhs=V[:, c, :], start=True, stop=False)
                return e.matmul(po[0:C, 0:128], lhsT=Q2[:, cs], rhs=sb_cur[:], start=False, stop=True)
            p.op("pe", mo, reads=[atb, V_b, Q2_b, sb_curb], writes=[pob])
            p.op("act", lambda e, po=po, c=c: e.activation(out=O[:, c, :], in_=po[0:C, 0:128], func=AF.Copy),
                 reads=[pob], writes=[O_b])
            p.op("pe", lambda e, pn=pn, kh=kh, c=c: e.matmul(pn[:, 0:128], lhsT=kh[:], rhs=V[:, c, :], start=True, stop=True),
                 reads=[khb, V_b], writes=[pnb])
            p.op("dve", lambda e, pn=pn, c=c: e.scalar_tensor_tensor(out=Sf[:], in0=Sf[:], scalar=EBL[:, c:c + 1], in1=pn[:, 0:128],
                                                                    op0=ALU.mult, op1=ALU.add),
                 reads=[Sf_b, EBL_b, pnb], writes=[Sf_b])
            p.op("act", lambda e, sb_nxt=sb_nxt: e.activation(out=sb_nxt[:], in_=Sf[:], func=AF.Copy),
                 reads=[Sf_b], writes=[sb_nxtb])
        if 'post' in ST:
            A64 = A[0:64, :].rearrange("p (c e) -> p c e", e=128)
            B64 = B[0:64, :].rearrange("p (c e) -> p c e", e=128)
            p.op("dve", lambda e: e.tensor_tensor(out=A64, in0=O[:, 0:32, :], in1=O[:, 0:32, :], op=ALU.mult), reads=[O_b], writes=[A_b])
            p.op("dve", lambda e: e.tensor_tensor(out=B64, in0=O[:, 32:64, :], in1=O[:, 32:64, :], op=ALU.mult), reads=[O_b], writes=[B_b])
            p.op("dve", lambda e: e.tensor_reduce(out=ssq[:, 0:32], in_=A64, axis=AX.X, op=ALU.add), reads=[A_b], writes=[ssq_b])
            p.op("dve", lambda e: e.tensor_reduce(out=ssq[:, 32:64], in_=B64, axis=AX.X, op=ALU.add), reads=[B_b, ssq_b], writes=[ssq_b])
            p.op("act", lambda e: e.activation(out=ssq[:], in_=ssq[:], func=AF.Sqrt, scale=1.0 / 128, bias=epsb[:, 0:1]),
                 reads=[ssq_b, epsb_b], writes=[ssq_b])
            p.op("dve", lambda e: e.reciprocal(out=ssq[:], in_=ssq[:]), reads=[ssq_b], writes=[ssq_b])
            p.op("dve", lambda e: e.tensor_tensor(out=O[:], in0=O[:], in1=ssq[:].unsqueeze(2).to_broadcast([64, NCH, 128]), op=ALU.mult),
                 reads=[O_b, ssq_b], writes=[O_b])
            p.op("dve", lambda e: e.tensor_tensor(out=O[:], in0=O[:], in1=nw[:].unsqueeze(1).to_broadcast([64, NCH, 128]), op=ALU.mult),
                 reads=[O_b, nw_b], writes=[O_b])
            p.op("act", lambda e: e.activation(out=A64, in_=G[:, 0:32, :], func=AF.Silu), reads=[G_b, A_b], writes=[A_b])
            p.op("act", lambda e: e.activation(out=B64, in_=G[:, 32:64, :], func=AF.Silu), reads=[G_b, B_b], writes=[B_b])
            p.op("dve", lambda e: e.tensor_tensor(out=O[:, 0:32, :], in0=O[:, 0:32, :], in1=A64, op=ALU.mult), reads=[O_b, A_b], writes=[O_b])
            p.op("dve", lambda e: e.tensor_tensor(out=O[:, 32:64, :], in0=O[:, 32:64, :], in1=B64, op=ALU.mult), reads=[O_b, B_b], writes=[O_b])
        p.dma(PF, ohg_v[:, :, hs], O[:], O_b, reads=[O_b])
    p.finish("sp")
    p.emit()
    p.close()
    return nc


T = 1024
EPS = 1e-6


def build_B(last):
    nc = bass.Bass("TRN2", target_bir_lowering=False)
    dt = nc.dram_tensor
    xT = dt("xT", [32, 128, T], F32, kind="ExternalInput").ap()
    vecs = dt("vecs", [128, 9, 32], F32, kind="ExternalInput").ap()
    wga = dt("wga", [32, 128, 32 * 128], F32, kind="ExternalInput").ap()
    wgb = dt("wgb", [32, 128, 32 * 128], F32, kind="ExternalInput").ap()
    onT = dt("onT", [16, 128, T], F32, kind="ExternalInput").ap()
    ohT = dt("ohT", [16, 128, T], F32, kind="ExternalInput").ap()
    wua = dt("wua", [32, 128, 16 * 128], F32, kind="ExternalInput").ap()
    wub = dt("wub", [32, 128, 16 * 128], F32, kind="ExternalInput").ap()
    wo = dt("wo", [32, 128, 32 * 128], F32, kind="ExternalInput").ap()
    w1 = dt("w1", [128, 128, 32 * 128], F32, kind="ExternalInput").ap()
    w2 = dt("w2", [8, 128, 128 * 512], F32, kind="ExternalInput").ap()
    ones_d = dt("ones", [128, 128], F32, kind="ExternalInput").ap()
    outT = dt("outT", [32, 128, T], F32, kind="ExternalOutput").ap()
    yT_s = dt("yT_s", [32, 128, T], BF16, kind="Internal").ap()
    x1T_s = dt("x1T_s", [32, 128, T], F32, kind="Internal").ap()
    aT_s = dt("aT_s", [128, 128, T], BF16, kind="Internal").ap()
    x2T_s = dt("x2T_s", [32, 128, T], F32, kind="Internal").ap() if last else None

    p = Prog(nc)
    ones, ones_b = p.sbuf("ones", [128, 128], BF16)
    vc, vc_b = p.sbuf("vc", [128, 9, 32], F32)
    A1, A1_b = p.sbuf("A1", [128, 32], F32)
    A2, A2_b = p.sbuf("A2", [128, 32], F32)
    rstd, rstd_b = p.sbuf("rstd", [128, T], F32)
    big0, big0_b = p.sbuf("big0", [128, 32, T], BF16)
    big1, big1_b = p.sbuf("big1", [128, 32, T], BF16)
    NXB = 3
    xb = [p.sbuf("xb%d" % i, [128, T], F32) for i in range(NXB)]
    tf = [p.sbuf("tf%d" % i, [128, T], F32) for i in range(2)]
    tb = [p.sbuf("tb%d" % i, [128, T], BF16) for i in range(2)]
    sg = [(tf[i % 2][0][:, (i // 2) * 512:(i // 2 + 1) * 512], tf[i % 2][1]) for i in range(4)]
    wA = [p.sbuf("wA%d" % i, [128, 4096], BF16) for i in range(2)]
    wB = [p.sbuf("wB%d" % i, [128, 4096], BF16) for i in range(2)]
    wC = [p.sbuf("wC%d" % i, [128, 2048], BF16) for i in range(2)]
    wD = [p.sbuf("wD%d" % i, [128, 2048], BF16) for i in range(2)]
    ab = [(tf[i][0][:].bitcast(BF16), tf[i][1]) for i in range(2)]
    ps = [p.psum("ps%d" % i, [128, 512], F32) for i in range(8)]

    vbufs = {}
    PI, PF = "pool", "sp"

    p.dma(PI, ones[:], ones_d, ones_b, writes=[ones_b])
    p.dma(PF, vc[:], vecs, vc_b, writes=[vc_b])
    p.op("dve", lambda e: e.scalar_tensor_tensor(out=A1[:], in0=vc[:, 1, :], scalar=1.0, in1=vc[:, 6, :],
                                                 op0=ALU.add, op1=ALU.mult), reads=[vc_b], writes=[A1_b])
    p.op("dve", lambda e: e.scalar_tensor_tensor(out=A2[:], in0=vc[:, 4, :], scalar=1.0, in1=vc[:, 7, :],
                                                 op0=ALU.add, op1=ALU.mult), reads=[vc_b], writes=[A2_b])

    xcnt = [0]

    def load_x(src_cg_ap, src_buf=None):
        i = xcnt[0] % NXB
        xcnt[0] += 1
        t, b = xb[i]
        p.dma(PF, t[:], src_cg_ap, b, reads=[src_buf] if src_buf else [], writes=[b])
        return t, b

    def stats_pass(src, src_bufs, ssA, ssB, keep=None, keep_b=None):
        for cg in range(32):
            t, b = load_x(src[cg], src_bufs[cg] if src_bufs else None)
            sq, sqb = tb[cg % 2]
            p.op("act", lambda e, t=t, sq=sq: e.activation(out=sq[:], in_=t[:], func=AF.Square),
                 reads=[b], writes=[sqb])
            if keep is not None:
                p.op("dve", lambda e, t=t, cg=cg: e.tensor_copy(out=keep[:, cg, :], in_=t[:]), reads=[b, keep_b], writes=[keep_b])

            def mm(e, sq=sq, cg=cg):
                e.matmul(ssA[0][:], lhsT=ones[:], rhs=sq[:, 0:512], start=(cg == 0), stop=(cg == 31))
                return e.matmul(ssB[0][:], lhsT=ones[:], rhs=sq[:, 512:1024], start=(cg == 0), stop=(cg == 31))
            p.op("pe", mm, reads=[sqb, ones_b], writes=[ssA[1], ssB[1]])

    def make_rstd(ssA, ssB):
        for h, ss in enumerate((ssA, ssB)):
            sl = slice(h * 512, (h + 1) * 512)
            p.op("act", lambda e, ss=ss, sl=sl: e.activation(out=rstd[:, sl], in_=ss[0][:], func=AF.Sqrt,
                                                            scale=1.0 / 4096, bias=EPSB[:, 0:1]),
                 reads=[ss[1], epsb_b], writes=[rstd_b])
        p.op("dve", lambda e: e.reciprocal(out=rstd[:], in_=rstd[:]), reads=[rstd_b], writes=[rstd_b])

    EPSB, epsb_b = p.sbuf("epsb", [128, 1], F32)
    p.op("pool", lambda e: e.memset(EPSB[:], EPS), writes=[epsb_b])

    def norm_apply(src, src_bufs, A, Bv, dst, dst_b):
        for cg in range(32):
            u, ub = tf[cg % 2]
            p.op("dve", lambda e, u=u, cg=cg: e.tensor_tensor(out=u[:], in0=dst[:, cg, :], in1=rstd[:], op=ALU.mult),
                 reads=[dst_b, rstd_b], writes=[ub])
            p.op("act", lambda e, u=u, cg=cg: e.activation(out=dst[:, cg, :], in_=u[:], func=AF.Identity,
                                                          scale=A[:, cg:cg + 1], bias=vc[:, Bv, cg:cg + 1]),
                 reads=[ub, A1_b, A2_b, vc_b], writes=[dst_b])

    stats_pass(xT, None, ps[0], ps[1], keep=big0, keep_b=big0_b)
    make_rstd(ps[0], ps[1])
    norm_apply(xT, None, A1, 0, big0, big0_b)
    p.dma(PI, big1[:, 0:16, :], onT.rearrange("k p t -> p k t"), big1_b, writes=[big1_b])
    p.dma(PI, big1[:, 16:32, :], ohT.rearrange("k p t -> p k t"), big1_b, writes=[big1_b])
    yb = [p.buf("yT%d" % cg) for cg in range(32)]
    for cg in range(32):
        (w_a, w_ab), (w_b, w_bb), (w_c, w_cb), (w_d, w_db) = wA[cg % 2], wB[cg % 2], wC[cg % 2], wD[cg % 2]
        p.dma(PI, w_a[:], wga[cg], w_ab, writes=[w_ab])
        p.dma(PI, w_b[:], wgb[cg], w_bb, writes=[w_bb])
        p.dma(PI, w_c[:], wua[cg], w_cb, writes=[w_cb])
        p.dma(PI, w_d[:], wub[cg], w_db, writes=[w_db])
        yt, ytb = tb[cg % 2]
        for tt in range(2):
            sl = slice(tt * 512, (tt + 1) * 512)
            pset = ps[4 * tt:4 * tt + 4]

            def mm(e, w_a=w_a, w_b=w_b, w_c=w_c, w_d=w_d, sl=sl, pset=pset):
                for kt in range(32):
                    e.matmul(pset[0][0][:], lhsT=w_a[:, kt * 128:(kt + 1) * 128], rhs=big0[:, kt, sl],
                             start=(kt == 0), stop=(kt == 31))
                for kt in range(32):
                    e.matmul(pset[1][0][:], lhsT=w_b[:, kt * 128:(kt + 1) * 128], rhs=big0[:, kt, sl],
                             start=(kt == 0), stop=(kt == 31))
                for k in range(16):
                    e.matmul(pset[2][0][:], lhsT=w_c[:, k * 128:(k + 1) * 128], rhs=big1[:, k, sl],
                             start=(k == 0), stop=(k == 15))
                r = None
                for k in range(16):
                    r = e.matmul(pset[3][0][:], lhsT=w_d[:, k * 128:(k + 1) * 128],
                                 rhs=big1[:, 16 + k, sl], start=(k == 0), stop=(k == 15))
                return r
            p.op("pe", mm, reads=[w_ab, w_bb, w_cb, w_db, big0_b, big1_b], writes=[q[1] for q in pset])
            sga, sgab = sg[2 * tt]
            sgb, sgbb = sg[2 * tt + 1]
            p.op("act", lambda e, sga=sga, pset=pset: e.activation(out=sga[:], in_=pset[0][0][:], func=AF.Sigmoid),
                 reads=[pset[0][1]], writes=[sgab])
            p.op("act", lambda e, sgb=sgb, pset=pset: e.activation(out=sgb[:], in_=pset[1][0][:], func=AF.Sigmoid),
                 reads=[pset[1][1]], writes=[sgbb])
            p.op("dve", lambda e, sga=sga, pset=pset: e.tensor_tensor(out=sga[:], in0=sga[:], in1=pset[2][0][:], op=ALU.mult),
                 reads=[sgab, pset[2][1]], writes=[sgab])
            p.op("dve", lambda e, sgb=sgb, pset=pset: e.tensor_tensor(out=sgb[:], in0=sgb[:], in1=pset[3][0][:], op=ALU.mult),
                 reads=[sgbb, pset[3][1]], writes=[sgbb])
            p.op("dve", lambda e, sga=sga, sgb=sgb, yt=yt, sl=sl: e.tensor_tensor(out=yt[:, sl], in0=sga[:], in1=sgb[:], op=ALU.add),
                 reads=[sgab, sgbb], writes=[ytb])
        p.dma(PF, yT_s[cg], yt[:], ytb, reads=[ytb], writes=[yb[cg]])
    p.dma(PF, big0[:], yT_s.rearrange("k p t -> p k t"), big0_b, reads=yb, writes=[big0_b])
    x1b = [p.buf("x1T%d" % cg) for cg in range(32)]
    pend_stats = []
    for cg in range(32):
        wt, wtb = (wA + wB)[cg % 4]
        p.dma(PI, wt[:, 0:4096], wo[cg], wtb, writes=[wtb])
        pset = ps[2 * (cg % 2):2 * (cg % 2) + 2]

        def mm(e, wt=wt, pset=pset):
            r = None
            for tt in range(2):
                for kt in range(32):
                    r = e.matmul(pset[tt][0][:], lhsT=wt[:, kt * 128:(kt + 1) * 128],
                                 rhs=big0[:, kt, tt * 512:(tt + 1) * 512], start=(kt == 0), stop=(kt == 31))
            return r
        p.op("pe", mm, reads=[wtb, big0_b], writes=[pset[0][1], pset[1][1]])
        for mm2_, sqb_ in pend_stats:
            p.op("pe", mm2_, reads=[sqb_, ones_b], writes=[ps[4][1], ps[5][1]])
        pend_stats = []
        t, b = load_x(xT[cg])
        for tt in range(2):
            sl = slice(tt * 512, (tt + 1) * 512)
            p.op("dve", lambda e, t=t, sl=sl, pset=pset, tt=tt, cg=cg: e.scalar_tensor_tensor(
                out=t[:, sl], in0=pset[tt][0][:], scalar=vc[:, 2, cg:cg + 1], in1=t[:, sl], op0=ALU.mult, op1=ALU.add),
                reads=[pset[tt][1], b, vc_b], writes=[b])
        p.dma(PF, x1T_s[cg], t[:], b, reads=[b], writes=[x1b[cg]])
        p.op("dve", lambda e, t=t, cg=cg: e.tensor_copy(out=big1[:, cg, :], in_=t[:]), reads=[b, big1_b], writes=[big1_b])
        sq, sqb = tb[cg % 2]
        p.op("act", lambda e, t=t, sq=sq: e.activation(out=sq[:], in_=t[:], func=AF.Square), reads=[b], writes=[sqb])

        def mm2(e, sq=sq, cg=cg):
            e.matmul(ps[4][0][:], lhsT=ones[:], rhs=sq[:, 0:512], start=(cg == 0), stop=(cg == 31))
            return e.matmul(ps[5][0][:], lhsT=ones[:], rhs=sq[:, 512:1024], start=(cg == 0), stop=(cg == 31))
        pend_stats.append((mm2, sqb))
    for mm2, sqb in pend_stats:
        p.op("pe", mm2, reads=[sqb, ones_b], writes=[ps[4][1], ps[5][1]])
    pend_stats = []
    make_rstd(ps[4], ps[5])
    norm_apply(x1T_s, x1b, A2, 3, big1, big1_b)
    ab_b = [p.buf("aT%d" % f) for f in range(128)]
    for fg in range(128):
        wt, wtb = (wA + wB)[fg % 4]
        p.dma(PI, wt[:, 0:4096], w1[fg], wtb, writes=[wtb])
        pset = ps[2 * (fg % 2):2 * (fg % 2) + 2]

        def mm(e, wt=wt, pset=pset):
            r = None
            for tt in range(2):
                for kt in range(32):
                    r = e.matmul(pset[tt][0][:], lhsT=wt[:, kt * 128:(kt + 1) * 128],
                                 rhs=big1[:, kt, tt * 512:(tt + 1) * 512], start=(kt == 0), stop=(kt == 31))
            return r
        p.op("pe", mm, reads=[wtb, big1_b], writes=[pset[0][1], pset[1][1]])
        r_, rb = tf[fg % 2]
        a_, a_b = tb[fg % 2]
        for tt in range(2):
            sl = slice(tt * 512, (tt + 1) * 512)
            p.op("act", lambda e, r_=r_, sl=sl, pset=pset, tt=tt: e.activation(out=r_[:, sl], in_=pset[tt][0][:], func=AF.Relu),
                 reads=[pset[tt][1]], writes=[rb])
            p.op("dve", lambda e, r_=r_, a_=a_, sl=sl, pset=pset, tt=tt: e.tensor_tensor(
                out=a_[:, sl], in0=r_[:, sl], in1=pset[tt][0][:], op=ALU.mult),
                reads=[rb, pset[tt][1]], writes=[a_b])
        p.dma(PF, aT_s[:, fg, :], a_[:], a_b, reads=[a_b], writes=[ab_b[fg]])
    dst = x2T_s if last else outT
    x2b = [p.buf("x2T%d" % cg) for cg in range(32)]
    FB = 8
    NSL = 8
    aslot = []
    for i in range(NSL):
        sb_ = p.buf("aslot%d" % i)
        sb_.w = big0_b.w
        sb_.r = list(big0_b.r)
        aslot.append((big0[:, 4 * i:4 * i + 4, :], sb_))
    acnt = 0
    tmpF = big1[:].rearrange("p k t -> p (k t)").bitcast(F32)
    tmps = []
    for i in range(8):
        tb_ = p.buf("mtmp%d" % i)
        tb_.w = big1_b.w
        tb_.r = list(big1_b.r)
        tmps.append((tmpF[:, i * 512:(i + 1) * 512], tb_))
    for db in range(8):
        for fc in range(128 // FB):
            wt, wtb = (wA + wB)[fc % 4]
            p.dma(PI, wt[:, 0:FB * 512], w2[db][:, fc * FB * 512:(fc + 1) * FB * 512], wtb, writes=[wtb])
            for f4 in range(FB // 4):
                f0 = fc * FB + f4 * 4
                at, atb = aslot[acnt % NSL]
                acnt += 1
                p.dma(PF, at, aT_s[:, f0:f0 + 4, :], atb, reads=ab_b[f0:f0 + 4], writes=[atb])

                def mm(e, wt=wt, at=at, f4=f4, f0=f0):
                    r = None
                    for fl in range(4):
                        fg = f0 + fl
                        wofs = (f4 * 4 + fl) * 512
                        for c4 in range(4):
                            for tt in range(2):
                                r = e.matmul(ps[c4 * 2 + tt][0][:], lhsT=wt[:, wofs + c4 * 128:wofs + (c4 + 1) * 128],
                                             rhs=at[:, fl, tt * 512:(tt + 1) * 512], start=(fg == 0), stop=(fg == 127))
                    return r
                p.op("pe", mm, reads=[wtb, atb], writes=[q[1] for q in ps])
        for c4 in range(4):
            cg = db * 4 + c4
            for tt in range(2):
                i = c4 * 2 + tt
                tm, tmb = tmps[i]
                if i % 2 == 0:
                    p.op("act", lambda e, tm=tm, i=i, cg=cg: e.activation(out=tm, in_=ps[i][0][:], func=AF.Identity,
                                                                       scale=vc[:, 5, cg:cg + 1]),
                         reads=[ps[i][1], vc_b, tmb], writes=[tmb])
                else:
                    p.op("dve", lambda e, tm=tm, i=i, cg=cg: e.tensor_scalar(out=tm, in0=ps[i][0][:], scalar1=vc[:, 5, cg:cg + 1],
                                                                          scalar2=None, op0=ALU.mult),
                         reads=[ps[i][1], vc_b, tmb], writes=[tmb])
        for c4 in range(4):
            cg = db * 4 + c4
            t, b = load_x(x1T_s[cg], x1b[cg])
            for tt in range(2):
                sl = slice(tt * 512, (tt + 1) * 512)
                tm, tmb = tmps[c4 * 2 + tt]
                p.op("dve", lambda e, t=t, sl=sl, tm=tm: e.tensor_tensor(out=t[:, sl], in0=t[:, sl], in1=tm, op=ALU.add),
                     reads=[tmb, b], writes=[b])
            p.dma(PF, dst[cg], t[:], b, reads=[b], writes=[x2b[cg]])
    if last:
        stats_pass(x2T_s, x2b, ps[0], ps[1])
        make_rstd(ps[0], ps[1])
        for cg in range(32):
            t, b = load_x(x2T_s[cg], x2b[cg])
            p.op("dve", lambda e, t=t: e.tensor_tensor(out=t[:], in0=t[:], in1=rstd[:], op=ALU.mult),
                 reads=[b, rstd_b], writes=[b])
            u, ub = tf[cg % 2]
            p.op("act", lambda e, t=t, u=u, cg=cg: e.activation(out=u[:], in_=t[:], func=AF.Identity, scale=vc[:, 8, cg:cg + 1]),
                 reads=[b, vc_b], writes=[ub])
            p.dma(PF, outT[cg], u[:], ub, reads=[ub])
    p.finish("sp")
    p.emit()
    p.close()
    return nc


def _lay_kc(w, nk):
    K, N = w.shape
    return np.ascontiguousarray(w.reshape(nk, 128, N // 128, 128).transpose(2, 1, 0, 3)).reshape(N // 128, 128, nk * 128)


def _run(nc, in_maps):
    res = run_bass_kernel_spmd(nc, in_maps, core_ids=list(range(8)))
    return res.results


def kernel(x, c, positions, ada_w, ada_b, norm_mix_w, w_in, nsa_cmp_pos, nsa_cmp_w1, nsa_cmp_w2,
           hgrn_lb_logits, hgrn_norm_w, w_up_a, w_up_b, w_out, norm_mlp_w, w_mlp1, w_mlp2, final_norm_w):
    A = lambda a: np.asarray(a)
    x, c, positions, ada_w, ada_b, norm_mix_w, w_in = A(x), A(c), A(positions), A(ada_w), A(ada_b), A(norm_mix_w), A(w_in)
    nsa_cmp_pos, nsa_cmp_w1, nsa_cmp_w2 = A(nsa_cmp_pos), A(nsa_cmp_w1), A(nsa_cmp_w2)
    hgrn_lb_logits, hgrn_norm_w, w_up_a, w_up_b, w_out = A(hgrn_lb_logits), A(hgrn_norm_w), A(w_up_a), A(w_up_b), A(w_out)
    norm_mlp_w, w_mlp1, w_mlp2, final_norm_w = A(norm_mlp_w), A(w_mlp1), A(w_mlp2), A(final_norm_w)
    S = 4096
    ones = np.ones((128, 128), f32)
    ncM = build_M()
    maps = []
    for i in range(8):
        l, j = i // 4, i % 4
        cols = slice(j * NCOL, (j + 1) * NCOL)
        maps.append(dict(c=c, adaw=np.ascontiguousarray(ada_w[l][:, cols]).reshape(128, 32, NCOL),
                         adab=np.ascontiguousarray(np.stack([ada_b[l][cols]] * 2))))
    r = _run(ncM, maps)
    mod = np.zeros((2, 2, 6 * 4096), f32)
    for i in range(8):
        l, j = i // 4, i % 4
        mod[l][:, j * NCOL:(j + 1) * NCOL] = r[i]["mod"]
    del maps
    consts = nsa_consts()
    rmask = np.ascontiguousarray(np.tile((np.arange(S) % 64 != 0).astype(f32)[None], (128, 1)))
    tri = np.triu(np.ones((64, 64), f32))
    ncA1 = build_A1()
    ncA2 = build_A2()
    O = dict(q=0, kc=2048, vc=2560, ks=3072, vs=3584, kw=4096, vw=4608, ng=5120, hq=5168, hf=7216, hi=9264, hg=11312, ga=13360, gb=17456)
    xcur = x
    for l in range(2):
        Wl = w_in[l]
        m6 = mod[l].reshape(2, 6, 4096)
        wF, wT = [], []
        for g in range(4):
            fc = np.concatenate([np.arange(O["q"] + g * 512, O["q"] + (g + 1) * 512)] +
                                [np.arange(O[n] + g * 128, O[n] + (g + 1) * 128) for n in ("kc", "vc", "ks", "kw")] +
                                [np.arange(O[n] + g * 512, O[n] + (g + 1) * 512) for n in ("hq", "hf")])
            tc = np.concatenate([np.arange(O[n] + g * 128, O[n] + (g + 1) * 128) for n in ("vs", "vw")] +
                                [np.arange(O[n] + g * 512, O[n] + (g + 1) * 512) for n in ("hi", "hg")] +
                                [np.array([O["ng"] + br * 16 + g * 4 + j for br in range(3) for j in range(4)])])
            wF.append(_lay_kc(np.ascontiguousarray(Wl[:, fc]), 32))
            wT.append(np.ascontiguousarray(np.ascontiguousarray(Wl[:, tc]).reshape(32, 128, NT).transpose(1, 0, 2)))
        xTb = [np.ascontiguousarray(xcur[b].T).reshape(32, 128, S) for b in range(2)]
        maps = []
        for i in range(8):
            b, g = i // 4, i % 4
            v3 = np.stack([m6[b, 0], m6[b, 1], norm_mix_w[l]])
            maps.append(dict(xT=xTb[b], vecs=np.ascontiguousarray(v3.reshape(3, 32, 128).transpose(2, 0, 1)),
                             wF=wF[g], wT=wT[g], ones=ones))
        rA1 = _run(ncA1, maps)
        del maps, wF, wT, xTb
        maps = []
        for i in range(8):
            b, g = i // 4, i % 4
            oF, oT = rA1[i]["outF"], rA1[i]["outT"]
            d = dict(qT=np.ascontiguousarray(oF[0:4]), kT=np.ascontiguousarray(oF[4:8]),
                     vs=np.ascontiguousarray(oT[:, 0:128]), vw=np.ascontiguousarray(oT[:, 128:256]),
                     ng=np.ascontiguousarray(oT[:, 1280:1292]),
                     pos=np.ascontiguousarray(np.tile(positions[b][None].astype(np.int32), (32, 1))),
                     cposT=np.ascontiguousarray(nsa_cmp_pos[l].transpose(0, 2, 1)),
                     cw1=np.ascontiguousarray(nsa_cmp_w1[l].transpose(0, 2, 1, 3)).reshape(2, 128, 32 * 128),
                     cw2=np.ascontiguousarray(nsa_cmp_w2[l]))
            d.update(consts)
            maps.append(d)
        rA2 = _run(ncA2, maps)
        del maps
        ncA3 = build_A3(l)
        maps = []
        for i in range(8):
            b, g = i // 4, i % 4
            oF, oT = rA1[i]["outF"], rA1[i]["outT"]
            lbl = np.ascontiguousarray(hgrn_lb_logits.reshape(2, 16, 128)[:, 4 * g:4 * g + 4, :].transpose(2, 0, 1))
            maps.append(dict(hqT=np.ascontiguousarray(oF[8:12]), hfT=np.ascontiguousarray(oF[12:16]),
                             hi=np.ascontiguousarray(oT[:, 256:768]), hg=np.ascontiguousarray(oT[:, 768:1280]),
                             lbl=lbl, nw64=np.ascontiguousarray(np.tile(hgrn_norm_w[l][None], (64, 1))),
                             rmask=rmask, tri=tri, ident=consts["ident"]))
        rA3 = _run(ncA3, maps)
        del maps, rA1
        on = [np.concatenate([rA2[b * 4 + g]["onsa"] for g in range(4)], axis=1) for b in range(2)]
        oh = [np.concatenate([rA3[b * 4 + g]["ohg"] for g in range(4)], axis=1) for b in range(2)]
        del rA2, rA3
        wga = _lay_kc(np.ascontiguousarray(Wl[:, O["ga"]:O["ga"] + 4096]), 32)
        wgb = _lay_kc(np.ascontiguousarray(Wl[:, O["gb"]:O["gb"] + 4096]), 32)
        wua = _lay_kc(w_up_a[l], 16)
        wub = _lay_kc(w_up_b[l], 16)
        wo = _lay_kc(w_out[l], 32)
        w1 = _lay_kc(w_mlp1[l], 32)
        w2 = np.ascontiguousarray(w_mlp2[l].reshape(128, 128, 8, 512).transpose(2, 1, 0, 3)).reshape(8, 128, 128 * 512)
        last = (l == 1)
        ncB = build_B(last)
        maps = []
        for i in range(8):
            b, s0 = i // 4, (i % 4) * 1024
            v9 = np.concatenate([m6[b], np.stack([norm_mix_w[l], norm_mlp_w[l], final_norm_w])], 0)
            maps.append(dict(xT=np.ascontiguousarray(xcur[b, s0:s0 + 1024].T).reshape(32, 128, 1024),
                             vecs=np.ascontiguousarray(v9.reshape(9, 32, 128).transpose(2, 0, 1)),
                             wga=wga, wgb=wgb,
                             onT=np.ascontiguousarray(on[b][s0:s0 + 1024].T).reshape(16, 128, 1024),
                             ohT=np.ascontiguousarray(oh[b][s0:s0 + 1024].T).reshape(16, 128, 1024),
                             wua=wua, wub=wub, wo=wo, w1=w1, w2=w2, ones=ones))
        rB = _run(ncB, maps)
        del maps, wga, wgb, wua, wub, wo, w1, w2
        xn = np.empty((2, S, 4096), f32)
        for i in range(8):
            b, s0 = i // 4, (i % 4) * 1024
            xn[b, s0:s0 + 1024] = rB[i]["outT"].reshape(4096, 1024).T
        del rB
        xcur = xn
    return xcur
```
